# Optimizing a Trainium2 kernel written in Bass

```python
import math
import jax, jax.numpy as jnp
from jax import lax
import numpy as np

D_MODEL = 2048
BATCH = 4
SEQ = 4096
DEPTH = 2
DEC_BATCH = 8
DEC_SEQ = 2048
PAST_LEN = 128

D_PLE = 256
EPS = 1e-6
N_BRANCH = 3
W_BR = 1024
W_A = W_BR
NB_A = 8
BW_A = W_A // NB_A
CONV_A = 4
C_RGLRU = 8.0
W_B = W_BR
NH_B = 4
DH_B = W_B // NH_B
CHUNK_B = 128
W_C = W_BR
NH_C = 8
DH_C = W_C // NH_C
CONV_C = 4
CHUNK_C = 64
D_FF = 3 * D_MODEL
CONV_FF = 3
PROJ_SIZES = (
    W_A, W_A,
    3 * W_B, W_B, 4 * NH_B,
    3 * W_C, W_C, 4 * NH_C,
    N_BRANCH * D_MODEL,
)
D_PROJ = 2 * W_A + 4 * W_B + 4 * NH_B + 4 * W_C + 4 * NH_C + N_BRANCH * D_MODEL

kernel_name = 'hybrid_bidir_rglru_mlstm_gdn_encoder'


def _rmsnorm(x, g):
    xf = x.astype(jnp.float32)
    y = xf * lax.rsqrt(jnp.mean(xf * xf, axis=-1, keepdims=True) + EPS)
    return (y * g.astype(jnp.float32)).astype(x.dtype)


def _head_rmsnorm(h, g):
    H, Dh = h.shape[-2:]
    y = h * lax.rsqrt(jnp.mean(h * h, axis=-1, keepdims=True) + EPS)
    return y * g.astype(jnp.float32).reshape(H, Dh)


def _l2norm(x):
    return x * lax.rsqrt(jnp.sum(x * x, axis=-1, keepdims=True) + EPS)


def _split(x, sizes):
    outs, start = [], 0
    for s in sizes:
        outs.append(x[..., start:start + s])
        start += s
    return outs


def _flip(t):
    return jnp.flip(t, axis=1)


def _dwconv(x, w):
    K, C = w.shape
    left = K // 2
    return lax.conv_general_dilated(
        x, w[:, None, :], window_strides=(1,), padding=[(left, K - 1 - left)],
        dimension_numbers=('NWC', 'WIO', 'NWC'), feature_group_count=C)


def _to_chunks(t, L):
    B, S, H = t.shape[:3]
    t = t.reshape(B, S // L, L, H, *t.shape[3:])
    return jnp.moveaxis(t, (1, 3), (0, 2))


def _from_chunks(t):
    N, B, H, L = t.shape[:4]
    t = jnp.moveaxis(t, (0, 2), (1, 3))
    return t.reshape(B, N * L, H, *t.shape[4:])


def _linear_scan(a, u):
    def comb(c1, c2):
        a1, u1 = c1
        a2, u2 = c2
        return a1 * a2, a2 * u1 + u2
    _, h = lax.associative_scan(comb, (a, u), axis=1)
    return h


def _rglru_branch(xa, ga, conv_w, conv_b, gate_w, gate_b, lam):
    B, S, W = xa.shape
    xc = _dwconv(xa, conv_w) + conv_b
    xb = xc.reshape(B, S, NB_A, BW_A)
    gpre = jnp.einsum('bsnc,egncd->egbsnd', xb, gate_w).reshape(2, 2, B, S, W) + gate_b[:, :, None, None, :]
    gpre = gpre.astype(jnp.float32)
    r = jax.nn.sigmoid(gpre[:, 0])
    i = jax.nn.sigmoid(gpre[:, 1])
    log_a = C_RGLRU * r * jax.nn.log_sigmoid(lam.astype(jnp.float32))[:, None, None, :]
    a = jnp.exp(log_a)
    u = jnp.sqrt(-jnp.expm1(2.0 * log_a)) * i * xc.astype(jnp.float32)[None]
    h = _linear_scan(a[0], u[0]) + _flip(_linear_scan(_flip(a[1]), _flip(u[1])))
    return (h * jax.nn.gelu(ga.astype(jnp.float32))).astype(xa.dtype)


def _mlstm_chunked(q, k, v, ipre, fpre):
    B, S, H, Dh = q.shape
    L = CHUNK_B
    qc = _to_chunks(q * (Dh ** -0.5), L)
    kc = _to_chunks(k, L)
    vc = _to_chunks(v, L)
    ic = _to_chunks(ipre, L)
    bc = jnp.cumsum(_to_chunks(jax.nn.log_sigmoid(fpre), L), axis=-1)
    tri = jnp.tril(jnp.ones((L, L), bool))

    def step(carry, xs):
        C, n, m = carry
        q_, k_, v_, i_, b_ = xs
        logD = jnp.where(tri, b_[..., :, None] - b_[..., None, :] + i_[..., None, :], -jnp.inf)
        inter_log = b_ + m[..., None]
        m_t = jnp.maximum(jnp.max(logD, axis=-1), inter_log)
        qk = jnp.einsum('bhtd,bhsd->bhts', q_, k_) * jnp.exp(logD - m_t[..., None])
        inter_w = jnp.exp(inter_log - m_t)
        num = jnp.einsum('bhts,bhse->bhte', qk, v_) + inter_w[..., None] * jnp.einsum('bhtd,bhde->bhte', q_, C)
        den = jnp.sum(qk, axis=-1) + inter_w * jnp.einsum('bhtd,bhd->bht', q_, n)
        h = num / jnp.maximum(jnp.abs(den), jnp.exp(-m_t))[..., None]
        bL = b_[..., -1]
        log_ws = bL[..., None] - b_ + i_
        m_new = jnp.maximum(bL + m, jnp.max(log_ws, axis=-1))
        ws = jnp.exp(log_ws - m_new[..., None])
        dec = jnp.exp(bL + m - m_new)
        C = dec[..., None, None] * C + jnp.einsum('bhs,bhsd,bhse->bhde', ws, k_, v_)
        n = dec[..., None] * n + jnp.einsum('bhs,bhsd->bhd', ws, k_)
        return (C, n, m_new), h

    init = (jnp.zeros((B, H, Dh, Dh), jnp.float32), jnp.zeros((B, H, Dh), jnp.float32),
            jnp.zeros((B, H), jnp.float32))
    _, hs = lax.scan(step, init, (qc, kc, vc, ic, bc))
    return _from_chunks(hs)


def _mlstm_branch(qkv, og, gates, gate_b, norm_g):
    B, S, _ = qkv.shape
    q, k, v = (t.astype(jnp.float32).reshape(B, S, NH_B, DH_B) for t in jnp.split(qkv, 3, axis=-1))
    gp = (gates + gate_b).astype(jnp.float32).reshape(B, S, 4, NH_B)
    h_f = _mlstm_chunked(q, k, v, gp[:, :, 0], gp[:, :, 1])
    h_b = _flip(_mlstm_chunked(_flip(q), _flip(k), _flip(v), _flip(gp[:, :, 2]), _flip(gp[:, :, 3])))
    h = _head_rmsnorm(h_f + h_b, norm_g).reshape(B, S, W_B)
    return (h * jax.nn.sigmoid(og.astype(jnp.float32))).astype(qkv.dtype)


def _gdn_chunked(q, k, v, g, beta):
    B, S, H, Dh = q.shape
    L = CHUNK_C
    qc, kc, vc = _to_chunks(q, L), _to_chunks(k, L), _to_chunks(v, L)
    G = jnp.cumsum(_to_chunks(g, L), axis=-1)
    bt = _to_chunks(beta, L)
    tri = jnp.tril(jnp.ones((L, L), bool))
    strict = jnp.tril(jnp.ones((L, L), bool), -1)
    diff = G[..., :, None] - G[..., None, :]
    decay = jnp.where(tri, jnp.exp(jnp.where(tri, diff, 0.0)), 0.0)
    kb = kc * bt[..., None]
    vb = vc * bt[..., None]
    A = jnp.where(strict, jnp.einsum('nbhid,nbhjd->nbhij', kb, kc) * decay, 0.0)
    rhs = jnp.concatenate([vb, kb * jnp.exp(G)[..., None]], axis=-1)
    sol = lax.linalg.triangular_solve(A + jnp.eye(L, dtype=A.dtype), rhs, left_side=True, lower=True,
                                      unit_diagonal=True)
    u, w = sol[..., :Dh], sol[..., Dh:]
    qk = jnp.where(tri, jnp.einsum('nbhid,nbhjd->nbhij', qc, kc) * decay, 0.0)
    qg = qc * jnp.exp(G)[..., None]
    kdec = kc * jnp.exp(G[..., -1:] - G)[..., None]
    gL = jnp.exp(G[..., -1])

    def step(St, xs):
        u_n, w_n, qk_n, qg_n, kdec_n, gL_n = xs
        v_new = u_n - jnp.einsum('bhld,bhde->bhle', w_n, St)
        o = jnp.einsum('bhld,bhde->bhle', qg_n, St) + jnp.einsum('bhij,bhje->bhie', qk_n, v_new)
        St = St * gL_n[..., None, None] + jnp.einsum('bhld,bhle->bhde', kdec_n, v_new)
        return St, o

    _, o = lax.scan(step, jnp.zeros((B, H, Dh, Dh), jnp.float32), (u, w, qk, qg, kdec, gL))
    return _from_chunks(o)


def _gdn_branch(qkv, z, gates, conv_w, A_log, dt_bias, norm_g):
    B, S, _ = qkv.shape
    qkv = jax.nn.silu(_dwconv(qkv, conv_w)).astype(jnp.float32)
    q, k, v = (t.reshape(B, S, NH_C, DH_C) for t in jnp.split(qkv, 3, axis=-1))
    q = _l2norm(q) * (DH_C ** -0.5)
    k = _l2norm(k)
    gp = gates.astype(jnp.float32).reshape(B, S, 4, NH_C)
    A = jnp.exp(A_log.astype(jnp.float32))
    dtb = dt_bias.astype(jnp.float32)
    g_f = -A[0] * jax.nn.softplus(gp[:, :, 0] + dtb[0])
    g_b = -A[1] * jax.nn.softplus(gp[:, :, 2] + dtb[1])
    beta_f = jax.nn.sigmoid(gp[:, :, 1])
    beta_b = jax.nn.sigmoid(gp[:, :, 3])
    o = _gdn_chunked(q, k, v, g_f, beta_f) + _flip(
        _gdn_chunked(_flip(q), _flip(k), _flip(v), _flip(g_b), _flip(beta_b)))
    o = _head_rmsnorm(o, norm_g).reshape(B, S, W_C)
    return (o * jax.nn.silu(z.astype(jnp.float32))).astype(z.dtype)


def _mixer(xn, w, l):
    B, S, _ = xn.shape
    proj = xn @ w['w_in'][l]
    xa, ga, qkv_b, ob, gates_b, qkv_c, zc, gates_c, mg = _split(proj, PROJ_SIZES)
    yA = _rglru_branch(xa, ga, w['rglru_conv_w'][l], w['rglru_conv_b'][l], w['rglru_gate_w'][l],
                       w['rglru_gate_b'][l], w['rglru_lambda'][l])
    yB = _mlstm_branch(qkv_b, ob, gates_b, w['mlstm_gate_b'][l], w['mlstm_norm_g'][l])
    yC = _gdn_branch(qkv_c, zc, gates_c, w['gdn_conv_w'][l], w['gdn_A_log'][l], w['gdn_dt_bias'][l],
                     w['gdn_norm_g'][l])
    ys = jnp.stack([yA, yB, yC], axis=2)
    branch = jnp.einsum('bsgw,gwd->bsgd', ys, w['w_branch'][l])
    gates = jax.nn.sigmoid(mg.reshape(B, S, N_BRANCH, D_MODEL))
    merged = jnp.sum(gates * branch, axis=2)
    return merged @ w['w_out'][l]


def _ffn(xn, w_up, conv_w, w_down):
    u = _dwconv(xn @ w_up, conv_w)
    gate, up = u[..., :D_FF], u[..., D_FF:]
    return (jax.nn.gelu(gate) * up) @ w_down


def _trunk(x, p, w):
    h = x
    for l in range(DEPTH):
        h = h + _mixer(_rmsnorm(h, w['norm_mix_g'][l]), w, l)
        h = h + _ffn(_rmsnorm(h, w['norm_ffn_g'][l]), w['ffn_w_up'][l], w['ffn_conv_w'][l], w['ffn_w_down'][l])
        gate = jax.nn.sigmoid(_rmsnorm(h, w['norm_ple_g'][l]) @ w['ple_w_gate'][l])
        h = h + gate * (p[l] @ w['ple_w_proj'][l])
    return _rmsnorm(h, w['norm_final_g'])


def setup_inputs(seed: int = 0) -> dict:
    key = jax.random.key(seed)
    ks = iter(jax.random.split(key, 40))

    def nrm(shape, scale):
        return scale * jax.random.normal(next(ks), shape, jnp.float32)

    def gain(shape):
        return 1.0 + 0.01 * jax.random.normal(next(ks), shape, jnp.float32)

    x_prompt = nrm((BATCH, SEQ, D_MODEL), 1.0)
    x_sample = nrm((DEC_BATCH, DEC_SEQ, D_MODEL), 1.0)
    p_prompt = nrm((DEPTH, BATCH, SEQ, D_PLE), 1.0)
    p_sample = nrm((DEPTH, DEC_BATCH, DEC_SEQ, D_PLE), 1.0)
    a0 = jax.random.uniform(next(ks), (DEPTH, 2, W_A), jnp.float32, 0.9, 0.999)
    rglru_lambda = jnp.log(a0) - jnp.log1p(-a0)
    ib = nrm((DEPTH, 2, NH_B), 0.1)
    fb = jnp.linspace(3.0, 6.0, NH_B, dtype=jnp.float32) + nrm((DEPTH, 2, NH_B), 0.1)
    mlstm_gate_b = jnp.stack([ib, fb], axis=2).reshape(DEPTH, 4 * NH_B)
    gdn_A_log = jnp.log(jax.random.uniform(next(ks), (DEPTH, 2, NH_C), jnp.float32, 1.0, 16.0))
    dt = jnp.exp(jax.random.uniform(next(ks), (DEPTH, 2, NH_C), jnp.float32, math.log(1e-3), math.log(1e-1)))
    gdn_dt_bias = dt + jnp.log(-jnp.expm1(-dt))
    return {
        'x_prompt': x_prompt,
        'x_sample': x_sample,
        'p_prompt': p_prompt,
        'p_sample': p_sample,
        'norm_mix_g': gain((DEPTH, D_MODEL)),
        'w_in': nrm((DEPTH, D_MODEL, D_PROJ), D_MODEL ** -0.5),
        'rglru_conv_w': nrm((DEPTH, CONV_A, W_A), CONV_A ** -0.5),
        'rglru_conv_b': nrm((DEPTH, W_A), 0.01),
        'rglru_gate_w': nrm((DEPTH, 2, 2, NB_A, BW_A, BW_A), BW_A ** -0.5),
        'rglru_gate_b': nrm((DEPTH, 2, 2, W_A), 0.01),
        'rglru_lambda': rglru_lambda,
        'mlstm_gate_b': mlstm_gate_b,
        'mlstm_norm_g': gain((DEPTH, W_B)),
        'gdn_conv_w': nrm((DEPTH, CONV_C, 3 * W_C), CONV_C ** -0.5),
        'gdn_A_log': gdn_A_log,
        'gdn_dt_bias': gdn_dt_bias,
        'gdn_norm_g': gain((DEPTH, W_C)),
        'w_branch': nrm((DEPTH, N_BRANCH, W_BR, D_MODEL), W_BR ** -0.5),
        'w_out': nrm((DEPTH, D_MODEL, D_MODEL), D_MODEL ** -0.5),
        'norm_ffn_g': gain((DEPTH, D_MODEL)),
        'ffn_w_up': nrm((DEPTH, D_MODEL, 2 * D_FF), D_MODEL ** -0.5),
        'ffn_conv_w': nrm((DEPTH, CONV_FF, 2 * D_FF), CONV_FF ** -0.5),
        'ffn_w_down': nrm((DEPTH, D_FF, D_MODEL), D_FF ** -0.5),
        'norm_ple_g': gain((DEPTH, D_MODEL)),
        'ple_w_gate': nrm((DEPTH, D_MODEL, D_MODEL), D_MODEL ** -0.5),
        'ple_w_proj': nrm((DEPTH, D_PLE, D_MODEL), D_PLE ** -0.5),
        'norm_final_g': gain((D_MODEL,)),
    }


def reference(x_prompt, x_sample, p_prompt, p_sample, norm_mix_g, w_in, rglru_conv_w, rglru_conv_b,
              rglru_gate_w, rglru_gate_b, rglru_lambda, mlstm_gate_b, mlstm_norm_g, gdn_conv_w, gdn_A_log,
              gdn_dt_bias, gdn_norm_g, w_branch, w_out, norm_ffn_g, ffn_w_up, ffn_conv_w, ffn_w_down,
              norm_ple_g, ple_w_gate, ple_w_proj, norm_final_g):
    w = dict(norm_mix_g=norm_mix_g, w_in=w_in, rglru_conv_w=rglru_conv_w, rglru_conv_b=rglru_conv_b,
             rglru_gate_w=rglru_gate_w, rglru_gate_b=rglru_gate_b, rglru_lambda=rglru_lambda,
             mlstm_gate_b=mlstm_gate_b, mlstm_norm_g=mlstm_norm_g, gdn_conv_w=gdn_conv_w,
             gdn_A_log=gdn_A_log, gdn_dt_bias=gdn_dt_bias, gdn_norm_g=gdn_norm_g, w_branch=w_branch,
             w_out=w_out, norm_ffn_g=norm_ffn_g, ffn_w_up=ffn_w_up, ffn_conv_w=ffn_conv_w,
             ffn_w_down=ffn_w_down, norm_ple_g=norm_ple_g, ple_w_gate=ple_w_gate, ple_w_proj=ple_w_proj,
             norm_final_g=norm_final_g)
    y_prompt = _trunk(x_prompt, p_prompt, w)
    y_sample = _trunk(x_sample, p_sample, w)
    return (y_prompt, y_sample)
```

```python
import math
from contextlib import ExitStack

import numpy as np
import concourse.bass as bass
import concourse.mybir as mybir
from concourse.bass_utils import run_bass_kernel_spmd

F32 = mybir.dt.float32
BF16 = mybir.dt.bfloat16
AF = mybir.ActivationFunctionType
ALU = mybir.AluOpType
AX = mybir.AxisListType

D = 2048
KD = 16
DPROJ = 16432
DFF = 6144
EPS = 1e-6
NEG = -30000.0

COMPUTE = ("pe", "act", "dve", "pool")
QUEUES = ("pe", "act", "dve", "pool", "sp")
NDSEM = 8


class Op:
    __slots__ = ("eng", "fn", "dma", "deps", "sig", "sem", "sigval", "waited")

    def __init__(self, eng, fn, dma):
        self.eng = eng
        self.fn = fn
        self.dma = dma
        self.deps = []
        self.sig = False
        self.sem = None
        self.sigval = 0
        self.waited = False


class Sched:
    def __init__(self):
        self.q = {e: [] for e in QUEUES}
        self.last_w = {}
        self.readers = {}
        self.dma_hist = {e: [] for e in QUEUES}
        self.all_dmas = []
        self.bar = []
        self.bar_pending = {e: False for e in QUEUES}
        self.dmas_since_bar = []

    def barrier(self):
        deps = [self.q[e][-1] for e in COMPUTE if self.q[e] and self.q[e][-1].fn is not None]
        deps += self.dmas_since_bar
        self.bar = deps
        self.dmas_since_bar = []
        for e in QUEUES:
            self.bar_pending[e] = True

    def add(self, eng, fn, reads=(), writes=(), dma=False):
        op = Op(eng, fn, dma)
        deps = {}

        def consider(d, kind):
            if d is None:
                return
            if (not d.dma) and d.eng == eng:
                if (not dma) and eng == "pe":
                    return
            deps[id(d)] = d

        for k in reads:
            consider(self.last_w.get(k), "raw")
            if k.startswith("ps"):
                for r in self.readers.get(k, ()):
                    if r.eng != eng:
                        deps[id(r)] = r
        for k in writes:
            consider(self.last_w.get(k), "waw")
            for r in self.readers.get(k, ()):
                consider(r, "war")
        if self.bar_pending[eng]:
            self.bar_pending[eng] = False
            for d in self.bar:
                if d.dma or d.eng != eng:
                    deps[id(d)] = d
        if dma:
            hist = self.dma_hist[eng]
            if len(hist) >= NDSEM:
                d = hist[-NDSEM]
                deps[id(d)] = d
            hist.append(op)
            self.all_dmas.append(op)
            self.dmas_since_bar.append(op)
        op.deps = list(deps.values())
        for k in reads:
            self.readers.setdefault(k, []).append(op)
        for k in writes:
            self.last_w[k] = op
            self.readers[k] = []
        self.q[eng].append(op)
        return op

    def pe(self, fn, reads=(), writes=()):
        return self.add("pe", fn, reads, writes)

    def act(self, fn, reads=(), writes=()):
        return self.add("act", fn, reads, writes)

    def dve(self, fn, reads=(), writes=()):
        return self.add("dve", fn, reads, writes)

    def pool(self, fn, reads=(), writes=()):
        return self.add("pool", fn, reads, writes)

    def dma(self, q, fn, reads=(), writes=()):
        return self.add(q, fn, reads, writes, dma=True)

    def emit(self, nc):
        fin = Op("sp", None, False)
        fin.deps = list(self.all_dmas)
        self.q["sp"].append(fin)
        for e in QUEUES:
            for op in self.q[e]:
                for d in op.deps:
                    d.waited = True
        with ExitStack() as es:
            csem = {e: es.enter_context(nc.semaphore("s_" + e)) for e in COMPUTE}
            dsem = {e: [es.enter_context(nc.semaphore("d_%s_%d" % (e, i))) for i in range(NDSEM)]
                    for e in QUEUES}
            for e in QUEUES:
                cnt = 0
                dcnt = [0] * NDSEM
                di = 0
                for op in self.q[e]:
                    if op.fn is None:
                        continue
                    if op.dma:
                        s = di % NDSEM
                        di += 1
                        dcnt[s] += 16
                        op.sem = dsem[e][s]
                        op.sigval = dcnt[s]
                        op.sig = True
                    elif op.waited:
                        cnt += 1
                        op.sem = csem[e]
                        op.sigval = cnt
                        op.sig = True
            block = es.enter_context(nc.Block())
            engs = {"pe": block.tensor, "act": block.scalar, "dve": block.vector,
                    "pool": block.gpsimd, "sp": block.sync}
            self.stats = {}
            for e in QUEUES:
                ops = self.q[e]
                nw = [0]

                def body(eng, ops=ops, nw=nw):
                    known = {}
                    for op in ops:
                        need = {}
                        for d in op.deps:
                            key = d.sem.name
                            if known.get(key, 0) >= d.sigval:
                                continue
                            if key not in need or need[key][1] < d.sigval:
                                need[key] = (d.sem, d.sigval)
                        for key, (sem, val) in need.items():
                            eng.wait_ge(sem, val)
                            known[key] = val
                            nw[0] += 1
                        if op.fn is None:
                            continue
                        ins = op.fn(eng)
                        if op.sig:
                            ins.then_inc(op.sem, 16 if op.dma else 1)

                engs[e](body)
                self.stats[e] = (len(ops), nw[0])


ARENA_WORDS = 48640


class B:
    def __init__(self, SEG, debug=False):
        self.SEG = SEG
        self.T = 2 * SEG
        self.debug = debug
        self.nc = bass.Bass("TRN2", target_bir_lowering=False)
        self.S = Sched()
        self.uid = 0

    def dram_in(self, name, shape):
        return self.nc.dram_tensor(name, list(shape), F32, kind="ExternalInput").ap()

    def scratch(self, name, shape, dt):
        kind = "ExternalOutput" if self.debug else "Internal"
        return self.nc.dram_tensor(name, list(shape), dt, kind=kind).ap()

    def alloc(self, free_shape, dt, parts=128):
        n = 1
        for s in free_shape:
            n *= s
        words = n if dt == F32 else (n + 1) // 2
        words = (words + 7) // 8 * 8
        assert self.off + words <= ARENA_WORDS, ("arena overflow", self.off, words)
        ap = self.arena[0:parts, self.off:self.off + words]
        self.off += words
        if dt == BF16:
            ap = ap.bitcast(BF16)
        ap = ap[:, 0:n]
        if len(free_shape) == 2:
            ap = ap.rearrange("p (a b) -> p a b", a=free_shape[0])
        elif len(free_shape) == 3:
            ap = ap.rearrange("p (a b c) -> p a b c", a=free_shape[0], b=free_shape[1])
        return ap

    def key(self, base):
        self.uid += 1
        return "%s#%d" % (base, self.uid)

    def phase(self):
        self.S.barrier()
        self.off = self.persist_off

    def mm(self, out, lhsT, rhs, start, stop, R, W):
        self.S.pe(lambda e: e.matmul(out, lhsT=lhsT, rhs=rhs, start=start, stop=stop), R, W)

    def tr(self, out, in_, ident, R, W):
        self.S.pe(lambda e: e.transpose(out, in_, ident), R, W)

    def act(self, out, in_, func, R, W, bias=0.0, scale=1.0):
        self.S.act(lambda e: e.activation(out=out, in_=in_, func=func, bias=bias, scale=scale), R, W)

    def tt(self, out, a, b, op, R, W, eng="dve"):
        self.S.add(eng, lambda e: e.tensor_tensor(out=out, in0=a, in1=b, op=op), R, W)

    def ts(self, out, a, s1, s2, op0, op1, R, W, eng="dve"):
        if s2 is None:
            self.S.add(eng, lambda e: e.tensor_scalar(out=out, in0=a, scalar1=s1, scalar2=None, op0=op0), R, W)
        else:
            self.S.add(eng, lambda e: e.tensor_scalar(out=out, in0=a, scalar1=s1, scalar2=s2, op0=op0, op1=op1), R, W)

    def stt(self, out, a, s, b, op0, op1, R, W):
        self.S.dve(lambda e: e.scalar_tensor_tensor(out=out, in0=a, scalar=s, in1=b, op0=op0, op1=op1), R, W)

    def copy(self, out, in_, R, W, eng="dve"):
        if eng == "act":
            self.S.act(lambda e: e.copy(out=out, in_=in_), R, W)
        else:
            self.S.add(eng, lambda e: e.tensor_copy(out=out, in_=in_), R, W)

    def dma(self, out, in_, R, W, q="sp", slow=False):
        if slow:
            self.S.dma(q, lambda e: e.dma_start(out=out, in_=in_, allow_slow_non_contiguous=True), R, W)
        else:
            self.S.dma(q, lambda e: e.dma_start(out=out, in_=in_), R, W)

    def memset(self, ap, val, W, eng="pool"):
        self.S.add(eng, lambda e: e.memset(ap, val), (), W)

    def evac(self, out, in_, R, W):
        self.ev = getattr(self, "ev", 0) + 1
        self.copy(out, in_, R, W, eng=("act" if self.ev % 2 else "dve"))

    def wsetup(self):
        self.wst = [self.alloc([16, 128], F32) for _ in range(2)]
        self.wbf = [self.alloc([16, 128], BF16) for _ in range(3)]
        self.wi = 0

    def load_w(self, src, nk, ncol=128):
        i = self.wi
        self.wi += 1
        st = self.wst[i % 2]
        bf = self.wbf[i % 3]
        ks, kb = "wst%d" % (i % 2), "wbf%d" % (i % 3)
        self.dma(st[:, 0:nk, 0:ncol], src.rearrange("(k p) n -> p k n", p=128), [], [ks])
        self.copy(bf[:, 0:nk, 0:ncol], st[:, 0:nk, 0:ncol], [ks], [kb], eng="pool")
        return bf, kb

    def colblocks(self, n):
        return [(a, min(a + 512, n)) for a in range(0, n, 512)]

    def ps(self, i):
        return self.psum[i // 2][:, (i % 2) * 512:(i % 2) * 512 + 512], "ps%d" % i


WNAMES = [
    ("norm_mix_g", (2, 2048)), ("w_in", (2, 2048, 16432)), ("rglru_conv_w", (2, 4, 1024)),
    ("rglru_conv_b", (2, 1024)), ("rglru_gate_w", (2, 2, 2, 8, 128, 128)), ("rglru_gate_b", (2, 2, 2, 1024)),
    ("rglru_lambda", (2, 2, 1024)), ("mlstm_gate_b", (2, 16)), ("mlstm_norm_g", (2, 1024)),
    ("gdn_conv_w", (2, 4, 3072)), ("gdn_A_log", (2, 2, 8)), ("gdn_dt_bias", (2, 2, 8)),
    ("gdn_norm_g", (2, 1024)), ("w_branch", (2, 3, 1024, 2048)), ("w_out", (2, 2048, 2048)),
    ("norm_ffn_g", (2, 2048)), ("ffn_w_up", (2, 2048, 12288)), ("ffn_conv_w", (2, 3, 12288)),
    ("ffn_w_down", (2, 6144, 2048)), ("norm_ple_g", (2, 2048)), ("ple_w_gate", (2, 2048, 2048)),
    ("ple_w_proj", (2, 256, 2048)), ("norm_final_g", (2048,)),
]

C_MIXG, C_FFNG, C_PLEG, C_FING = 0, 16, 32, 48
C_ACW, C_ACB, C_AGB, C_ALAM = 64, 96, 104, 136
C_BNG, C_CCW, C_CNG, C_FCW = 152, 160, 256, 264
C_AC8 = 552
NCOLS = 568


def build_nc(SEG=2048, debug=False, nlayers=2, stop_after=None):
    b = B(SEG, debug)
    nc, S, T = b.nc, b.S, b.T
    NT = T // 128
    x_d = b.dram_in("x", (T, D))
    p_d = b.dram_in("p", (2, T, 256))
    flag_d = b.dram_in("flag", (128, 1))
    W = {n: b.dram_in(n, s) for n, s in WNAMES}
    y_d = nc.dram_tensor("y", [T, D], F32, kind="ExternalOutput").ap()
    hA = b.scratch("hA", (16, 128, T), F32)
    hM = b.scratch("hM", (16, 128, T), F32)
    Ain = b.scratch("Ain", (16, 128, T), BF16)
    Bqk = b.scratch("Bqk", (16, 128, T), BF16)
    Bo = b.scratch("Bo", (8, 128, T), BF16)
    Bv = b.scratch("Bv", (T, 1024), BF16)
    Bk = b.scratch("Bk", (T, 1024), BF16)
    Gt = b.scratch("Gt", (T, 48), F32)
    Cqkv = b.scratch("Cqkv", (24, 128, T), BF16)
    Cz = b.scratch("Cz", (8, 128, T), BF16)
    MG = b.scratch("MG", (48, 128, T), BF16)
    Y = b.scratch("Y", (24, 128, T), BF16)

    with ExitStack() as es:
        b.arena = es.enter_context(nc.sbuf_tensor("arena", [128, ARENA_WORDS], F32))
        b.psum = [es.enter_context(nc.psum_tensor("psum%d" % i, [128, 1024], F32)) for i in range(4)]
        b.off = 0
        ident = b.alloc([128], F32)
        ones_f = b.alloc([128], F32)
        ones_b = b.alloc([128], BF16)
        Uf = b.alloc([128], F32)
        Ub = b.alloc([128], F32)
        NGf = b.alloc([128], F32)
        NGb = b.alloc([128], F32)
        PSf = b.alloc([128], F32)
        PSb = b.alloc([128], F32)
        flag = b.alloc([1], F32)
        cols = b.alloc([NCOLS], F32)
        gbB = b.alloc([16], F32)
        dtb = b.alloc([16], F32)
        negA = b.alloc([16], F32)
        b.persist_off = b.off

        CK = []

        def aff(ap, src_val, cmp, fill, pattern, cm):
            k = "const%d" % len(CK)
            CK.append(k)
            b.memset(ap, src_val, [k])
            S.pool(lambda e: e.affine_select(out=ap, in_=ap, compare_op=cmp, fill=fill, base=0,
                                              pattern=pattern, channel_multiplier=cm), [k], [k])

        aff(ident, 0.0, ALU.not_equal, 1.0, [[-1, 128]], 1)
        b.memset(ones_f, 1.0, ["const_of"])
        b.memset(ones_b, 1.0, ["const_ob"])
        CK += ["const_of", "const_ob", "const_flag"]
        aff(Uf, 1.0, ALU.is_ge, 0.0, [[1, 128]], -1)
        aff(Ub, 1.0, ALU.is_ge, 0.0, [[-1, 128]], 1)
        aff(NGf, 0.0, ALU.is_ge, NEG, [[1, 128]], -1)
        aff(NGb, 0.0, ALU.is_ge, NEG, [[-1, 128]], 1)
        aff(PSf, 0.0, ALU.is_gt, -NEG, [[-1, 128]], 1)
        aff(PSb, 0.0, ALU.is_gt, -NEG, [[1, 128]], -1)
        b.dma(flag, flag_d, [], ["const_flag"])

        def load_cols(layer):
            b.phase()
            stage = b.alloc([128], F32)
            items = [
                (W["norm_mix_g"][layer].rearrange("(c p) -> c p", p=128), 16, C_MIXG),
                (W["norm_ffn_g"][layer].rearrange("(c p) -> c p", p=128), 16, C_FFNG),
                (W["norm_ple_g"][layer].rearrange("(c p) -> c p", p=128), 16, C_PLEG),
                (W["norm_final_g"].rearrange("(c p) -> c p", p=128), 16, C_FING),
                (W["rglru_conv_w"][layer].rearrange("k (n p) -> (k n) p", p=128), 32, C_ACW),
                (W["rglru_conv_b"][layer].rearrange("(n p) -> n p", p=128), 8, C_ACB),
                (W["rglru_gate_b"][layer].rearrange("e g (n p) -> (e g n) p", p=128), 32, C_AGB),
                (W["rglru_lambda"][layer].rearrange("e (n p) -> (e n) p", p=128), 16, C_ALAM),
                (W["mlstm_norm_g"][layer].rearrange("(n p) -> n p", p=128), 8, C_BNG),
                (W["gdn_conv_w"][layer].rearrange("k (c p) -> (k c) p", p=128), 96, C_CCW),
                (W["gdn_norm_g"][layer].rearrange("(n p) -> n p", p=128), 8, C_CNG),
            ]
            fcw = W["ffn_conv_w"][layer].rearrange("k (c p) -> (k c) p", p=128)
            for q in range(3):
                items.append((fcw[q * 96:(q + 1) * 96, :], 96, C_FCW + q * 96))
            for i, (src, R, c0) in enumerate(items):
                pst, pk = b.ps(i % 2)
                b.dma(stage[0:R, :], src, [], ["lc_stage"])
                b.tr(pst[:, 0:R], stage[0:R, :], ident[0:R, 0:R], ["lc_stage"] + CK, [pk])
                b.copy(cols[:, c0:c0 + R], pst[:, 0:R], [pk], ["cols"], eng="act")
            tmp = b.alloc([16], F32)
            b.act(tmp, cols[:, C_ALAM:C_ALAM + 16], AF.Exp, ["cols"], ["lc_tmp"], scale=-1.0)
            b.act(tmp, tmp, AF.Ln, ["lc_tmp"], ["lc_tmp"], bias=1.0)
            b.S.act(lambda e: e.mul(out=cols[:, C_AC8:C_AC8 + 16], in_=tmp, mul=-8.0), ["lc_tmp"], ["cols"])
            b.dma(gbB, W["mlstm_gate_b"][layer:layer + 1, :].partition_broadcast(128), [], ["bc"])
            b.dma(dtb, W["gdn_dt_bias"][layer:layer + 1].rearrange("a e h -> a (e h)").partition_broadcast(128), [], ["bc"])
            b.dma(negA, W["gdn_A_log"][layer:layer + 1].rearrange("a e h -> a (e h)").partition_broadcast(128), [], ["bc"])
            b.act(negA, negA, AF.Exp, ["bc"], ["bc"])
            b.S.act(lambda e: e.mul(out=negA, in_=negA, mul=-1.0), ["bc"], ["bc"])

        def phase0():
            b.phase()
            xin = [b.alloc([D], F32) for _ in range(2)]
            xo = [b.alloc([16, 128], F32) for _ in range(2)]
            for tt in range(NT):
                xi, xk = xin[tt % 2], "xin%d" % (tt % 2)
                o, ok = xo[tt % 2], "xo%d" % (tt % 2)
                b.dma(xi, x_d[tt * 128:(tt + 1) * 128, :], [], [xk])
                for g in range(4):
                    pst, pk = b.ps(g % 2 + 2 * (tt % 2))
                    for j in range(4):
                        c = g * 4 + j
                        b.tr(pst[:, j * 128:(j + 1) * 128], xi[:, c * 128:(c + 1) * 128], ident, [xk] + CK, [pk])
                    b.evac(o[:, g * 4:(g + 1) * 4, :], pst.rearrange("p (a b) -> p a b", a=4), [pk], [ok])
                b.dma(hA[:, :, tt * 128:(tt + 1) * 128].rearrange("c p t -> p c t"), o, [ok], ["hA"], q="act")

        b.cols, b.ident, b.ones_b, b.ones_f, b.flag, b.CK = cols, ident, ones_b, ones_f, flag, CK
        ctx = dict(locals())
        phase0()
        for layer in range(nlayers):
            load_cols(layer)
            phase1(b, ctx, layer)
            if stop_after == "p1":
                break
            mixerA(b, ctx, layer)
            if stop_after == "mA":
                break
            mixerB(b, ctx, layer)
            if stop_after == "mB":
                break
            mixerC(b, ctx, layer)
            if stop_after == "mC":
                break
            phase3a(b, ctx, layer)
            if stop_after == "p3a":
                break
            phase3b(b, ctx, layer, last=(layer == nlayers - 1))
        S.emit(nc)
    return nc, b


def phase1(b, c, layer):
    T = b.T
    TB = min(T, 2048)
    cols, CK = b.cols, b.CK
    W, hA = c["W"], c["hA"]
    groups = [(0, 16, c["Ain"]), (2048, 16, c["Bqk"]), (5120, 8, c["Bo"]), (6160, 24, c["Cqkv"]),
              (9232, 8, c["Cz"]), (10288, 48, c["MG"])]
    for tb in range(T // TB):
        b.phase()
        t0 = tb * TB
        b.wsetup()
        xn = b.alloc([16, TB], BF16)
        hc = [b.alloc([TB], F32) for _ in range(2)]
        sq = [b.alloc([TB], BF16) for _ in range(2)]
        rstd = b.alloc([TB], F32)
        ost = [b.alloc([TB], BF16) for _ in range(2)]
        cbs = b.colblocks(TB)
        for ch in range(16):
            h, hk = hc[ch % 2], "hc%d" % (ch % 2)
            q, qk = sq[ch % 2], "sq%d" % (ch % 2)
            b.dma(h, hA[ch, :, t0:t0 + TB], [], [hk])
            b.act(q, h, AF.Square, [hk], [qk])
            for i, (a, e_) in enumerate(cbs):
                pst, pk = b.ps(i)
                b.mm(pst[:, 0:e_ - a], b.ones_b, q[:, a:e_], ch == 0, ch == 15, [qk] + CK, [pk])
            b.ts(xn[:, ch, :], h, cols[:, C_MIXG + ch:C_MIXG + ch + 1], None, ALU.mult, None, [hk, "cols"], ["xn"])
        for i, (a, e_) in enumerate(cbs):
            pst, pk = b.ps(i)
            b.act(rstd[:, a:e_], pst[:, 0:e_ - a], AF.Sqrt, [pk], ["rstd"], bias=EPS, scale=1.0 / D)
        b.S.dve(lambda e: e.reciprocal(out=rstd, in_=rstd), ["rstd"], ["rstd"])
        for ch in range(16):
            b.tt(xn[:, ch, :], xn[:, ch, :], rstd, ALU.mult, ["xn", "rstd"], ["xn"])
        pi = 0
        oi = 0
        for col0, nch, dst in groups:
            for j in range(nch):
                wt, wk = b.load_w(W["w_in"][layer, :, col0 + j * 128:col0 + (j + 1) * 128], 16)
                o, ok = ost[oi % 2], "ost%d" % (oi % 2)
                oi += 1
                for (a, e_) in cbs:
                    pst, pk = b.ps(4 + pi % 4)
                    pi += 1
                    for k in range(16):
                        b.mm(pst[:, 0:e_ - a], wt[:, k, :], xn[:, k, a:e_], k == 0, k == 15, [wk, "xn"], [pk])
                    b.evac(o[:, a:e_], pst[:, 0:e_ - a], [pk], [ok])
                b.dma(dst[j, :, t0:t0 + TB], o, [ok], [], q="act")
        wtb = b.alloc([16, 512], BF16)
        otm = [b.alloc([512], BF16) for _ in range(2)]
        for (col0, dst) in [(4096, c["Bv"]), (3072, c["Bk"])]:
            for half in range(2):
                for qd in range(4):
                    cc = col0 + half * 512 + qd * 128
                    wt, wk = b.load_w(W["w_in"][layer, :, cc:cc + 128], 16)
                    b.copy(wtb[:, :, qd * 128:(qd + 1) * 128], wt[:, 0:16, :], [wk], ["wtb"], eng="pool")
                for tt in range(TB // 128):
                    pst, pk = b.ps(4 + pi % 4)
                    pi += 1
                    for k in range(16):
                        b.mm(pst, xn[:, k, tt * 128:(tt + 1) * 128], wtb[:, k, :], k == 0, k == 15, ["xn", "wtb"], [pk])
                    o, ok = otm[tt % 2], "otm%d" % (tt % 2)
                    b.evac(o, pst, [pk], [ok])
                    b.dma(dst[t0 + tt * 128:t0 + (tt + 1) * 128, half * 512:(half + 1) * 512], o, [ok], [], q="act")
        gst = b.alloc([16, 48], F32)
        gbf = b.alloc([16, 48], BF16)
        og = [b.alloc([48], F32) for _ in range(2)]
        b.dma(gst[:, :, 0:16], W["w_in"][layer, :, 6144:6160].rearrange("(k p) n -> p k n", p=128), [], ["gst"])
        b.dma(gst[:, :, 16:48], W["w_in"][layer, :, 10256:10288].rearrange("(k p) n -> p k n", p=128), [], ["gst"])
        b.copy(gbf, gst, ["gst"], ["gbf"], eng="pool")
        for tt in range(TB // 128):
            pst, pk = b.ps(4 + pi % 4)
            pi += 1
            for k in range(16):
                b.mm(pst[:, 0:48], xn[:, k, tt * 128:(tt + 1) * 128], gbf[:, k, :], k == 0, k == 15, ["xn", "gbf"], [pk])
            o, ok = og[tt % 2], "og%d" % (tt % 2)
            b.evac(o, pst[:, 0:48], [pk], [ok])
            b.dma(c["Gt"][t0 + tt * 128:t0 + (tt + 1) * 128, :], o, [ok], [], q="act")


def mixerA(b, c, layer):
    T, SEG = b.T, b.SEG
    cols, CK, flag, S = b.cols, b.CK, b.flag, b.S
    W, Ain, Y = c["W"], c["Ain"], c["Y"]
    b.phase()
    b.wsetup()
    PW = SEG + 3
    xp = b.alloc([2 * PW], BF16)
    ga = b.alloc([T], BF16)
    xcb = b.alloc([T], BF16)
    yb = b.alloc([T], BF16)
    xc = b.alloc([T], F32)
    rf = b.alloc([T], F32)
    uf = b.alloc([T], F32)
    sf = b.alloc([T], F32)
    hf = b.alloc([T], F32)
    hb = b.alloc([T], F32)
    carry = b.alloc([2], F32)
    cbs = b.colblocks(T)
    for n in range(8):
        for s in range(2):
            b.dma(xp[:, s * PW + 2:s * PW + 2 + SEG], Ain[n, :, s * SEG:(s + 1) * SEG], [], ["xp"])
        b.dma(ga, Ain[8 + n, :, :], [], ["ga"])
        b.memset(xp[:, 0:2], 0.0, ["xp"])
        b.memset(xp[:, 2 * PW - 1:2 * PW], 0.0, ["xp"])
        b.ts(xp[:, PW - 1:PW], xp[:, PW + 2:PW + 3], flag, None, ALU.mult, None, ["xp"] + CK, ["xp"])
        b.ts(xp[:, PW:PW + 2], xp[:, SEG:SEG + 2], flag, None, ALU.mult, None, ["xp"] + CK, ["xp"])
        for s in range(2):
            o = s * PW
            dst = xc[:, s * SEG:(s + 1) * SEG]
            b.ts(dst, xp[:, o:o + SEG], cols[:, C_ACW + n:C_ACW + n + 1], cols[:, C_ACB + n:C_ACB + n + 1],
                 ALU.mult, ALU.add, ["xp", "cols"], ["xc"])
            for k in range(1, 4):
                b.stt(dst, xp[:, o + k:o + k + SEG], cols[:, C_ACW + k * 8 + n:C_ACW + k * 8 + n + 1], dst,
                      ALU.mult, ALU.add, ["xp", "cols", "xc"], ["xc"])
        b.copy(xcb, xc, ["xc"], ["xcb"], eng="act")
        for e in range(2):
            wr, wrk = b.load_w(W["rglru_gate_w"][layer, e, 0, n], 1)
            wi_, wik = b.load_w(W["rglru_gate_w"][layer, e, 1, n], 1)
            for (wt, wk, dst, dk, g) in ((wr, wrk, rf, "rf", 0), (wi_, wik, uf, "uf", 1)):
                bias = cols[:, C_AGB + (e * 2 + g) * 8 + n:C_AGB + (e * 2 + g) * 8 + n + 1]
                for ci, (a, e_) in enumerate(cbs):
                    pst, pk = b.ps(ci % 8)
                    b.mm(pst[:, 0:e_ - a], wt[:, 0, :], xcb[:, a:e_], True, True, [wk, "xcb"], [pk])
                    b.act(dst[:, a:e_], pst[:, 0:e_ - a], AF.Sigmoid, [pk, "cols"], [dk], bias=bias)
            c8 = cols[:, C_AC8 + e * 8 + n:C_AC8 + e * 8 + n + 1]
            b.act(rf, rf, AF.Exp, ["rf", "cols"], ["rf"], scale=c8)
            b.act(sf, rf, AF.Square, ["rf"], ["sf"])
            b.act(sf, sf, AF.Sqrt, ["sf"], ["sf"], bias=1.0, scale=-1.0)
            b.tt(uf, uf, xc, ALU.mult, ["uf", "xc"], ["uf"])
            b.tt(uf, uf, sf, ALU.mult, ["uf", "sf"], ["uf"])
            if e == 0:
                S.dve(lambda en: en.tensor_tensor_scan(out=hf[:, 0:SEG], data0=rf[:, 0:SEG], data1=uf[:, 0:SEG],
                                                       initial=0.0, op0=ALU.mult, op1=ALU.add), ["rf", "uf"], ["hf"])
                b.tt(carry[:, 0:1], hf[:, SEG - 1:SEG], flag, ALU.mult, ["hf"] + CK, ["carry0"])
                S.dve(lambda en: en.tensor_tensor_scan(out=hf[:, SEG:T], data0=rf[:, SEG:T], data1=uf[:, SEG:T],
                                                       initial=carry[:, 0:1], op0=ALU.mult, op1=ALU.add),
                      ["rf", "uf", "carry0"], ["hf"])
            else:
                S.dve(lambda en: en.tensor_tensor_scan(out=hb[:, SEG:T][:, ::-1], data0=rf[:, SEG:T][:, ::-1],
                                                       data1=uf[:, SEG:T][:, ::-1], initial=0.0,
                                                       op0=ALU.mult, op1=ALU.add), ["rf", "uf"], ["hb"])
                b.tt(carry[:, 1:2], hb[:, SEG:SEG + 1], flag, ALU.mult, ["hb"] + CK, ["carry1"])
                S.dve(lambda en: en.tensor_tensor_scan(out=hb[:, 0:SEG][:, ::-1], data0=rf[:, 0:SEG][:, ::-1],
                                                       data1=uf[:, 0:SEG][:, ::-1], initial=carry[:, 1:2],
                                                       op0=ALU.mult, op1=ALU.add), ["rf", "uf", "carry1"], ["hb"])
        b.act(sf, ga, AF.Gelu_apprx_tanh, ["ga"], ["sf"])
        b.tt(hf, hf, hb, ALU.add, ["hf", "hb"], ["hf"])
        b.tt(yb, hf, sf, ALU.mult, ["hf", "sf"], ["yb"])
        b.dma(Y[n, :, :], yb, ["yb"], [], q="act")


def bc_mid(ap2, n):
    a = ap2.ap
    return bass.AP(ap2.tensor, ap2.offset, [list(a[0]), [0, n]] + [list(x) for x in a[1:]])


def bc_last(ap1, n):
    a = ap1.ap
    return bass.AP(ap1.tensor, ap1.offset, [list(a[0]), [0, n]])


def mixerB(b, c, layer):
    T, SEG = b.T, b.SEG
    NCH, NCS = T // 128, SEG // 128
    cols, CK, flag, S, ident = b.cols, b.CK, b.flag, b.S, b.ident
    Uf, Ub, NGf, NGb, gbB = c["Uf"], c["Ub"], c["NGf"], c["NGb"], c["gbB"]
    Bqk, Bv, Bk, Bo, Gt, Y = c["Bqk"], c["Bv"], c["Bk"], c["Bo"], c["Gt"], c["Y"]
    b.phase()
    G = b.alloc([NCH, 16], F32)
    lf = b.alloc([NCH, 8], F32)
    sc = b.alloc([NCH, 8], F32)
    dec = b.alloc([NCH, 8], F32)
    b.dma(G, Gt[:, 0:16].rearrange("(c p) n -> p c n", p=128), [], ["G"])
    b.tt(G, G, bc_mid(gbB, NCH), ALU.add, ["G", "bc"], ["G"])
    b.act(lf[:, :, 0:4], G[:, :, 4:8], AF.Exp, ["G"], ["lf"], scale=-1.0)
    b.act(lf[:, :, 4:8], G[:, :, 12:16], AF.Exp, ["G"], ["lf"], scale=-1.0)
    b.act(lf, lf, AF.Ln, ["lf"], ["lf"], bias=1.0)
    S.act(lambda e: e.mul(out=lf, in_=lf, mul=-1.0), ["lf"], ["lf"])
    p0, k0 = b.ps(0)
    p1, k1 = b.ps(1)
    p2, k2 = b.ps(2)
    v3 = lambda p, n, w: p[:, 0:n * w].rearrange("p (a b) -> p a b", b=w)
    b.mm(v3(p0, NCH, 4), Uf, lf[:, :, 0:4], True, True, ["lf"] + CK, [k0])
    b.mm(v3(p1, NCH, 4), Ub, lf[:, :, 4:8], True, True, ["lf"] + CK, [k1])
    b.tt(sc[:, :, 0:4], G[:, :, 0:4], v3(p0, NCH, 4), ALU.subtract, ["G", k0], ["sc"])
    b.tt(sc[:, :, 4:8], G[:, :, 8:12], v3(p1, NCH, 4), ALU.subtract, ["G", k1], ["sc"])
    b.mm(v3(p2, NCH, 8), b.ones_f, lf, True, True, ["lf"] + CK, [k2])
    b.act(dec, v3(p2, NCH, 8), AF.Exp, [k2], ["dec"])

    qT = b.alloc([2, T], BF16)
    kT = b.alloc([2, T], BF16)
    ogT = b.alloc([2, T], BF16)
    yB = b.alloc([2, T], BF16)
    vaug = b.alloc([NCH, 260], BF16)
    ktm = b.alloc([NCH, 256], BF16)
    hsum = b.alloc([NCH, 256], F32)
    Cm = b.alloc([2, 260], F32)
    Cb = b.alloc([2, 260], BF16)
    Wrow = b.alloc([128], F32)
    DT = b.alloc([128], F32)
    ST = b.alloc([128], BF16)
    qw = b.alloc([2, 128], BF16)
    kw = b.alloc([256], BF16)
    rden = b.alloc([1], F32)
    sqc = b.alloc([256], F32)
    hn = b.alloc([256], F32)
    ssq = b.alloc([1], F32)
    sgo = b.alloc([2, 128], F32)
    for hd in range(4):
        b.dma(qT, Bqk[2 * hd:2 * hd + 2, :, :].rearrange("c p t -> p c t"), [], ["qT"])
        b.dma(kT, Bqk[8 + 2 * hd:8 + 2 * hd + 2, :, :].rearrange("c p t -> p c t"), [], ["kT"])
        b.dma(ogT, Bo[2 * hd:2 * hd + 2, :, :].rearrange("c p t -> p c t"), [], ["ogT"])
        b.dma(vaug[:, :, 0:256], Bv[:, hd * 256:(hd + 1) * 256].rearrange("(c p) e -> p c e", p=128), [], ["vaug"])
        b.dma(ktm, Bk[:, hd * 256:(hd + 1) * 256].rearrange("(c p) e -> p c e", p=128), [], ["ktm"])
        b.memset(vaug[:, :, 256:257], 1.0, ["vaug"])
        b.ts(qT, qT, 1.0 / 16.0, None, ALU.mult, None, ["qT"], ["qT"])
        for e in range(2):
            U, NG = (Uf, NGf) if e == 0 else (Ub, NGb)
            b.memset(Cm, 0.0, ["Cm"])
            b.memset(Cb, 0.0, ["Cb"])
            order = list(range(NCH)) if e == 0 else list(range(NCH - 1, -1, -1))
            for idx, ch in enumerate(order):
                if idx == NCS:
                    b.ts(Cm, Cm, flag, None, ALU.mult, None, ["Cm"] + CK, ["Cm"])
                    b.copy(Cb, Cm, ["Cm"], ["Cb"], eng="act")
                col = e * 4 + hd
                cs = slice(ch * 128, (ch + 1) * 128)
                lfb = bc_last(lf[:, ch, col:col + 1], 128)
                pB, pBk = b.ps(0)
                pD, pDk = b.ps(1)
                pS, pSk = b.ps(2)
                pO, pOk = b.ps(3)
                b.mm(pB[:, 0:128], lfb, U, True, True, ["lf"] + CK, [pBk])
                b.act(Wrow, pB[:, 0:128], AF.Exp, [pBk], ["Wrow"])
                b.mm(pD[:, 0:128], lfb, U, True, False, ["lf"] + CK, [pDk])
                b.mm(pD[:, 0:128], ident, NG, False, True, CK, [pDk])
                b.act(DT, pD[:, 0:128], AF.Exp, [pDk, "sc"], ["DT"], bias=sc[:, ch, col:col + 1])
                b.mm(pS[:, 0:128], kT[:, 0, cs], qT[:, 0, cs], True, False, ["kT", "qT"], [pSk])
                b.mm(pS[:, 0:128], kT[:, 1, cs], qT[:, 1, cs], False, True, ["kT", "qT"], [pSk])
                b.tt(ST, pS[:, 0:128], DT, ALU.mult, [pSk, "DT"], ["ST"])
                for dc in range(2):
                    b.tt(qw[:, dc, :], qT[:, dc, cs], Wrow, ALU.mult, ["qT", "Wrow"], ["qw"], eng="pool")
                b.mm(pO[:, 0:257], ST, vaug[:, ch, 0:257], True, False, ["ST", "vaug"], [pOk])
                b.mm(pO[:, 0:257], qw[:, 0, :], Cb[:, 0, 0:257], False, False, ["qw", "Cb"], [pOk])
                b.mm(pO[:, 0:257], qw[:, 1, :], Cb[:, 1, 0:257], False, True, ["qw", "Cb"], [pOk])
                b.act(rden, pO[:, 256:257], AF.Abs, [pOk], ["rden"])
                b.ts(rden, rden, 1.0, None, ALU.max, None, ["rden"], ["rden"])
                S.dve(lambda en: en.reciprocal(out=rden, in_=rden), ["rden"], ["rden"])
                if e == 0:
                    b.ts(hsum[:, ch, :], pO[:, 0:256], rden, None, ALU.mult, None, [pOk, "rden"], ["hsum"])
                else:
                    b.stt(hsum[:, ch, :], pO[:, 0:256], rden, hsum[:, ch, :], ALU.mult, ALU.add,
                          [pOk, "rden", "hsum"], ["hsum"])
                wcol = DT[:, 127:128] if e == 0 else DT[:, 0:1]
                b.ts(kw, ktm[:, ch, :], wcol, None, ALU.mult, None, ["ktm", "DT"], ["kw"], eng="pool")
                for dc in range(2):
                    pC, pCk = b.ps(4 + dc)
                    b.mm(pC[:, 0:257], kw[:, dc * 128:(dc + 1) * 128], vaug[:, ch, 0:257], True, True, ["kw", "vaug"], [pCk])
                    b.stt(Cm[:, dc, 0:257], Cm[:, dc, 0:257], dec[:, ch, col:col + 1], pC[:, 0:257], ALU.mult, ALU.add,
                          ["Cm", "dec", pCk], ["Cm"])
                b.copy(Cb[:, :, 0:257], Cm[:, :, 0:257], ["Cm"], ["Cb"], eng="act")
        for ch in range(NCH):
            cs = slice(ch * 128, (ch + 1) * 128)
            pT, pTk = b.ps(6 + ch % 2)
            b.act(sqc, hsum[:, ch, :], AF.Square, ["hsum"], ["sqc"])
            S.dve(lambda en, ch=ch: en.reduce_sum(out=ssq, in_=sqc, axis=AX.X), ["sqc"], ["ssq"])
            b.act(ssq, ssq, AF.Sqrt, ["ssq"], ["ssq"], bias=EPS, scale=1.0 / 256.0)
            S.dve(lambda en: en.reciprocal(out=ssq, in_=ssq), ["ssq"], ["ssq"])
            b.ts(hn, hsum[:, ch, :], ssq, None, ALU.mult, None, ["hsum", "ssq"], ["hn"])
            for ec in range(2):
                b.tr(pT[:, ec * 128:(ec + 1) * 128], hn[:, ec * 128:(ec + 1) * 128], ident, ["hn"] + CK, [pTk])
            b.act(sgo, ogT[:, :, cs], AF.Sigmoid, ["ogT"], ["sgo"])
            for ec in range(2):
                gcol = cols[:, C_BNG + 2 * hd + ec:C_BNG + 2 * hd + ec + 1]
                b.stt(yB[:, ec, cs], pT[:, ec * 128:(ec + 1) * 128], gcol, sgo[:, ec, :], ALU.mult, ALU.mult,
                      [pTk, "cols", "sgo"], ["yB"])
        for ec in range(2):
            b.dma(Y[8 + 2 * hd + ec, :, :], yB[:, ec, :], ["yB"], [], q="act")


def mixerC(b, c, layer):
    T, SEG = b.T, b.SEG
    NC, NCS = T // 64, SEG // 64
    cols, CK, flag, S, ident = b.cols, b.CK, b.flag, b.S, b.ident
    Uf, Ub, NGf, NGb, PSf, PSb, dtb, negA = (c[k] for k in ("Uf", "Ub", "NGf", "NGb", "PSf", "PSb", "dtb", "negA"))
    Cqkv, Cz, Gt, Y = c["Cqkv"], c["Cz"], c["Gt"], c["Y"]
    b.phase()
    H = 64
    a64 = lambda shape, dt: b.alloc(shape, dt)[0:H]
    gp = a64([NC, 32], F32)
    g = a64([NC, 16], F32)
    be = a64([NC, 16], F32)
    Gc = a64([NC, 16], F32)
    nG = a64([NC, 16], F32)
    eg = a64([NC, 16], F32)
    ed = a64([NC, 16], F32)
    bk = a64([NC, 16], F32)
    gL = b.alloc([NC, 16], F32)
    b.dma(gp, Gt[:, 16:48].rearrange("(c p) n -> p c n", p=H), [], ["gpTt"])
    b.tt(g[:, :, 0:8], gp[:, :, 0:8], bc_mid(dtb[0:H, 0:8], NC), ALU.add, ["gpTt", "bc"], ["g"])
    b.tt(g[:, :, 8:16], gp[:, :, 16:24], bc_mid(dtb[0:H, 8:16], NC), ALU.add, ["gpTt", "bc"], ["g"])
    b.act(g, g, AF.Exp, ["g"], ["g"])
    b.act(g, g, AF.Ln, ["g"], ["g"], bias=1.0)
    b.tt(g, g, bc_mid(negA[0:H, :], NC), ALU.mult, ["g", "bc"], ["g"])
    b.act(be[:, :, 0:8], gp[:, :, 8:16], AF.Sigmoid, ["gpTt"], ["be"])
    b.act(be[:, :, 8:16], gp[:, :, 24:32], AF.Sigmoid, ["gpTt"], ["be"])
    v3 = lambda p, parts, n, w: p[0:parts, 0:n * w].rearrange("p (a b) -> p a b", b=w)
    for e in range(2):
        U = Uf if e == 0 else Ub
        pa, ka = b.ps(0 + e)
        pb_, kb_ = b.ps(2 + e)
        pc, kc = b.ps(4 + e)
        gs = g[:, :, e * 8:(e + 1) * 8]
        b.mm(v3(pa, H, NC, 8), U[0:H, 0:H], gs, True, True, ["g"] + CK, [ka])
        b.copy(Gc[:, :, e * 8:(e + 1) * 8], v3(pa, H, NC, 8), [ka], ["Gc"], eng="act")
        b.mm(v3(pb_, H, NC, 8), b.ones_f[0:H, 0:H], gs, True, True, ["g"] + CK, [kb_])
        b.tt(ed[:, :, e * 8:(e + 1) * 8], v3(pb_, H, NC, 8), Gc[:, :, e * 8:(e + 1) * 8], ALU.subtract, [kb_, "Gc"], ["ed"])
        b.mm(v3(pc, 128, NC, 8), b.ones_f[0:H, :], gs, True, True, ["g"] + CK, [kc])
        b.act(gL[:, :, e * 8:(e + 1) * 8], v3(pc, 128, NC, 8), AF.Exp, [kc], ["gL"])
    b.act(ed, ed, AF.Exp, ["ed"], ["ed"])
    b.act(eg, Gc, AF.Exp, ["Gc"], ["eg"])
    b.tt(bk, be, eg, ALU.mult, ["be", "eg"], ["bk"])
    S.act(lambda en: en.mul(out=nG, in_=Gc, mul=-1.0), ["Gc"], ["nG"])

    import os as _os
    MCS = _os.environ.get("MCSTOP", "")
    if MCS == "0":
        return
    PW = SEG + 3
    xp = b.alloc([2 * PW], BF16)
    cvs = b.alloc([T], F32)
    sqb = b.alloc([T], BF16)
    yC = sqb
    qT = b.alloc([T], BF16)
    kT = b.alloc([T], BF16)
    zs = xp[:, 0:T]
    rn = b.alloc([512], F32)
    ktm = a64([NC, 128], BF16)
    vtm = a64([NC, 128], BF16)
    osum = a64([NC, 128], F32)
    Ttb = gp.bitcast(BF16)[:, 0:NC * 64].rearrange("p (a b) -> p a b", b=64) if False else a64([NC, 64], BF16)
    qkb = a64([NC, 64], BF16)
    Sm = b.alloc([128], F32)
    Sb = b.alloc([128], BF16)
    DTt = a64([64], F32)
    DAt = a64([64], F32)
    Nt = [a64([64], F32) for _ in range(2)]
    Mt = [a64([64], F32) for _ in range(2)]
    Pt = [a64([64], F32) for _ in range(2)]
    vb = a64([128], F32)
    negr = a64([128], BF16)
    vnew = a64([128], BF16)
    kdec = a64([128], BF16)
    o1 = a64([128], F32)
    o2 = a64([128], F32)
    sqc = a64([128], F32)
    on = a64([128], F32)
    ssq = a64([1], F32)
    I64 = ident[0:H, 0:H]
    cbs = b.colblocks(T)

    def conv_silu(ci):
        for s in range(2):
            b.dma(xp[:, s * PW + 2:s * PW + 2 + SEG], Cqkv[ci, :, s * SEG:(s + 1) * SEG], [], ["xp"])
        b.memset(xp[:, 0:2], 0.0, ["xp"])
        b.memset(xp[:, 2 * PW - 1:2 * PW], 0.0, ["xp"])
        b.ts(xp[:, PW - 1:PW], xp[:, PW + 2:PW + 3], flag, None, ALU.mult, None, ["xp"] + CK, ["xp"])
        b.ts(xp[:, PW:PW + 2], xp[:, SEG:SEG + 2], flag, None, ALU.mult, None, ["xp"] + CK, ["xp"])
        for s in range(2):
            o = s * PW
            dst = cvs[:, s * SEG:(s + 1) * SEG]
            wc = lambda k: cols[:, C_CCW + k * 24 + ci:C_CCW + k * 24 + ci + 1]
            b.ts(dst, xp[:, o:o + SEG], wc(0), None, ALU.mult, None, ["xp", "cols"], ["cvs"])
            for k in range(1, 4):
                b.stt(dst, xp[:, o + k:o + k + SEG], wc(k), dst, ALU.mult, ALU.add, ["xp", "cols", "cvs"], ["cvs"])
        b.act(cvs, cvs, AF.Silu, ["cvs"], ["cvs"])

    def l2norm(dstT, dkey, scale):
        b.act(sqb, cvs, AF.Square, ["cvs"], ["sqb"])
        for ci_, (a, e_) in enumerate(cbs):
            pst, pk = b.ps(ci_ % 2)
            b.mm(pst[:, 0:e_ - a], b.ones_b, sqb[:, a:e_], True, True, ["sqb"] + CK, [pk])
            b.act(rn[:, 0:e_ - a], pst[:, 0:e_ - a], AF.Sqrt, [pk], ["rn"], bias=EPS)
            S.dve(lambda en, w=e_ - a: en.reciprocal(out=rn[:, 0:w], in_=rn[:, 0:w]), ["rn"], ["rn"])
            b.stt(cvs[:, a:e_], cvs[:, a:e_], scale, rn[:, 0:e_ - a], ALU.mult, ALU.mult, ["cvs", "rn"], ["cvs"])
        b.copy(dstT, cvs, ["cvs"], [dkey], eng="act")

    def to_tm(dst, dkey):
        for ch in range(NC):
            pst, pk = b.ps(2 + (ch // 4) % 2)
            sub = pst[0:H, (ch % 4) * 128:(ch % 4 + 1) * 128]
            b.tr(sub, cvs[:, ch * 64:(ch + 1) * 64], ident, ["cvs"] + CK, [pk])
            if ch % 4 == 3 or ch == NC - 1:
                n = ch % 4 + 1
                c0 = ch - n + 1
                b.evac(dst[:, c0:c0 + n, :], pst[0:H, 0:n * 128].rearrange("p (a b) -> p a b", b=128), [pk], [dkey])

    for hd in range(8):
        conv_silu(hd)
        l2norm(qT, "qT", 128.0 ** -0.5)
        conv_silu(8 + hd)
        l2norm(kT, "kT", 1.0)
        to_tm(ktm, "ktm")
        conv_silu(16 + hd)
        to_tm(vtm, "vtm")
        b.dma(zs, Cz[hd, :, :], [], ["xp"])
        b.act(zs, zs, AF.Silu, ["xp"], ["xp"])
        if MCS == "1":
            return
        for e in range(2):
            col = e * 8 + hd
            U, NGm, PSm = (Uf, NGf, PSf) if e == 0 else (Ub, NGb, PSb)
            U64, NG64, PS64 = U[0:H, 0:H], NGm[0:H, 0:H], PSm[0:H, 0:H]
            for ch in range(NC):
                cs = slice(ch * 64, (ch + 1) * 64)
                gb = bc_last(g[:, ch, col:col + 1], H)
                r = lambda i: b.ps(i)[0][0:H, 0:64]
                rk = lambda i: b.ps(i)[1]
                b.mm(r(0), gb, U64, True, False, ["g"] + CK, [rk(0)])
                b.mm(r(0), I64, NG64, False, True, CK, [rk(0)])
                b.act(DTt, r(0), AF.Exp, [rk(0), "nG"], ["DTt"], bias=nG[:, ch, col:col + 1])
                b.mm(r(1), gb, U64, True, False, ["g"] + CK, [rk(1)])
                b.mm(r(1), I64, PS64, False, True, CK, [rk(1)])
                b.act(DAt, r(1), AF.Exp, [rk(1), "Gc"], ["DAt"], bias=Gc[:, ch, col:col + 1], scale=-1.0)
                if MCS == "2a":
                    continue
                b.mm(r(2), kT[:, cs], kT[:, cs], True, True, ["kT"], [rk(2)])
                b.mm(r(3), kT[:, cs], qT[:, cs], True, True, ["kT", "qT"], [rk(3)])
                b.stt(Nt[0], r(2), be[:, ch, col:col + 1], DAt, ALU.mult, ALU.mult, [rk(2), "be", "DAt"], ["N0"])
                b.tt(qkb[:, ch, :], r(3), DTt, ALU.mult, [rk(3), "DTt"], ["qkb"])
                if MCS == "2b":
                    continue
                b.mm(r(4), Nt[0], I64, True, True, ["N0"] + CK, [rk(4)])
                if MCS == "2c1":
                    continue
                b.copy(Mt[0], r(4), [rk(4)], ["M0"], eng="act")
                if MCS == "2c2":
                    continue
                if MCS == "2cx":
                    b.tt(Pt[0], I64, r(4), ALU.subtract, [rk(4), "M0"] + CK, ["P0"])
                    continue
                b.tt(Pt[0], I64, r(4), ALU.subtract, [rk(4)] + CK, ["P0"])
                if MCS == "2c":
                    continue
                for lv in range(1, 6):
                    pi_, ci_ = (lv - 1) % 2, lv % 2
                    b.mm(r(5), Mt[pi_], Nt[pi_], True, True, ["M%d" % pi_, "N%d" % pi_], [rk(5)])
                    b.copy(Nt[ci_], r(5), [rk(5)], ["N%d" % ci_], eng="act")
                    if lv < 5:
                        b.mm(r(6), Nt[pi_], Mt[pi_], True, True, ["M%d" % pi_, "N%d" % pi_], [rk(6)])
                        b.copy(Mt[ci_], r(6), [rk(6)], ["M%d" % ci_], eng="act")
                    b.mm(r(7), Nt[ci_], Pt[pi_], True, True, ["N%d" % ci_, "P%d" % pi_], [rk(7)])
                    if lv < 5:
                        b.tt(Pt[ci_], Pt[pi_], r(7), ALU.add, ["P%d" % pi_, rk(7)], ["P%d" % ci_])
                    else:
                        b.tt(Ttb[:, ch, :], Pt[pi_], r(7), ALU.add, ["P%d" % pi_, rk(7)], ["Ttb"])
            if MCS in ("2", "2a", "2b", "2c", "2c1", "2c2", "2cx"):
                return
            b.memset(Sm, 0.0, ["Sm"])
            b.memset(Sb, 0.0, ["Sb"])
            order = list(range(NC)) if e == 0 else list(range(NC - 1, -1, -1))
            for idx, ch in enumerate(order):
                if idx == NCS:
                    b.ts(Sm, Sm, flag, None, ALU.mult, None, ["Sm"] + CK, ["Sm"])
                    b.copy(Sb, Sm, ["Sm"], ["Sb"], eng="act")
                cs = slice(ch * 64, (ch + 1) * 64)
                pA, pAk = b.ps(4)
                pB, pBk = b.ps(5)
                pC, pCk = b.ps(6)
                pD, pDk = b.ps(7)
                b.mm(pA[0:H, 0:128], kT[:, cs], Sb, True, True, ["kT", "Sb"], [pAk])
                b.mm(pA[0:H, 128:256], qT[:, cs], Sb, True, True, ["qT", "Sb"], [pAk])
                b.ts(vb, vtm[:, ch, :], be[:, ch, col:col + 1], None, ALU.mult, None, ["vtm", "be"], ["vb"], eng="pool")
                b.stt(negr, pA[0:H, 0:128], bk[:, ch, col:col + 1], vb, ALU.mult, ALU.subtract, [pAk, "bk", "vb"], ["negr"])
                b.mm(pB[0:H, 0:128], Ttb[:, ch, :], negr, True, True, ["Ttb", "negr"], [pBk])
                S.act(lambda en, pB=pB: en.mul(out=vnew, in_=pB[0:H, 0:128], mul=-1.0), [pBk], ["vnew"])
                b.mm(pC[0:H, 0:128], qkb[:, ch, :], vnew, True, True, ["qkb", "vnew"], [pCk])
                b.ts(kdec, ktm[:, ch, :], ed[:, ch, col:col + 1], None, ALU.mult, None, ["ktm", "ed"], ["kdec"], eng="pool")
                b.mm(pD[:, 0:128], kdec, vnew, True, True, ["kdec", "vnew"], [pDk])
                b.copy(o1, pC[0:H, 0:128], [pCk], ["o1"], eng="act")
                if e == 0:
                    b.stt(osum[:, ch, :], pA[0:H, 128:256], eg[:, ch, col:col + 1], o1, ALU.mult, ALU.add,
                          [pAk, "eg", "o1"], ["osum"])
                else:
                    b.stt(o2, pA[0:H, 128:256], eg[:, ch, col:col + 1], o1, ALU.mult, ALU.add, [pAk, "eg", "o1"], ["o2"])
                    b.tt(osum[:, ch, :], osum[:, ch, :], o2, ALU.add, ["osum", "o2"], ["osum"])
                b.stt(Sm, Sm, gL[:, ch, col:col + 1], pD[:, 0:128], ALU.mult, ALU.add, ["Sm", "gL", pDk], ["Sm"])
                b.copy(Sb, Sm, ["Sm"], ["Sb"], eng="act")
        if MCS == "3":
            return
        for ch in range(NC):
            cs = slice(ch * 64, (ch + 1) * 64)
            pT, pTk = b.ps(ch % 2)
            b.act(sqc, osum[:, ch, :], AF.Square, ["osum"], ["sqc"])
            S.dve(lambda en: en.reduce_sum(out=ssq, in_=sqc, axis=AX.X), ["sqc"], ["ssq"])
            b.act(ssq, ssq, AF.Sqrt, ["ssq"], ["ssq"], bias=EPS, scale=1.0 / 128.0)
            S.dve(lambda en: en.reciprocal(out=ssq, in_=ssq), ["ssq"], ["ssq"])
            b.ts(on, osum[:, ch, :], ssq, None, ALU.mult, None, ["osum", "ssq"], ["on"])
            b.tr(pT[:, 0:H], on, I64, ["on"] + CK, [pTk])
            b.stt(yC[:, cs], pT[:, 0:H], cols[:, C_CNG + hd:C_CNG + hd + 1], zs[:, cs], ALU.mult, ALU.mult,
                  [pTk, "cols", "xp"], ["sqb"])
        b.dma(Y[16 + hd, :, :], yC, ["sqb"], [], q="act")


def phase3a(b, c, layer):
    T = b.T
    TB = min(T, 512)
    W, Y, MG, hA, hM = c["W"], c["Y"], c["MG"], c["hA"], c["hM"]
    for tb in range(T // TB):
        b.phase()
        t0 = tb * TB
        b.wsetup()
        Yt = b.alloc([24, TB], BF16)
        h = b.alloc([16, TB], F32)
        mrg = b.alloc([16, TB], BF16)
        mgt = [b.alloc([3, TB], BF16) for _ in range(2)]
        sg = [b.alloc([TB], F32) for _ in range(3)]
        acc = b.alloc([TB], F32)
        tmp = b.alloc([TB], F32)
        b.dma(Yt, Y[:, :, t0:t0 + TB].rearrange("c p t -> p c t"), [], ["Yt"])
        b.dma(h, hA[:, :, t0:t0 + TB].rearrange("c p t -> p c t"), [], ["h%d" % i for i in range(16)])
        for m in range(16):
            mg_ = mgt[m % 2]
            pks = []
            for g in range(3):
                mk = "mgt%d_%d" % (m % 2, g)
                b.dma(mg_[:, g, :], MG[g * 16 + m, :, t0:t0 + TB], [], [mk])
                wt, wk = b.load_w(W["w_branch"][layer, g, :, m * 128:(m + 1) * 128], 8)
                pst, pk = b.ps(g + 4 * (m % 2))
                pks.append((pst, pk))
                for k in range(8):
                    b.mm(pst[:, 0:TB], wt[:, k, :], Yt[:, g * 8 + k, :], k == 0, k == 7, [wk, "Yt"], [pk])
                b.act(sg[g], mg_[:, g, :], AF.Sigmoid, [mk], ["sg%d" % g])
            b.tt(acc, pks[0][0][:, 0:TB], sg[0], ALU.mult, [pks[0][1], "sg0"], ["acc"])
            b.tt(tmp, pks[1][0][:, 0:TB], sg[1], ALU.mult, [pks[1][1], "sg1"], ["tmp"])
            b.tt(acc, acc, tmp, ALU.add, ["acc", "tmp"], ["acc"])
            b.tt(tmp, pks[2][0][:, 0:TB], sg[2], ALU.mult, [pks[2][1], "sg2"], ["tmp"])
            b.tt(mrg[:, m, :], acc, tmp, ALU.add, ["acc", "tmp"], ["mrg"])
        for m in range(16):
            wt, wk = b.load_w(W["w_out"][layer, :, m * 128:(m + 1) * 128], 16)
            pst, pk = b.ps(3 + 4 * (m % 2))
            for k in range(16):
                b.mm(pst[:, 0:TB], wt[:, k, :], mrg[:, k, :], k == 0, k == 15, [wk, "mrg"], [pk])
            b.tt(h[:, m, :], h[:, m, :], pst[:, 0:TB], ALU.add, [pk, "h%d" % m], ["h%d" % m])
        b.dma(hM[:, :, t0:t0 + TB].rearrange("c p t -> p c t"), h, ["h%d" % i for i in range(16)], [], q="act")


def phase3b(b, c, layer, last):
    T, SEG = b.T, b.SEG
    TB = min(SEG, 512)
    NE = TB + 2
    cols, CK, flag, S, ident = b.cols, b.CK, b.flag, b.S, b.ident
    W, hA, hM, p_d, y_d = c["W"], c["hA"], c["hM"], c["p_d"], c["y_d"]
    HK = ["h%d" % i for i in range(16)]
    for tb in range(T // TB):
        b.phase()
        t0 = tb * TB
        b.wsetup()
        h = b.alloc([16, NE], F32)
        xn = b.alloc([16, NE], BF16)
        sq = [b.alloc([NE], BF16) for _ in range(2)]
        rstd = b.alloc([NE], F32)
        act_ = b.alloc([48, TB], BF16)
        cg = b.alloc([TB], F32)
        cu = b.alloc([TB], F32)
        pin = b.alloc([256], F32)
        pT = b.alloc([2, TB], BF16)
        b.dma(h[:, :, 1:TB + 1], hM[:, :, t0:t0 + TB].rearrange("c p t -> p c t"), [], HK)
        if t0 > 0:
            b.dma(h[:, :, 0:1], hM[:, :, t0 - 1:t0].rearrange("c p t -> p c t"), [], ["hl"], slow=True)
        else:
            b.memset(h[:, :, 0:1], 0.0, ["hl"])
        if t0 + TB < T:
            b.dma(h[:, :, TB + 1:TB + 2], hM[:, :, t0 + TB:t0 + TB + 1].rearrange("c p t -> p c t"), [], ["hr"], slow=True)
        else:
            b.memset(h[:, :, TB + 1:TB + 2], 0.0, ["hr"])
        HALL = HK + ["hl", "hr"]

        def rms(lo, hi, gcol0, out, okey, hkeys):
            n = hi - lo
            cb = [(a + lo, e_ + lo) for (a, e_) in b.colblocks(n)]
            for ch in range(16):
                q, qk = sq[ch % 2], "sq%d" % (ch % 2)
                b.act(q[:, lo:hi], h[:, ch, lo:hi], AF.Square, hkeys, [qk])
                for i, (a, e_) in enumerate(cb):
                    pst, pk = b.ps(i)
                    b.mm(pst[:, 0:e_ - a], b.ones_b, q[:, a:e_], ch == 0, ch == 15, [qk] + CK, [pk])
            for i, (a, e_) in enumerate(cb):
                pst, pk = b.ps(i)
                b.act(rstd[:, a:e_], pst[:, 0:e_ - a], AF.Sqrt, [pk], ["rstd"], bias=EPS, scale=1.0 / D)
            S.dve(lambda en: en.reciprocal(out=rstd[:, lo:hi], in_=rstd[:, lo:hi]), ["rstd"], ["rstd"])
            for ch in range(16):
                b.stt(out[:, ch, :], h[:, ch, lo:hi], cols[:, gcol0 + ch:gcol0 + ch + 1], rstd[:, lo:hi],
                      ALU.mult, ALU.mult, hkeys + ["cols", "rstd"], [okey])

        rms(0, NE, C_FFNG, xn, "xn", HALL)
        if t0 == SEG:
            b.ts(xn[:, :, 0:1], xn[:, :, 0:1], flag, None, ALU.mult, None, ["xn"] + CK, ["xn"])
        if t0 + TB == SEG:
            b.ts(xn[:, :, TB + 1:TB + 2], xn[:, :, TB + 1:TB + 2], flag, None, ALU.mult, None, ["xn"] + CK, ["xn"])
        cbs = b.colblocks(NE)
        for j in range(48):
            wg, wgk = b.load_w(W["ffn_w_up"][layer, :, j * 128:(j + 1) * 128], 16)
            pg = b.psum[2 * (j % 2)]
            pgk = ["ps%d" % (4 * (j % 2)), "ps%d" % (4 * (j % 2) + 1)]
            for (a, e_) in cbs:
                for k in range(16):
                    b.mm(pg[:, a:e_], wg[:, k, :], xn[:, k, a:e_], k == 0, k == 15, [wgk, "xn"], pgk)
            wu, wuk = b.load_w(W["ffn_w_up"][layer, :, DFF + j * 128:DFF + (j + 1) * 128], 16)
            pu = b.psum[2 * (j % 2) + 1]
            puk = ["ps%d" % (4 * (j % 2) + 2), "ps%d" % (4 * (j % 2) + 3)]
            for (a, e_) in cbs:
                for k in range(16):
                    b.mm(pu[:, a:e_], wu[:, k, :], xn[:, k, a:e_], k == 0, k == 15, [wuk, "xn"], puk)
            for (pp, ppk, dst, dk, ci) in ((pg, pgk, cg, "cg", j), (pu, puk, cu, "cu", 48 + j)):
                wc = lambda k: cols[:, C_FCW + k * 96 + ci:C_FCW + k * 96 + ci + 1]
                b.ts(dst, pp[:, 1:TB + 1], wc(1), None, ALU.mult, None, ppk + ["cols"], [dk])
                b.stt(dst, pp[:, 0:TB], wc(0), dst, ALU.mult, ALU.add, ppk + ["cols", dk], [dk])
                b.stt(dst, pp[:, 2:TB + 2], wc(2), dst, ALU.mult, ALU.add, ppk + ["cols", dk], [dk])
            b.act(cg, cg, AF.Gelu_apprx_tanh, ["cg"], ["cg"])
            b.tt(act_[:, j, :], cg, cu, ALU.mult, ["cg", "cu"], ["act"])
        for m in range(16):
            pst, pk = b.ps(m % 2)
            for q_ in range(3):
                wt, wk = b.load_w(W["ffn_w_down"][layer, q_ * 2048:(q_ + 1) * 2048, m * 128:(m + 1) * 128], 16)
                for k in range(16):
                    b.mm(pst[:, 0:TB], wt[:, k, :], act_[:, q_ * 16 + k, :], q_ == 0 and k == 0, q_ == 2 and k == 15,
                         [wk, "act"], [pk])
            b.tt(h[:, m, 1:TB + 1], h[:, m, 1:TB + 1], pst[:, 0:TB], ALU.add, [pk, "h%d" % m], ["h%d" % m])
        xn3 = xn[:, :, 1:TB + 1]
        rms(1, TB + 1, C_PLEG, xn3, "xn", HK)
        for tt in range(TB // 128):
            b.dma(pin, p_d[layer, t0 + tt * 128:t0 + (tt + 1) * 128, :], [], ["pin"])
            pst, pk = b.ps(2 + tt % 2)
            for k in range(2):
                b.tr(pst[:, k * 128:(k + 1) * 128], pin[:, k * 128:(k + 1) * 128], ident, ["pin"] + CK, [pk])
            b.evac(pT[:, :, tt * 128:(tt + 1) * 128], pst[:, 0:256].rearrange("p (a b) -> p a b", a=2), [pk], ["pT"])
        for m in range(16):
            wg, wgk = b.load_w(W["ple_w_gate"][layer, :, m * 128:(m + 1) * 128], 16)
            pa, pak = b.ps(4 + 2 * (m % 2))
            for k in range(16):
                b.mm(pa[:, 0:TB], wg[:, k, :], xn3[:, k, :], k == 0, k == 15, [wgk, "xn"], [pak])
            wp, wpk = b.load_w(W["ple_w_proj"][layer, :, m * 128:(m + 1) * 128], 2)
            pb_, pbk = b.ps(5 + 2 * (m % 2))
            for k in range(2):
                b.mm(pb_[:, 0:TB], wp[:, k, :], pT[:, k, :], k == 0, k == 1, [wpk, "pT"], [pbk])
            b.act(cg, pa[:, 0:TB], AF.Sigmoid, [pak], ["cg"])
            b.tt(cu, pb_[:, 0:TB], cg, ALU.mult, [pbk, "cg"], ["cu"])
            b.tt(h[:, m, 1:TB + 1], h[:, m, 1:TB + 1], cu, ALU.add, ["cu", "h%d" % m], ["h%d" % m])
        if not last:
            b.dma(hA[:, :, t0:t0 + TB].rearrange("c p t -> p c t"), h[:, :, 1:TB + 1], HK, [], q="act")
        else:
            if b.debug:
                b.dma(hA[:, :, t0:t0 + TB].rearrange("c p t -> p c t"), h[:, :, 1:TB + 1], HK, [], q="act")
            xf = b.alloc([16, TB], F32)
            yo = [b.alloc([D], F32) for _ in range(2)]
            rms(1, TB + 1, C_FING, xf, "xf", HK)
            for tt in range(TB // 128):
                o, ok = yo[tt % 2], "yo%d" % (tt % 2)
                for g in range(4):
                    pst, pk = b.ps(4 + (tt * 4 + g) % 4)
                    for j in range(4):
                        ch = g * 4 + j
                        b.tr(pst[:, j * 128:(j + 1) * 128], xf[:, ch, tt * 128:(tt + 1) * 128], ident, ["xf"] + CK, [pk])
                    b.evac(o[:, g * 512:(g + 1) * 512], pst, [pk], [ok])
                b.dma(y_d[t0 + tt * 128:t0 + (tt + 1) * 128, :], o, [ok], [], q="act")


_NC_CACHE = {}


def kernel(**inputs):
    SEG = 2048
    T = 2 * SEG
    if "nc" not in _NC_CACHE:
        _NC_CACHE["nc"] = build_nc(SEG=SEG)[0]
    nc = _NC_CACHE["nc"]
    xp_ = np.asarray(inputs["x_prompt"], dtype=np.float32)
    xs_ = np.asarray(inputs["x_sample"], dtype=np.float32)
    pp_ = np.asarray(inputs["p_prompt"], dtype=np.float32)
    ps_ = np.asarray(inputs["p_sample"], dtype=np.float32)
    wts = {n: np.ascontiguousarray(np.asarray(inputs[n], dtype=np.float32)) for n, _ in WNAMES}
    in_maps = []
    for core in range(8):
        if core < 4:
            x = xp_[core]
            p = pp_[:, core]
            f = 1.0
        else:
            j = 2 * (core - 4)
            x = xs_[j:j + 2].reshape(T, D)
            p = ps_[:, j:j + 2].reshape(2, T, 256)
            f = 0.0
        m = {"x": np.ascontiguousarray(x), "p": np.ascontiguousarray(p),
             "flag": np.full((128, 1), f, np.float32)}
        m.update(wts)
        in_maps.append(m)
    res = run_bass_kernel_spmd(nc, in_maps, core_ids=list(range(8)))
    outs = [np.asarray(r["y"], dtype=np.float32) for r in res.results]
    y_prompt = np.stack(outs[0:4], axis=0)
    y_sample = np.concatenate([o.reshape(2, SEG, D) for o in outs[4:8]], axis=0)
    return (y_prompt, y_sample)
```

```python
import math
from contextlib import ExitStack

import numpy as np
import concourse.bass as bass
import concourse.mybir as mybir
from concourse.bass_utils import run_bass_kernel_spmd

F32 = mybir.dt.float32
BF16 = mybir.dt.bfloat16
AF = mybir.ActivationFunctionType
ALU = mybir.AluOpType
AX = mybir.AxisListType

D = 2048
KD = 16
DPROJ = 16432
DFF = 6144
EPS = 1e-6
NEG = -30000.0

COMPUTE = ("pe", "act", "dve", "pool")
QUEUES = ("pe", "act", "dve", "pool", "sp")
NDSEM = 8


class Op:
    __slots__ = ("eng", "fn", "dma", "deps", "sig", "sem", "sigval", "waited")

    def __init__(self, eng, fn, dma):
        self.eng = eng
        self.fn = fn
        self.dma = dma
        self.deps = []
        self.sig = False
        self.sem = None
        self.sigval = 0
        self.waited = False


class Sched:
    def __init__(self):
        self.q = {e: [] for e in QUEUES}
        self.last_w = {}
        self.readers = {}
        self.dma_hist = {e: [] for e in QUEUES}
        self.all_dmas = []
        self.bar = []
        self.bar_pending = {e: False for e in QUEUES}
        self.dmas_since_bar = []

    def barrier(self):
        deps = [self.q[e][-1] for e in COMPUTE if self.q[e] and self.q[e][-1].fn is not None]
        deps += self.dmas_since_bar
        self.bar = deps
        self.dmas_since_bar = []
        for e in QUEUES:
            self.bar_pending[e] = True

    def add(self, eng, fn, reads=(), writes=(), dma=False):
        op = Op(eng, fn, dma)
        deps = {}

        def consider(d, kind):
            if d is None:
                return
            if (not d.dma) and d.eng == eng:
                if (not dma) and eng == "pe":
                    return
            deps[id(d)] = d

        for k in reads:
            consider(self.last_w.get(k), "raw")
            if k.startswith("ps"):
                for r in self.readers.get(k, ()):
                    if r.eng != eng:
                        deps[id(r)] = r
        for k in writes:
            consider(self.last_w.get(k), "waw")
            for r in self.readers.get(k, ()):
                consider(r, "war")
        if self.bar_pending[eng]:
            self.bar_pending[eng] = False
            for d in self.bar:
                if d.dma or d.eng != eng:
                    deps[id(d)] = d
        if dma:
            hist = self.dma_hist[eng]
            if len(hist) >= NDSEM:
                d = hist[-NDSEM]
                deps[id(d)] = d
            hist.append(op)
            self.all_dmas.append(op)
            self.dmas_since_bar.append(op)
        op.deps = list(deps.values())
        for k in reads:
            self.readers.setdefault(k, []).append(op)
        for k in writes:
            self.last_w[k] = op
            self.readers[k] = []
        self.q[eng].append(op)
        return op

    def pe(self, fn, reads=(), writes=()):
        return self.add("pe", fn, reads, writes)

    def act(self, fn, reads=(), writes=()):
        return self.add("act", fn, reads, writes)

    def dve(self, fn, reads=(), writes=()):
        return self.add("dve", fn, reads, writes)

    def pool(self, fn, reads=(), writes=()):
        return self.add("pool", fn, reads, writes)

    def dma(self, q, fn, reads=(), writes=()):
        return self.add(q, fn, reads, writes, dma=True)

    def emit(self, nc):
        fin = Op("sp", None, False)
        fin.deps = list(self.all_dmas)
        self.q["sp"].append(fin)
        for e in QUEUES:
            for op in self.q[e]:
                for d in op.deps:
                    d.waited = True
        with ExitStack() as es:
            csem = {e: es.enter_context(nc.semaphore("s_" + e)) for e in COMPUTE}
            dsem = {e: [es.enter_context(nc.semaphore("d_%s_%d" % (e, i))) for i in range(NDSEM)]
                    for e in QUEUES}
            for e in QUEUES:
                cnt = 0
                dcnt = [0] * NDSEM
                di = 0
                for op in self.q[e]:
                    if op.fn is None:
                        continue
                    if op.dma:
                        s = di % NDSEM
                        di += 1
                        dcnt[s] += 16
                        op.sem = dsem[e][s]
                        op.sigval = dcnt[s]
                        op.sig = True
                    elif op.waited:
                        cnt += 1
                        op.sem = csem[e]
                        op.sigval = cnt
                        op.sig = True
            block = es.enter_context(nc.Block())
            engs = {"pe": block.tensor, "act": block.scalar, "dve": block.vector,
                    "pool": block.gpsimd, "sp": block.sync}
            self.stats = {}
            for e in QUEUES:
                ops = self.q[e]
                nw = [0]

                def body(eng, ops=ops, nw=nw):
                    known = {}
                    for op in ops:
                        need = {}
                        for d in op.deps:
                            key = d.sem.name
                            if known.get(key, 0) >= d.sigval:
                                continue
                            if key not in need or need[key][1] < d.sigval:
                                need[key] = (d.sem, d.sigval)
                        for key, (sem, val) in need.items():
                            eng.wait_ge(sem, val)
                            known[key] = val
                            nw[0] += 1
                        if op.fn is None:
                            continue
                        ins = op.fn(eng)
                        if op.sig:
                            ins.then_inc(op.sem, 16 if op.dma else 1)

                engs[e](body)
                self.stats[e] = (len(ops), nw[0])


ARENA_WORDS = 48640


class B:
    def __init__(self, SEG, debug=False):
        self.SEG = SEG
        self.T = 2 * SEG
        self.debug = debug
        self.nc = bass.Bass("TRN2", target_bir_lowering=False)
        self.S = Sched()
        self.uid = 0

    def dram_in(self, name, shape):
        return self.nc.dram_tensor(name, list(shape), F32, kind="ExternalInput").ap()

    def scratch(self, name, shape, dt):
        kind = "ExternalOutput" if self.debug else "Internal"
        return self.nc.dram_tensor(name, list(shape), dt, kind=kind).ap()

    def alloc(self, free_shape, dt, parts=128):
        n = 1
        for s in free_shape:
            n *= s
        words = n if dt == F32 else (n + 1) // 2
        words = (words + 7) // 8 * 8
        assert self.off + words <= ARENA_WORDS, ("arena overflow", self.off, words)
        ap = self.arena[0:parts, self.off:self.off + words]
        self.off += words
        if dt == BF16:
            ap = ap.bitcast(BF16)
        ap = ap[:, 0:n]
        if len(free_shape) == 2:
            ap = ap.rearrange("p (a b) -> p a b", a=free_shape[0])
        elif len(free_shape) == 3:
            ap = ap.rearrange("p (a b c) -> p a b c", a=free_shape[0], b=free_shape[1])
        return ap

    def key(self, base):
        self.uid += 1
        return "%s#%d" % (base, self.uid)

    def phase(self):
        self.S.barrier()
        self.off = self.persist_off

    def mm(self, out, lhsT, rhs, start, stop, R, W):
        self.S.pe(lambda e: e.matmul(out, lhsT=lhsT, rhs=rhs, start=start, stop=stop), R, W)

    def tr(self, out, in_, ident, R, W):
        self.S.pe(lambda e: e.transpose(out, in_, ident), R, W)

    def act(self, out, in_, func, R, W, bias=0.0, scale=1.0):
        self.S.act(lambda e: e.activation(out=out, in_=in_, func=func, bias=bias, scale=scale), R, W)

    def tt(self, out, a, b, op, R, W, eng="dve"):
        self.S.add(eng, lambda e: e.tensor_tensor(out=out, in0=a, in1=b, op=op), R, W)

    def ts(self, out, a, s1, s2, op0, op1, R, W, eng="dve"):
        if s2 is None:
            self.S.add(eng, lambda e: e.tensor_scalar(out=out, in0=a, scalar1=s1, scalar2=None, op0=op0), R, W)
        else:
            self.S.add(eng, lambda e: e.tensor_scalar(out=out, in0=a, scalar1=s1, scalar2=s2, op0=op0, op1=op1), R, W)

    def stt(self, out, a, s, b, op0, op1, R, W):
        self.S.dve(lambda e: e.scalar_tensor_tensor(out=out, in0=a, scalar=s, in1=b, op0=op0, op1=op1), R, W)

    def copy(self, out, in_, R, W, eng="dve"):
        if eng == "act":
            self.S.act(lambda e: e.copy(out=out, in_=in_), R, W)
        else:
            self.S.add(eng, lambda e: e.tensor_copy(out=out, in_=in_), R, W)

    def dma(self, out, in_, R, W, q="sp", slow=False):
        if slow:
            self.S.dma(q, lambda e: e.dma_start(out=out, in_=in_, allow_slow_non_contiguous=True), R, W)
        else:
            self.S.dma(q, lambda e: e.dma_start(out=out, in_=in_), R, W)

    def memset(self, ap, val, W, eng="pool"):
        self.S.add(eng, lambda e: e.memset(ap, val), (), W)

    def evac(self, out, in_, R, W):
        self.ev = getattr(self, "ev", 0) + 1
        self.copy(out, in_, R, W, eng=("act" if self.ev % 2 else "dve"))

    def wsetup(self):
        self.wst = [self.alloc([16, 128], F32) for _ in range(2)]
        self.wbf = [self.alloc([16, 128], BF16) for _ in range(3)]
        self.wi = 0

    def load_w(self, src, nk, ncol=128):
        i = self.wi
        self.wi += 1
        st = self.wst[i % 2]
        bf = self.wbf[i % 3]
        ks, kb = "wst%d" % (i % 2), "wbf%d" % (i % 3)
        wc = getattr(self, "wcache", None)
        if wc is not None:
            idx = wc["idx"]
            wc["idx"] += 1
            slot = wc["ap"][idx, :, 0:nk * ncol].rearrange("p (k n) -> p k n", n=ncol)
            if wc["mode"] == "use":
                self.dma(bf[:, 0:nk, 0:ncol], slot, [], [kb])
                return bf, kb
        self.dma(st[:, 0:nk, 0:ncol], src.rearrange("(k p) n -> p k n", p=128), [], [ks])
        self.copy(bf[:, 0:nk, 0:ncol], st[:, 0:nk, 0:ncol], [ks], [kb], eng="pool")
        if wc is not None:
            self.dma(slot, bf[:, 0:nk, 0:ncol], [kb], [], q="pool")
        return bf, kb

    def colblocks(self, n):
        return [(a, min(a + 512, n)) for a in range(0, n, 512)]

    def ps(self, i):
        return self.psum[i // 2][:, (i % 2) * 512:(i % 2) * 512 + 512], "ps%d" % i


WNAMES = [
    ("norm_mix_g", (2, 2048)), ("w_in", (2, 2048, 16432)), ("rglru_conv_w", (2, 4, 1024)),
    ("rglru_conv_b", (2, 1024)), ("rglru_gate_w", (2, 2, 2, 8, 128, 128)), ("rglru_gate_b", (2, 2, 2, 1024)),
    ("rglru_lambda", (2, 2, 1024)), ("mlstm_gate_b", (2, 16)), ("mlstm_norm_g", (2, 1024)),
    ("gdn_conv_w", (2, 4, 3072)), ("gdn_A_log", (2, 2, 8)), ("gdn_dt_bias", (2, 2, 8)),
    ("gdn_norm_g", (2, 1024)), ("w_branch", (2, 3, 1024, 2048)), ("w_out", (2, 2048, 2048)),
    ("norm_ffn_g", (2, 2048)), ("ffn_w_up", (2, 2048, 12288)), ("ffn_conv_w", (2, 3, 12288)),
    ("ffn_w_down", (2, 6144, 2048)), ("norm_ple_g", (2, 2048)), ("ple_w_gate", (2, 2048, 2048)),
    ("ple_w_proj", (2, 256, 2048)), ("norm_final_g", (2048,)),
]

C_MIXG, C_FFNG, C_PLEG, C_FING = 0, 16, 32, 48
C_ACW, C_ACB, C_AGB, C_ALAM = 64, 96, 104, 136
C_BNG, C_CCW, C_CNG, C_FCW = 152, 160, 256, 264
C_AC8 = 552
NCOLS = 568


def build_nc(SEG=2048, debug=False, nlayers=2, stop_after=None):
    b = B(SEG, debug)
    nc, S, T = b.nc, b.S, b.T
    NT = T // 128
    x_d = b.dram_in("x", (T, D))
    p_d = b.dram_in("p", (2, T, 256))
    flag_d = b.dram_in("flag", (128, 1))
    W = {n: b.dram_in(n, s) for n, s in WNAMES}
    y_d = nc.dram_tensor("y", [T, D], F32, kind="ExternalOutput").ap()
    hA = b.scratch("hA", (16, 128, T), F32)
    hM = b.scratch("hM", (16, 128, T), F32)
    Ain = b.scratch("Ain", (16, 128, T), BF16)
    Bqk = b.scratch("Bqk", (16, 128, T), BF16)
    Bo = b.scratch("Bo", (8, 128, T), BF16)
    Bv = b.scratch("Bv", (T, 1024), BF16)
    Bk = b.scratch("Bk", (T, 1024), BF16)
    Gt = b.scratch("Gt", (T, 48), F32)
    Cqkv = b.scratch("Cqkv", (24, 128, T), BF16)
    Cz = b.scratch("Cz", (8, 128, T), BF16)
    MG = b.scratch("MG", (48, 128, T), BF16)
    Y = b.scratch("Y", (24, 128, T), BF16)
    Wc3a = nc.dram_tensor("Wc3a", [64, 128, 2048], BF16, kind="Internal").ap()
    Wc3b = nc.dram_tensor("Wc3b", [208, 128, 2048], BF16, kind="Internal").ap()

    with ExitStack() as es:
        b.arena = es.enter_context(nc.sbuf_tensor("arena", [128, ARENA_WORDS], F32))
        b.psum = [es.enter_context(nc.psum_tensor("psum%d" % i, [128, 1024], F32)) for i in range(4)]
        b.off = 0
        ident = b.alloc([128], F32)
        ones_f = b.alloc([128], F32)
        ones_b = b.alloc([128], BF16)
        Uf = b.alloc([128], F32)
        Ub = b.alloc([128], F32)
        NGf = b.alloc([128], F32)
        NGb = b.alloc([128], F32)
        PSf = b.alloc([128], F32)
        PSb = b.alloc([128], F32)
        flag = b.alloc([1], F32)
        cols = b.alloc([NCOLS], F32)
        gbB = b.alloc([16], F32)
        dtb = b.alloc([16], F32)
        negA = b.alloc([16], F32)
        b.persist_off = b.off

        CK = []

        def aff(ap, src_val, cmp, fill, pattern, cm):
            k = "const%d" % len(CK)
            CK.append(k)
            b.memset(ap, src_val, [k])
            S.pool(lambda e: e.affine_select(out=ap, in_=ap, compare_op=cmp, fill=fill, base=0,
                                              pattern=pattern, channel_multiplier=cm), [k], [k])

        aff(ident, 0.0, ALU.not_equal, 1.0, [[-1, 128]], 1)
        b.memset(ones_f, 1.0, ["const_of"])
        b.memset(ones_b, 1.0, ["const_ob"])
        CK += ["const_of", "const_ob", "const_flag"]
        aff(Uf, 1.0, ALU.is_ge, 0.0, [[1, 128]], -1)
        aff(Ub, 1.0, ALU.is_ge, 0.0, [[-1, 128]], 1)
        aff(NGf, 0.0, ALU.is_ge, NEG, [[1, 128]], -1)
        aff(NGb, 0.0, ALU.is_ge, NEG, [[-1, 128]], 1)
        aff(PSf, 0.0, ALU.is_gt, -NEG, [[-1, 128]], 1)
        aff(PSb, 0.0, ALU.is_gt, -NEG, [[1, 128]], -1)
        b.dma(flag, flag_d, [], ["const_flag"])

        def load_cols(layer):
            b.phase()
            stage = b.alloc([128], F32)
            items = [
                (W["norm_mix_g"][layer].rearrange("(c p) -> c p", p=128), 16, C_MIXG),
                (W["norm_ffn_g"][layer].rearrange("(c p) -> c p", p=128), 16, C_FFNG),
                (W["norm_ple_g"][layer].rearrange("(c p) -> c p", p=128), 16, C_PLEG),
                (W["norm_final_g"].rearrange("(c p) -> c p", p=128), 16, C_FING),
                (W["rglru_conv_w"][layer].rearrange("k (n p) -> (k n) p", p=128), 32, C_ACW),
                (W["rglru_conv_b"][layer].rearrange("(n p) -> n p", p=128), 8, C_ACB),
                (W["rglru_gate_b"][layer].rearrange("e g (n p) -> (e g n) p", p=128), 32, C_AGB),
                (W["rglru_lambda"][layer].rearrange("e (n p) -> (e n) p", p=128), 16, C_ALAM),
                (W["mlstm_norm_g"][layer].rearrange("(n p) -> n p", p=128), 8, C_BNG),
                (W["gdn_conv_w"][layer].rearrange("k (c p) -> (k c) p", p=128), 96, C_CCW),
                (W["gdn_norm_g"][layer].rearrange("(n p) -> n p", p=128), 8, C_CNG),
            ]
            fcw = W["ffn_conv_w"][layer].rearrange("k (c p) -> (k c) p", p=128)
            for q in range(3):
                items.append((fcw[q * 96:(q + 1) * 96, :], 96, C_FCW + q * 96))
            for i, (src, R, c0) in enumerate(items):
                pst, pk = b.ps(i % 2)
                b.dma(stage[0:R, :], src, [], ["lc_stage"])
                b.tr(pst[:, 0:R], stage[0:R, :], ident[0:R, 0:R], ["lc_stage"] + CK, [pk])
                b.copy(cols[:, c0:c0 + R], pst[:, 0:R], [pk], ["cols"], eng="act")
            tmp = b.alloc([16], F32)
            b.act(tmp, cols[:, C_ALAM:C_ALAM + 16], AF.Exp, ["cols"], ["lc_tmp"], scale=-1.0)
            b.act(tmp, tmp, AF.Ln, ["lc_tmp"], ["lc_tmp"], bias=1.0)
            b.S.act(lambda e: e.mul(out=cols[:, C_AC8:C_AC8 + 16], in_=tmp, mul=-8.0), ["lc_tmp"], ["cols"])
            b.dma(gbB, W["mlstm_gate_b"][layer:layer + 1, :].partition_broadcast(128), [], ["bc"])
            b.dma(dtb, W["gdn_dt_bias"][layer:layer + 1].rearrange("a e h -> a (e h)").partition_broadcast(128), [], ["bc"])
            b.dma(negA, W["gdn_A_log"][layer:layer + 1].rearrange("a e h -> a (e h)").partition_broadcast(128), [], ["bc"])
            b.act(negA, negA, AF.Exp, ["bc"], ["bc"])
            b.S.act(lambda e: e.mul(out=negA, in_=negA, mul=-1.0), ["bc"], ["bc"])

        def phase0():
            b.phase()
            xin = [b.alloc([D], F32) for _ in range(2)]
            xo = [b.alloc([16, 128], F32) for _ in range(2)]
            for tt in range(NT):
                xi, xk = xin[tt % 2], "xin%d" % (tt % 2)
                o, ok = xo[tt % 2], "xo%d" % (tt % 2)
                b.dma(xi, x_d[tt * 128:(tt + 1) * 128, :], [], [xk])
                for g in range(4):
                    pst, pk = b.ps(g % 2 + 2 * (tt % 2))
                    for j in range(4):
                        c = g * 4 + j
                        b.tr(pst[:, j * 128:(j + 1) * 128], xi[:, c * 128:(c + 1) * 128], ident, [xk] + CK, [pk])
                    b.evac(o[:, g * 4:(g + 1) * 4, :], pst.rearrange("p (a b) -> p a b", a=4), [pk], [ok])
                b.dma(hA[:, :, tt * 128:(tt + 1) * 128].rearrange("c p t -> p c t"), o, [ok], ["hA"], q="act")

        b.cols, b.ident, b.ones_b, b.ones_f, b.flag, b.CK = cols, ident, ones_b, ones_f, flag, CK
        ctx = dict(locals())
        phase0()
        for layer in range(nlayers):
            load_cols(layer)
            phase1(b, ctx, layer)
            if stop_after == "p1":
                break
            mixerA(b, ctx, layer)
            if stop_after == "mA":
                break
            mixerB(b, ctx, layer)
            if stop_after == "mB":
                break
            mixerC(b, ctx, layer)
            if stop_after == "mC":
                break
            phase3a(b, ctx, layer)
            if stop_after == "p3a":
                break
            phase3b(b, ctx, layer, last=(layer == nlayers - 1))
        S.emit(nc)
    return nc, b


def phase1(b, c, layer):
    T = b.T
    TB = min(T, 2048)
    cols, CK = b.cols, b.CK
    W, hA = c["W"], c["hA"]
    groups = [(0, 16, c["Ain"]), (2048, 16, c["Bqk"]), (5120, 8, c["Bo"]), (6160, 24, c["Cqkv"]),
              (9232, 8, c["Cz"]), (10288, 48, c["MG"])]
    for tb in range(T // TB):
        b.phase()
        t0 = tb * TB
        b.wsetup()
        xn = b.alloc([16, TB], BF16)
        hc = [b.alloc([TB], F32) for _ in range(2)]
        sq = [b.alloc([TB], BF16) for _ in range(2)]
        rstd = b.alloc([TB], F32)
        ost = [b.alloc([TB], BF16) for _ in range(2)]
        cbs = b.colblocks(TB)
        for ch in range(16):
            h, hk = hc[ch % 2], "hc%d" % (ch % 2)
            q, qk = sq[ch % 2], "sq%d" % (ch % 2)
            b.dma(h, hA[ch, :, t0:t0 + TB], [], [hk])
            b.act(q, h, AF.Square, [hk], [qk])
            for i, (a, e_) in enumerate(cbs):
                pst, pk = b.ps(i)
                b.mm(pst[:, 0:e_ - a], b.ones_b, q[:, a:e_], ch == 0, ch == 15, [qk] + CK, [pk])
            b.ts(xn[:, ch, :], h, cols[:, C_MIXG + ch:C_MIXG + ch + 1], None, ALU.mult, None, [hk, "cols"], ["xn"])
        for i, (a, e_) in enumerate(cbs):
            pst, pk = b.ps(i)
            b.act(rstd[:, a:e_], pst[:, 0:e_ - a], AF.Sqrt, [pk], ["rstd"], bias=EPS, scale=1.0 / D)
        b.S.dve(lambda e: e.reciprocal(out=rstd, in_=rstd), ["rstd"], ["rstd"])
        for ch in range(16):
            b.tt(xn[:, ch, :], xn[:, ch, :], rstd, ALU.mult, ["xn", "rstd"], ["xn"])
        pi = 0
        oi = 0
        for col0, nch, dst in groups:
            for j in range(nch):
                wt, wk = b.load_w(W["w_in"][layer, :, col0 + j * 128:col0 + (j + 1) * 128], 16)
                o, ok = ost[oi % 2], "ost%d" % (oi % 2)
                oi += 1
                for (a, e_) in cbs:
                    pst, pk = b.ps(4 + pi % 4)
                    pi += 1
                    for k in range(16):
                        b.mm(pst[:, 0:e_ - a], wt[:, k, :], xn[:, k, a:e_], k == 0, k == 15, [wk, "xn"], [pk])
                    b.evac(o[:, a:e_], pst[:, 0:e_ - a], [pk], [ok])
                b.dma(dst[j, :, t0:t0 + TB], o, [ok], [], q="act")
        wtb = b.alloc([16, 512], BF16)
        otm = [b.alloc([512], BF16) for _ in range(2)]
        for (col0, dst) in [(4096, c["Bv"]), (3072, c["Bk"])]:
            for half in range(2):
                for qd in range(4):
                    cc = col0 + half * 512 + qd * 128
                    wt, wk = b.load_w(W["w_in"][layer, :, cc:cc + 128], 16)
                    b.copy(wtb[:, :, qd * 128:(qd + 1) * 128], wt[:, 0:16, :], [wk], ["wtb"], eng="pool")
                for tt in range(TB // 128):
                    pst, pk = b.ps(4 + pi % 4)
                    pi += 1
                    for k in range(16):
                        b.mm(pst, xn[:, k, tt * 128:(tt + 1) * 128], wtb[:, k, :], k == 0, k == 15, ["xn", "wtb"], [pk])
                    o, ok = otm[tt % 2], "otm%d" % (tt % 2)
                    b.evac(o, pst, [pk], [ok])
                    b.dma(dst[t0 + tt * 128:t0 + (tt + 1) * 128, half * 512:(half + 1) * 512], o, [ok], [], q="act")
        gst = b.alloc([16, 48], F32)
        gbf = b.alloc([16, 48], BF16)
        og = [b.alloc([48], F32) for _ in range(2)]
        b.dma(gst[:, :, 0:16], W["w_in"][layer, :, 6144:6160].rearrange("(k p) n -> p k n", p=128), [], ["gst"])
        b.dma(gst[:, :, 16:48], W["w_in"][layer, :, 10256:10288].rearrange("(k p) n -> p k n", p=128), [], ["gst"])
        b.copy(gbf, gst, ["gst"], ["gbf"], eng="pool")
        for tt in range(TB // 128):
            pst, pk = b.ps(4 + pi % 4)
            pi += 1
            for k in range(16):
                b.mm(pst[:, 0:48], xn[:, k, tt * 128:(tt + 1) * 128], gbf[:, k, :], k == 0, k == 15, ["xn", "gbf"], [pk])
            o, ok = og[tt % 2], "og%d" % (tt % 2)
            b.evac(o, pst[:, 0:48], [pk], [ok])
            b.dma(c["Gt"][t0 + tt * 128:t0 + (tt + 1) * 128, :], o, [ok], [], q="act")


def mixerA(b, c, layer):
    T, SEG = b.T, b.SEG
    cols, CK, flag, S = b.cols, b.CK, b.flag, b.S
    W, Ain, Y = c["W"], c["Ain"], c["Y"]
    b.phase()
    b.wsetup()
    PW = SEG + 3
    xp = b.alloc([2 * PW], BF16)
    ga = b.alloc([T], BF16)
    xcb = b.alloc([T], BF16)
    yb = b.alloc([T], BF16)
    xc = b.alloc([T], F32)
    rf = b.alloc([T], F32)
    uf = b.alloc([T], F32)
    sf = b.alloc([T], F32)
    hf = b.alloc([T], F32)
    hb = b.alloc([T], F32)
    carry = b.alloc([2], F32)
    cbs = b.colblocks(T)
    for n in range(8):
        for s in range(2):
            b.dma(xp[:, s * PW + 2:s * PW + 2 + SEG], Ain[n, :, s * SEG:(s + 1) * SEG], [], ["xp"])
        b.dma(ga, Ain[8 + n, :, :], [], ["ga"])
        b.memset(xp[:, 0:2], 0.0, ["xp"])
        b.memset(xp[:, 2 * PW - 1:2 * PW], 0.0, ["xp"])
        b.ts(xp[:, PW - 1:PW], xp[:, PW + 2:PW + 3], flag, None, ALU.mult, None, ["xp"] + CK, ["xp"])
        b.ts(xp[:, PW:PW + 2], xp[:, SEG:SEG + 2], flag, None, ALU.mult, None, ["xp"] + CK, ["xp"])
        for s in range(2):
            o = s * PW
            dst = xc[:, s * SEG:(s + 1) * SEG]
            b.ts(dst, xp[:, o:o + SEG], cols[:, C_ACW + n:C_ACW + n + 1], cols[:, C_ACB + n:C_ACB + n + 1],
                 ALU.mult, ALU.add, ["xp", "cols"], ["xc"])
            for k in range(1, 4):
                b.stt(dst, xp[:, o + k:o + k + SEG], cols[:, C_ACW + k * 8 + n:C_ACW + k * 8 + n + 1], dst,
                      ALU.mult, ALU.add, ["xp", "cols", "xc"], ["xc"])
        b.copy(xcb, xc, ["xc"], ["xcb"], eng="act")
        for e in range(2):
            wr, wrk = b.load_w(W["rglru_gate_w"][layer, e, 0, n], 1)
            wi_, wik = b.load_w(W["rglru_gate_w"][layer, e, 1, n], 1)
            for (wt, wk, dst, dk, g) in ((wr, wrk, rf, "rf", 0), (wi_, wik, uf, "uf", 1)):
                bias = cols[:, C_AGB + (e * 2 + g) * 8 + n:C_AGB + (e * 2 + g) * 8 + n + 1]
                for ci, (a, e_) in enumerate(cbs):
                    pst, pk = b.ps(ci % 8)
                    b.mm(pst[:, 0:e_ - a], wt[:, 0, :], xcb[:, a:e_], True, True, [wk, "xcb"], [pk])
                    b.act(dst[:, a:e_], pst[:, 0:e_ - a], AF.Sigmoid, [pk, "cols"], [dk], bias=bias)
            c8 = cols[:, C_AC8 + e * 8 + n:C_AC8 + e * 8 + n + 1]
            b.act(rf, rf, AF.Exp, ["rf", "cols"], ["rf"], scale=c8)
            b.act(sf, rf, AF.Square, ["rf"], ["sf"])
            b.act(sf, sf, AF.Sqrt, ["sf"], ["sf"], bias=1.0, scale=-1.0)
            b.tt(uf, uf, xc, ALU.mult, ["uf", "xc"], ["uf"])
            b.tt(uf, uf, sf, ALU.mult, ["uf", "sf"], ["uf"])
            if e == 0:
                S.dve(lambda en: en.tensor_tensor_scan(out=hf[:, 0:SEG], data0=rf[:, 0:SEG], data1=uf[:, 0:SEG],
                                                       initial=0.0, op0=ALU.mult, op1=ALU.add), ["rf", "uf"], ["hf"])
                b.tt(carry[:, 0:1], hf[:, SEG - 1:SEG], flag, ALU.mult, ["hf"] + CK, ["carry0"])
                S.dve(lambda en: en.tensor_tensor_scan(out=hf[:, SEG:T], data0=rf[:, SEG:T], data1=uf[:, SEG:T],
                                                       initial=carry[:, 0:1], op0=ALU.mult, op1=ALU.add),
                      ["rf", "uf", "carry0"], ["hf"])
            else:
                S.dve(lambda en: en.tensor_tensor_scan(out=hb[:, SEG:T][:, ::-1], data0=rf[:, SEG:T][:, ::-1],
                                                       data1=uf[:, SEG:T][:, ::-1], initial=0.0,
                                                       op0=ALU.mult, op1=ALU.add), ["rf", "uf"], ["hb"])
                b.tt(carry[:, 1:2], hb[:, SEG:SEG + 1], flag, ALU.mult, ["hb"] + CK, ["carry1"])
                S.dve(lambda en: en.tensor_tensor_scan(out=hb[:, 0:SEG][:, ::-1], data0=rf[:, 0:SEG][:, ::-1],
                                                       data1=uf[:, 0:SEG][:, ::-1], initial=carry[:, 1:2],
                                                       op0=ALU.mult, op1=ALU.add), ["rf", "uf", "carry1"], ["hb"])
        b.act(sf, ga, AF.Gelu_apprx_tanh, ["ga"], ["sf"])
        b.tt(hf, hf, hb, ALU.add, ["hf", "hb"], ["hf"])
        b.tt(yb, hf, sf, ALU.mult, ["hf", "sf"], ["yb"])
        b.dma(Y[n, :, :], yb, ["yb"], [], q="act")


def bc_mid(ap2, n):
    a = ap2.ap
    return bass.AP(ap2.tensor, ap2.offset, [list(a[0]), [0, n]] + [list(x) for x in a[1:]])


def bc_l3(ap3, n):
    a = ap3.ap
    return bass.AP(ap3.tensor, ap3.offset, [list(a[0]), list(a[1]), [0, n]])


def bc_last(ap1, n):
    a = ap1.ap
    return bass.AP(ap1.tensor, ap1.offset, [list(a[0]), [0, n]])


def mixerB(b, c, layer):
    T, SEG = b.T, b.SEG
    NCH, NCS = T // 128, SEG // 128
    cols, CK, flag, S, ident = b.cols, b.CK, b.flag, b.S, b.ident
    Uf, Ub, NGf, NGb, gbB = c["Uf"], c["Ub"], c["NGf"], c["NGb"], c["gbB"]
    Bqk, Bv, Bk, Bo, Gt, Y = c["Bqk"], c["Bv"], c["Bk"], c["Bo"], c["Gt"], c["Y"]
    b.phase()
    G = b.alloc([NCH, 16], F32)
    lf = b.alloc([NCH, 8], F32)
    sc = b.alloc([NCH, 8], F32)
    dec = b.alloc([NCH, 8], F32)
    b.dma(G, Gt[:, 0:16].rearrange("(c p) n -> p c n", p=128), [], ["G"])
    b.tt(G, G, bc_mid(gbB, NCH), ALU.add, ["G", "bc"], ["G"])
    b.act(lf[:, :, 0:4], G[:, :, 4:8], AF.Exp, ["G"], ["lf"], scale=-1.0)
    b.act(lf[:, :, 4:8], G[:, :, 12:16], AF.Exp, ["G"], ["lf"], scale=-1.0)
    b.act(lf, lf, AF.Ln, ["lf"], ["lf"], bias=1.0)
    S.act(lambda e: e.mul(out=lf, in_=lf, mul=-1.0), ["lf"], ["lf"])
    p0, k0 = b.ps(0)
    p1, k1 = b.ps(1)
    p2, k2 = b.ps(2)
    v3 = lambda p, n, w: p[:, 0:n * w].rearrange("p (a b) -> p a b", b=w)
    b.mm(v3(p0, NCH, 4), Uf, lf[:, :, 0:4], True, True, ["lf"] + CK, [k0])
    b.mm(v3(p1, NCH, 4), Ub, lf[:, :, 4:8], True, True, ["lf"] + CK, [k1])
    b.tt(sc[:, :, 0:4], G[:, :, 0:4], v3(p0, NCH, 4), ALU.subtract, ["G", k0], ["sc"])
    b.tt(sc[:, :, 4:8], G[:, :, 8:12], v3(p1, NCH, 4), ALU.subtract, ["G", k1], ["sc"])
    b.mm(v3(p2, NCH, 8), b.ones_f, lf, True, True, ["lf"] + CK, [k2])
    b.act(dec, v3(p2, NCH, 8), AF.Exp, [k2], ["dec"])

    qT = b.alloc([2, T], BF16)
    kT = b.alloc([2, T], BF16)
    ogT = b.alloc([2, T], BF16)
    yB = b.alloc([2, T], BF16)
    vaug = b.alloc([NCH, 260], BF16)
    ktm = b.alloc([NCH, 256], BF16)
    hsum = b.alloc([NCH, 256], F32)
    Cm = b.alloc([2, 260], F32)
    Cb = b.alloc([2, 260], BF16)
    Wrow = b.alloc([128], F32)
    DT = b.alloc([128], F32)
    ST = b.alloc([128], BF16)
    qw = b.alloc([2, 128], BF16)
    kw = b.alloc([256], BF16)
    rden = b.alloc([1], F32)
    sqc = b.alloc([256], F32)
    hn = b.alloc([256], F32)
    ssq = b.alloc([1], F32)
    sgo = b.alloc([2, 128], F32)
    for hd in range(4):
        b.dma(qT, Bqk[2 * hd:2 * hd + 2, :, :].rearrange("c p t -> p c t"), [], ["qT"])
        b.dma(kT, Bqk[8 + 2 * hd:8 + 2 * hd + 2, :, :].rearrange("c p t -> p c t"), [], ["kT"])
        b.dma(ogT, Bo[2 * hd:2 * hd + 2, :, :].rearrange("c p t -> p c t"), [], ["ogT"])
        b.dma(vaug[:, :, 0:256], Bv[:, hd * 256:(hd + 1) * 256].rearrange("(c p) e -> p c e", p=128), [], ["vaug"])
        b.dma(ktm, Bk[:, hd * 256:(hd + 1) * 256].rearrange("(c p) e -> p c e", p=128), [], ["ktm"])
        b.memset(vaug[:, :, 256:257], 1.0, ["vaug"])
        b.ts(qT, qT, 1.0 / 16.0, None, ALU.mult, None, ["qT"], ["qT"])
        for e in range(2):
            U, NG = (Uf, NGf) if e == 0 else (Ub, NGb)
            b.memset(Cm, 0.0, ["Cm"])
            b.memset(Cb, 0.0, ["Cb"])
            order = list(range(NCH)) if e == 0 else list(range(NCH - 1, -1, -1))
            for idx, ch in enumerate(order):
                if idx == NCS:
                    b.ts(Cm, Cm, flag, None, ALU.mult, None, ["Cm"] + CK, ["Cm"])
                    b.copy(Cb, Cm, ["Cm"], ["Cb"], eng="act")
                col = e * 4 + hd
                cs = slice(ch * 128, (ch + 1) * 128)
                lfb = bc_last(lf[:, ch, col:col + 1], 128)
                pB, pBk = b.ps(0)
                pD, pDk = b.ps(1)
                pS, pSk = b.ps(2)
                pO, pOk = b.ps(3)
                b.mm(pB[:, 0:128], lfb, U, True, True, ["lf"] + CK, [pBk])
                b.act(Wrow, pB[:, 0:128], AF.Exp, [pBk], ["Wrow"])
                b.mm(pD[:, 0:128], lfb, U, True, False, ["lf"] + CK, [pDk])
                b.mm(pD[:, 0:128], ident, NG, False, True, CK, [pDk])
                b.act(DT, pD[:, 0:128], AF.Exp, [pDk, "sc"], ["DT"], bias=sc[:, ch, col:col + 1])
                b.mm(pS[:, 0:128], kT[:, 0, cs], qT[:, 0, cs], True, False, ["kT", "qT"], [pSk])
                b.mm(pS[:, 0:128], kT[:, 1, cs], qT[:, 1, cs], False, True, ["kT", "qT"], [pSk])
                b.tt(ST, pS[:, 0:128], DT, ALU.mult, [pSk, "DT"], ["ST"])
                for dc in range(2):
                    b.tt(qw[:, dc, :], qT[:, dc, cs], Wrow, ALU.mult, ["qT", "Wrow"], ["qw"], eng="pool")
                b.mm(pO[:, 0:257], ST, vaug[:, ch, 0:257], True, False, ["ST", "vaug"], [pOk])
                b.mm(pO[:, 0:257], qw[:, 0, :], Cb[:, 0, 0:257], False, False, ["qw", "Cb"], [pOk])
                b.mm(pO[:, 0:257], qw[:, 1, :], Cb[:, 1, 0:257], False, True, ["qw", "Cb"], [pOk])
                b.act(rden, pO[:, 256:257], AF.Abs, [pOk], ["rden"])
                b.ts(rden, rden, 1.0, None, ALU.max, None, ["rden"], ["rden"])
                S.dve(lambda en: en.reciprocal(out=rden, in_=rden), ["rden"], ["rden"])
                if e == 0:
                    b.ts(hsum[:, ch, :], pO[:, 0:256], rden, None, ALU.mult, None, [pOk, "rden"], ["hsum"])
                else:
                    b.stt(hsum[:, ch, :], pO[:, 0:256], rden, hsum[:, ch, :], ALU.mult, ALU.add,
                          [pOk, "rden", "hsum"], ["hsum"])
                wcol = DT[:, 127:128] if e == 0 else DT[:, 0:1]
                b.ts(kw, ktm[:, ch, :], wcol, None, ALU.mult, None, ["ktm", "DT"], ["kw"], eng="pool")
                for dc in range(2):
                    pC, pCk = b.ps(4 + dc)
                    b.mm(pC[:, 0:257], kw[:, dc * 128:(dc + 1) * 128], vaug[:, ch, 0:257], True, True, ["kw", "vaug"], [pCk])
                    b.stt(Cm[:, dc, 0:257], Cm[:, dc, 0:257], dec[:, ch, col:col + 1], pC[:, 0:257], ALU.mult, ALU.add,
                          ["Cm", "dec", pCk], ["Cm"])
                b.copy(Cb[:, :, 0:257], Cm[:, :, 0:257], ["Cm"], ["Cb"], eng="act")
        for ch in range(NCH):
            cs = slice(ch * 128, (ch + 1) * 128)
            pT, pTk = b.ps(6 + ch % 2)
            b.act(sqc, hsum[:, ch, :], AF.Square, ["hsum"], ["sqc"])
            S.dve(lambda en, ch=ch: en.reduce_sum(out=ssq, in_=sqc, axis=AX.X), ["sqc"], ["ssq"])
            b.act(ssq, ssq, AF.Sqrt, ["ssq"], ["ssq"], bias=EPS, scale=1.0 / 256.0)
            S.dve(lambda en: en.reciprocal(out=ssq, in_=ssq), ["ssq"], ["ssq"])
            b.ts(hn, hsum[:, ch, :], ssq, None, ALU.mult, None, ["hsum", "ssq"], ["hn"])
            for ec in range(2):
                b.tr(pT[:, ec * 128:(ec + 1) * 128], hn[:, ec * 128:(ec + 1) * 128], ident, ["hn"] + CK, [pTk])
            b.act(sgo, ogT[:, :, cs], AF.Sigmoid, ["ogT"], ["sgo"])
            for ec in range(2):
                gcol = cols[:, C_BNG + 2 * hd + ec:C_BNG + 2 * hd + ec + 1]
                b.stt(yB[:, ec, cs], pT[:, ec * 128:(ec + 1) * 128], gcol, sgo[:, ec, :], ALU.mult, ALU.mult,
                      [pTk, "cols", "sgo"], ["yB"])
        for ec in range(2):
            b.dma(Y[8 + 2 * hd + ec, :, :], yB[:, ec, :], ["yB"], [], q="act")


def mixerC(b, c, layer):
    T, SEG = b.T, b.SEG
    NC, NCS = T // 64, SEG // 64
    cols, CK, flag, S, ident = b.cols, b.CK, b.flag, b.S, b.ident
    Uf, Ub, NGf, NGb, PSf, PSb, dtb, negA = (c[k] for k in ("Uf", "Ub", "NGf", "NGb", "PSf", "PSb", "dtb", "negA"))
    Cqkv, Cz, Gt, Y = c["Cqkv"], c["Cz"], c["Gt"], c["Y"]
    b.phase()
    H = 64
    a64 = lambda shape, dt: b.alloc(shape, dt)[0:H]
    gp = a64([NC, 32], F32)
    g = a64([NC, 16], F32)
    be = a64([NC, 16], F32)
    Gc = a64([NC, 16], F32)
    nG = a64([NC, 16], F32)
    eg = a64([NC, 16], F32)
    ed = a64([NC, 16], F32)
    bk = a64([NC, 16], F32)
    gL = b.alloc([NC, 16], F32)
    b.dma(gp, Gt[:, 16:48].rearrange("(c p) n -> p c n", p=H), [], ["gpTt"])
    b.tt(g[:, :, 0:8], gp[:, :, 0:8], bc_mid(dtb[0:H, 0:8], NC), ALU.add, ["gpTt", "bc"], ["g"])
    b.tt(g[:, :, 8:16], gp[:, :, 16:24], bc_mid(dtb[0:H, 8:16], NC), ALU.add, ["gpTt", "bc"], ["g"])
    b.act(g, g, AF.Exp, ["g"], ["g"])
    b.act(g, g, AF.Ln, ["g"], ["g"], bias=1.0)
    b.tt(g, g, bc_mid(negA[0:H, :], NC), ALU.mult, ["g", "bc"], ["g"])
    b.act(be[:, :, 0:8], gp[:, :, 8:16], AF.Sigmoid, ["gpTt"], ["be"])
    b.act(be[:, :, 8:16], gp[:, :, 24:32], AF.Sigmoid, ["gpTt"], ["be"])
    v3 = lambda p, parts, n, w: p[0:parts, 0:n * w].rearrange("p (a b) -> p a b", b=w)
    for e in range(2):
        U = Uf if e == 0 else Ub
        pa, ka = b.ps(0 + e)
        pb_, kb_ = b.ps(2 + e)
        pc, kc = b.ps(4 + e)
        gs = g[:, :, e * 8:(e + 1) * 8]
        b.mm(v3(pa, H, NC, 8), U[0:H, 0:H], gs, True, True, ["g"] + CK, [ka])
        b.copy(Gc[:, :, e * 8:(e + 1) * 8], v3(pa, H, NC, 8), [ka], ["Gc"], eng="act")
        b.mm(v3(pb_, H, NC, 8), b.ones_f[0:H, 0:H], gs, True, True, ["g"] + CK, [kb_])
        b.tt(ed[:, :, e * 8:(e + 1) * 8], v3(pb_, H, NC, 8), Gc[:, :, e * 8:(e + 1) * 8], ALU.subtract, [kb_, "Gc"], ["ed"])
        b.mm(v3(pc, 128, NC, 8), b.ones_f[0:H, :], gs, True, True, ["g"] + CK, [kc])
        b.act(gL[:, :, e * 8:(e + 1) * 8], v3(pc, 128, NC, 8), AF.Exp, [kc], ["gL"])
    b.act(ed, ed, AF.Exp, ["ed"], ["ed"])
    b.act(eg, Gc, AF.Exp, ["Gc"], ["eg"])
    b.tt(bk, be, eg, ALU.mult, ["be", "eg"], ["bk"])
    S.act(lambda en: en.mul(out=nG, in_=Gc, mul=-1.0), ["Gc"], ["nG"])

    PW = SEG + 3
    xp = b.alloc([2 * PW], BF16)
    cvs = b.alloc([T], F32)
    sqb = b.alloc([T], BF16)
    yC = sqb
    qT = b.alloc([T], BF16)
    kT = b.alloc([T], BF16)
    zs = xp[:, 0:T]
    rn = b.alloc([512], F32)
    ktm = a64([NC, 128], BF16)
    vtm = a64([NC, 128], BF16)
    osum = a64([NC, 128], F32)
    Ttb = gp.bitcast(BF16)[:, 0:NC * 64].rearrange("p (a b) -> p a b", b=64) if False else a64([NC, 64], BF16)
    qkb = a64([NC, 64], BF16)
    Sm = b.alloc([128], F32)
    Sb = b.alloc([128], BF16)
    G = min(8, NC)
    gsrc = cvs if T >= 8 * G * 64 else b.alloc([8 * G * 64], F32)
    gbuf = lambda i: gsrc[0:H, i * G * 64:(i + 1) * G * 64].rearrange("p (a b) -> p a b", b=64)
    DTg, DAg = gbuf(0), gbuf(1)
    Ng = [gbuf(2), gbuf(3)]
    Mg = [gbuf(4), gbuf(5)]
    Pg = [gbuf(6), gbuf(7)]
    sqg = gsrc[0:H, 0:G * 128].rearrange("p (a b) -> p a b", b=128)
    ssg = a64([G], F32)
    vb = a64([128], F32)
    negr = a64([128], BF16)
    vnew = a64([128], BF16)
    kdec = a64([128], BF16)
    o1 = a64([128], F32)
    o2 = a64([128], F32)
    sqc = a64([128], F32)
    on = a64([128], F32)
    ssq = a64([1], F32)
    I64 = ident[0:H, 0:H]
    cbs = b.colblocks(T)

    def conv_silu(ci):
        for s in range(2):
            b.dma(xp[:, s * PW + 2:s * PW + 2 + SEG], Cqkv[ci, :, s * SEG:(s + 1) * SEG], [], ["xp"])
        b.memset(xp[:, 0:2], 0.0, ["xp"])
        b.memset(xp[:, 2 * PW - 1:2 * PW], 0.0, ["xp"])
        b.ts(xp[:, PW - 1:PW], xp[:, PW + 2:PW + 3], flag, None, ALU.mult, None, ["xp"] + CK, ["xp"])
        b.ts(xp[:, PW:PW + 2], xp[:, SEG:SEG + 2], flag, None, ALU.mult, None, ["xp"] + CK, ["xp"])
        for s in range(2):
            o = s * PW
            dst = cvs[:, s * SEG:(s + 1) * SEG]
            wc = lambda k: cols[:, C_CCW + k * 24 + ci:C_CCW + k * 24 + ci + 1]
            b.ts(dst, xp[:, o:o + SEG], wc(0), None, ALU.mult, None, ["xp", "cols"], ["cvs"])
            for k in range(1, 4):
                b.stt(dst, xp[:, o + k:o + k + SEG], wc(k), dst, ALU.mult, ALU.add, ["xp", "cols", "cvs"], ["cvs"])
        b.act(cvs, cvs, AF.Silu, ["cvs"], ["cvs"])

    def l2norm(dstT, dkey, scale):
        b.act(sqb, cvs, AF.Square, ["cvs"], ["sqb"])
        for ci_, (a, e_) in enumerate(cbs):
            pst, pk = b.ps(ci_ % 2)
            b.mm(pst[:, 0:e_ - a], b.ones_b, sqb[:, a:e_], True, True, ["sqb"] + CK, [pk])
            b.act(rn[:, 0:e_ - a], pst[:, 0:e_ - a], AF.Sqrt, [pk], ["rn"], bias=EPS)
            S.dve(lambda en, w=e_ - a: en.reciprocal(out=rn[:, 0:w], in_=rn[:, 0:w]), ["rn"], ["rn"])
            b.stt(cvs[:, a:e_], cvs[:, a:e_], scale, rn[:, 0:e_ - a], ALU.mult, ALU.mult, ["cvs", "rn"], ["cvs"])
        b.copy(dstT, cvs, ["cvs"], [dkey], eng="act")

    def to_tm(dst, dkey):
        for ch in range(NC):
            pst, pk = b.ps(2 + (ch // 4) % 2)
            sub = pst[0:H, (ch % 4) * 128:(ch % 4 + 1) * 128]
            b.tr(sub, cvs[:, ch * 64:(ch + 1) * 64], ident, ["cvs"] + CK, [pk])
            if ch % 4 == 3 or ch == NC - 1:
                n = ch % 4 + 1
                c0 = ch - n + 1
                b.evac(dst[:, c0:c0 + n, :], pst[0:H, 0:n * 128].rearrange("p (a b) -> p a b", b=128), [pk], [dkey])

    for hd in range(8):
        S.barrier()
        conv_silu(hd)
        l2norm(qT, "qT", 128.0 ** -0.5)
        conv_silu(8 + hd)
        l2norm(kT, "kT", 1.0)
        to_tm(ktm, "ktm")
        conv_silu(16 + hd)
        to_tm(vtm, "vtm")
        b.dma(zs, Cz[hd, :, :], [], ["xp"])
        b.act(zs, zs, AF.Silu, ["xp"], ["xp"])
        S.barrier()
        for e in range(2):
            col = e * 8 + hd
            U, NGm, PSm = (Uf, NGf, PSf) if e == 0 else (Ub, NGb, PSb)
            U64, NG64, PS64 = U[0:H, 0:H], NGm[0:H, 0:H], PSm[0:H, 0:H]
            for c0 in range(0, NC, G):
                P = [b.ps(i) for i in range(8)]
                V = lambda i: P[i][0][0:H, 0:G * 64].rearrange("p (a b) -> p a b", b=64)
                slot = lambda i, j: P[i][0][0:H, j * 64:(j + 1) * 64]
                PK = lambda i: P[i][1]
                for j in range(G):
                    gb = bc_last(g[:, c0 + j, col:col + 1], H)
                    b.mm(slot(0, j), gb, U64, True, False, ["g"] + CK, [PK(0)])
                    b.mm(slot(0, j), I64, NG64, False, True, CK, [PK(0)])
                    b.mm(slot(1, j), gb, U64, True, False, ["g"] + CK, [PK(1)])
                    b.mm(slot(1, j), I64, PS64, False, True, CK, [PK(1)])
                b.tt(DTg, V(0), bc_l3(nG[:, c0:c0 + G, col:col + 1], 64), ALU.add, [PK(0), "nG"], ["DTg"])
                b.act(DTg, DTg, AF.Exp, ["DTg"], ["DTg"])
                b.tt(DAg, bc_l3(Gc[:, c0:c0 + G, col:col + 1], 64), V(1), ALU.subtract, [PK(1), "Gc"], ["DAg"])
                b.act(DAg, DAg, AF.Exp, ["DAg"], ["DAg"])
                b.tt(DAg, DAg, bc_l3(be[:, c0:c0 + G, col:col + 1], 64), ALU.mult, ["DAg", "be"], ["DAg"], eng="pool")
                for j in range(G):
                    cs = slice((c0 + j) * 64, (c0 + j + 1) * 64)
                    b.mm(slot(2, j), kT[:, cs], kT[:, cs], True, True, ["kT"], [PK(2)])
                    b.mm(slot(3, j), kT[:, cs], qT[:, cs], True, True, ["kT", "qT"], [PK(3)])
                b.tt(Ng[0], V(2), DAg, ALU.mult, [PK(2), "DAg"], ["N0"])
                b.tt(qkb[:, c0:c0 + G, :], V(3), DTg, ALU.mult, [PK(3), "DTg"], ["qkb"])
                for j in range(G):
                    b.mm(slot(4, j), Ng[0][:, j, :], I64, True, True, ["N0"] + CK, [PK(4)])
                b.copy(Mg[0], V(4), [PK(4)], ["M0"], eng="act")
                b.tt(Pg[0], bc_mid(I64, G), V(4), ALU.subtract, [PK(4)] + CK, ["P0"])
                for lv in range(1, 6):
                    pi_, ci_ = (lv - 1) % 2, lv % 2
                    for j in range(G):
                        b.mm(slot(5, j), Mg[pi_][:, j, :], Ng[pi_][:, j, :], True, True, ["M%d" % pi_, "N%d" % pi_], [PK(5)])
                    b.copy(Ng[ci_], V(5), [PK(5)], ["N%d" % ci_], eng="act")
                    if lv < 5:
                        for j in range(G):
                            b.mm(slot(6, j), Ng[pi_][:, j, :], Mg[pi_][:, j, :], True, True, ["M%d" % pi_, "N%d" % pi_], [PK(6)])
                        b.copy(Mg[ci_], V(6), [PK(6)], ["M%d" % ci_], eng="dve")
                    for j in range(G):
                        b.mm(slot(7, j), Ng[ci_][:, j, :], Pg[pi_][:, j, :], True, True, ["N%d" % ci_, "P%d" % pi_], [PK(7)])
                    if lv < 5:
                        b.tt(Pg[ci_], Pg[pi_], V(7), ALU.add, ["P%d" % pi_, PK(7)], ["P%d" % ci_])
                    else:
                        b.tt(Ttb[:, c0:c0 + G, :], Pg[pi_], V(7), ALU.add, ["P%d" % pi_, PK(7)], ["Ttb"])
            b.memset(Sm, 0.0, ["Sm"])
            b.memset(Sb, 0.0, ["Sb"])
            order = list(range(NC)) if e == 0 else list(range(NC - 1, -1, -1))
            for idx, ch in enumerate(order):
                if idx == NCS:
                    b.ts(Sm, Sm, flag, None, ALU.mult, None, ["Sm"] + CK, ["Sm"])
                    b.copy(Sb, Sm, ["Sm"], ["Sb"], eng="act")
                cs = slice(ch * 64, (ch + 1) * 64)
                pA, pAk = b.ps(4)
                pB, pBk = b.ps(5)
                pC, pCk = b.ps(6)
                pD, pDk = b.ps(7)
                b.mm(pA[0:H, 0:128], kT[:, cs], Sb, True, True, ["kT", "Sb"], [pAk])
                b.mm(pA[0:H, 128:256], qT[:, cs], Sb, True, True, ["qT", "Sb"], [pAk])
                b.ts(vb, vtm[:, ch, :], be[:, ch, col:col + 1], None, ALU.mult, None, ["vtm", "be"], ["vb"], eng="pool")
                b.stt(negr, pA[0:H, 0:128], bk[:, ch, col:col + 1], vb, ALU.mult, ALU.subtract, [pAk, "bk", "vb"], ["negr"])
                b.mm(pB[0:H, 0:128], Ttb[:, ch, :], negr, True, True, ["Ttb", "negr"], [pBk])
                S.act(lambda en, pB=pB: en.mul(out=vnew, in_=pB[0:H, 0:128], mul=-1.0), [pBk], ["vnew"])
                b.mm(pC[0:H, 0:128], qkb[:, ch, :], vnew, True, True, ["qkb", "vnew"], [pCk])
                b.ts(kdec, ktm[:, ch, :], ed[:, ch, col:col + 1], None, ALU.mult, None, ["ktm", "ed"], ["kdec"], eng="pool")
                b.mm(pD[:, 0:128], kdec, vnew, True, True, ["kdec", "vnew"], [pDk])
                b.copy(o1, pC[0:H, 0:128], [pCk], ["o1"], eng="act")
                if e == 0:
                    b.stt(osum[:, ch, :], pA[0:H, 128:256], eg[:, ch, col:col + 1], o1, ALU.mult, ALU.add,
                          [pAk, "eg", "o1"], ["osum"])
                else:
                    b.stt(o2, pA[0:H, 128:256], eg[:, ch, col:col + 1], o1, ALU.mult, ALU.add, [pAk, "eg", "o1"], ["o2"])
                    b.tt(osum[:, ch, :], osum[:, ch, :], o2, ALU.add, ["osum", "o2"], ["osum"])
                b.stt(Sm, Sm, gL[:, ch, col:col + 1], pD[:, 0:128], ALU.mult, ALU.add, ["Sm", "gL", pDk], ["Sm"])
                b.copy(Sb, Sm, ["Sm"], ["Sb"], eng="act")
        S.barrier()
        for gi, c0 in enumerate(range(0, NC, G)):
            pT, pTk = b.ps(gi % 2)
            og = osum[:, c0:c0 + G, :]
            b.act(sqg, og, AF.Square, ["osum"], ["sqg"])
            S.dve(lambda en: en.reduce_sum(out=ssg, in_=sqg, axis=AX.X), ["sqg"], ["ssg"])
            b.act(ssg, ssg, AF.Sqrt, ["ssg"], ["ssg"], bias=EPS, scale=1.0 / 128.0)
            S.dve(lambda en: en.reciprocal(out=ssg, in_=ssg), ["ssg"], ["ssg"])
            b.tt(sqg, og, bc_l3(ssg.rearrange("p (a b) -> p a b", b=1), 128), ALU.mult, ["osum", "ssg"], ["sqg"])
            for j in range(G):
                b.tr(pT[:, j * 64:(j + 1) * 64], sqg[:, j, :], I64, ["sqg"] + CK, [pTk])
            cs = slice(c0 * 64, (c0 + G) * 64)
            b.stt(yC[:, cs], pT[:, 0:G * 64], cols[:, C_CNG + hd:C_CNG + hd + 1], zs[:, cs], ALU.mult, ALU.mult,
                  [pTk, "cols", "xp"], ["sqb"])
        b.dma(Y[16 + hd, :, :], yC, ["sqb"], [], q="act")


def phase3a(b, c, layer):
    T = b.T
    TB = min(T, 512)
    W, Y, MG, hA, hM = c["W"], c["Y"], c["MG"], c["hA"], c["hM"]
    for tb in range(T // TB):
        b.phase()
        t0 = tb * TB
        b.wsetup()
        b.wcache = {"ap": c["Wc3a"], "idx": 0, "mode": "fill" if tb == 0 else "use"}
        Yt = b.alloc([24, TB], BF16)
        h = b.alloc([16, TB], F32)
        mrg = b.alloc([16, TB], BF16)
        mgt = [b.alloc([3, TB], BF16) for _ in range(2)]
        sg = [b.alloc([TB], F32) for _ in range(3)]
        acc = b.alloc([TB], F32)
        tmp = b.alloc([TB], F32)
        b.dma(Yt, Y[:, :, t0:t0 + TB].rearrange("c p t -> p c t"), [], ["Yt"])
        b.dma(h, hA[:, :, t0:t0 + TB].rearrange("c p t -> p c t"), [], ["h%d" % i for i in range(16)])
        for m in range(16):
            mg_ = mgt[m % 2]
            pks = []
            for g in range(3):
                mk = "mgt%d_%d" % (m % 2, g)
                b.dma(mg_[:, g, :], MG[g * 16 + m, :, t0:t0 + TB], [], [mk])
                wt, wk = b.load_w(W["w_branch"][layer, g, :, m * 128:(m + 1) * 128], 8)
                pst, pk = b.ps(g + 4 * (m % 2))
                pks.append((pst, pk))
                for k in range(8):
                    b.mm(pst[:, 0:TB], wt[:, k, :], Yt[:, g * 8 + k, :], k == 0, k == 7, [wk, "Yt"], [pk])
                b.act(sg[g], mg_[:, g, :], AF.Sigmoid, [mk], ["sg%d" % g])
            b.tt(acc, pks[0][0][:, 0:TB], sg[0], ALU.mult, [pks[0][1], "sg0"], ["acc"])
            b.tt(tmp, pks[1][0][:, 0:TB], sg[1], ALU.mult, [pks[1][1], "sg1"], ["tmp"])
            b.tt(acc, acc, tmp, ALU.add, ["acc", "tmp"], ["acc"])
            b.tt(tmp, pks[2][0][:, 0:TB], sg[2], ALU.mult, [pks[2][1], "sg2"], ["tmp"])
            b.tt(mrg[:, m, :], acc, tmp, ALU.add, ["acc", "tmp"], ["mrg"])
        for m in range(16):
            wt, wk = b.load_w(W["w_out"][layer, :, m * 128:(m + 1) * 128], 16)
            pst, pk = b.ps(3 + 4 * (m % 2))
            for k in range(16):
                b.mm(pst[:, 0:TB], wt[:, k, :], mrg[:, k, :], k == 0, k == 15, [wk, "mrg"], [pk])
            b.tt(h[:, m, :], h[:, m, :], pst[:, 0:TB], ALU.add, [pk, "h%d" % m], ["h%d" % m])
        b.dma(hM[:, :, t0:t0 + TB].rearrange("c p t -> p c t"), h, ["h%d" % i for i in range(16)], [], q="act")
    b.wcache = None


def phase3b(b, c, layer, last):
    T, SEG = b.T, b.SEG
    TB = min(SEG, 512)
    NE = TB + 2
    cols, CK, flag, S, ident = b.cols, b.CK, b.flag, b.S, b.ident
    W, hA, hM, p_d, y_d = c["W"], c["hA"], c["hM"], c["p_d"], c["y_d"]
    HK = ["h%d" % i for i in range(16)]
    for tb in range(T // TB):
        b.phase()
        t0 = tb * TB
        b.wsetup()
        b.wcache = {"ap": c["Wc3b"], "idx": 0, "mode": "fill" if tb == 0 else "use"}
        h = b.alloc([16, NE], F32)
        xn = b.alloc([16, NE], BF16)
        sq = [b.alloc([NE], BF16) for _ in range(2)]
        rstd = b.alloc([NE], F32)
        act_ = b.alloc([48, TB], BF16)
        cg = b.alloc([TB], F32)
        cu = b.alloc([TB], F32)
        pin = b.alloc([256], F32)
        pT = b.alloc([2, TB], BF16)
        b.dma(h[:, :, 1:TB + 1], hM[:, :, t0:t0 + TB].rearrange("c p t -> p c t"), [], HK)
        if t0 > 0:
            b.dma(h[:, :, 0:1], hM[:, :, t0 - 1:t0].rearrange("c p t -> p c t"), [], ["hl"], slow=True)
        else:
            b.memset(h[:, :, 0:1], 0.0, ["hl"])
        if t0 + TB < T:
            b.dma(h[:, :, TB + 1:TB + 2], hM[:, :, t0 + TB:t0 + TB + 1].rearrange("c p t -> p c t"), [], ["hr"], slow=True)
        else:
            b.memset(h[:, :, TB + 1:TB + 2], 0.0, ["hr"])
        HALL = HK + ["hl", "hr"]

        def rms(lo, hi, gcol0, out, okey, hkeys):
            n = hi - lo
            cb = [(a + lo, e_ + lo) for (a, e_) in b.colblocks(n)]
            for ch in range(16):
                q, qk = sq[ch % 2], "sq%d" % (ch % 2)
                b.act(q[:, lo:hi], h[:, ch, lo:hi], AF.Square, hkeys, [qk])
                for i, (a, e_) in enumerate(cb):
                    pst, pk = b.ps(i)
                    b.mm(pst[:, 0:e_ - a], b.ones_b, q[:, a:e_], ch == 0, ch == 15, [qk] + CK, [pk])
            for i, (a, e_) in enumerate(cb):
                pst, pk = b.ps(i)
                b.act(rstd[:, a:e_], pst[:, 0:e_ - a], AF.Sqrt, [pk], ["rstd"], bias=EPS, scale=1.0 / D)
            S.dve(lambda en: en.reciprocal(out=rstd[:, lo:hi], in_=rstd[:, lo:hi]), ["rstd"], ["rstd"])
            for ch in range(16):
                b.stt(out[:, ch, :], h[:, ch, lo:hi], cols[:, gcol0 + ch:gcol0 + ch + 1], rstd[:, lo:hi],
                      ALU.mult, ALU.mult, hkeys + ["cols", "rstd"], [okey])

        rms(0, NE, C_FFNG, xn, "xn", HALL)
        if t0 == SEG:
            b.ts(xn[:, :, 0:1], xn[:, :, 0:1], flag, None, ALU.mult, None, ["xn"] + CK, ["xn"])
        if t0 + TB == SEG:
            b.ts(xn[:, :, TB + 1:TB + 2], xn[:, :, TB + 1:TB + 2], flag, None, ALU.mult, None, ["xn"] + CK, ["xn"])
        cbs = b.colblocks(NE)
        for j in range(48):
            wg, wgk = b.load_w(W["ffn_w_up"][layer, :, j * 128:(j + 1) * 128], 16)
            pg = b.psum[2 * (j % 2)]
            pgk = ["ps%d" % (4 * (j % 2)), "ps%d" % (4 * (j % 2) + 1)]
            for (a, e_) in cbs:
                for k in range(16):
                    b.mm(pg[:, a:e_], wg[:, k, :], xn[:, k, a:e_], k == 0, k == 15, [wgk, "xn"], pgk)
            wu, wuk = b.load_w(W["ffn_w_up"][layer, :, DFF + j * 128:DFF + (j + 1) * 128], 16)
            pu = b.psum[2 * (j % 2) + 1]
            puk = ["ps%d" % (4 * (j % 2) + 2), "ps%d" % (4 * (j % 2) + 3)]
            for (a, e_) in cbs:
                for k in range(16):
                    b.mm(pu[:, a:e_], wu[:, k, :], xn[:, k, a:e_], k == 0, k == 15, [wuk, "xn"], puk)
            for (pp, ppk, dst, dk, ci) in ((pg, pgk, cg, "cg", j), (pu, puk, cu, "cu", 48 + j)):
                wc = lambda k: cols[:, C_FCW + k * 96 + ci:C_FCW + k * 96 + ci + 1]
                b.ts(dst, pp[:, 1:TB + 1], wc(1), None, ALU.mult, None, ppk + ["cols"], [dk])
                b.stt(dst, pp[:, 0:TB], wc(0), dst, ALU.mult, ALU.add, ppk + ["cols", dk], [dk])
                b.stt(dst, pp[:, 2:TB + 2], wc(2), dst, ALU.mult, ALU.add, ppk + ["cols", dk], [dk])
            b.act(cg, cg, AF.Gelu_apprx_tanh, ["cg"], ["cg"])
            b.tt(act_[:, j, :], cg, cu, ALU.mult, ["cg", "cu"], ["act"])
        for m in range(16):
            pst, pk = b.ps(m % 2)
            for q_ in range(3):
                wt, wk = b.load_w(W["ffn_w_down"][layer, q_ * 2048:(q_ + 1) * 2048, m * 128:(m + 1) * 128], 16)
                for k in range(16):
                    b.mm(pst[:, 0:TB], wt[:, k, :], act_[:, q_ * 16 + k, :], q_ == 0 and k == 0, q_ == 2 and k == 15,
                         [wk, "act"], [pk])
            b.tt(h[:, m, 1:TB + 1], h[:, m, 1:TB + 1], pst[:, 0:TB], ALU.add, [pk, "h%d" % m], ["h%d" % m])
        xn3 = xn[:, :, 1:TB + 1]
        rms(1, TB + 1, C_PLEG, xn3, "xn", HK)
        for tt in range(TB // 128):
            b.dma(pin, p_d[layer, t0 + tt * 128:t0 + (tt + 1) * 128, :], [], ["pin"])
            pst, pk = b.ps(2 + tt % 2)
            for k in range(2):
                b.tr(pst[:, k * 128:(k + 1) * 128], pin[:, k * 128:(k + 1) * 128], ident, ["pin"] + CK, [pk])
            b.evac(pT[:, :, tt * 128:(tt + 1) * 128], pst[:, 0:256].rearrange("p (a b) -> p a b", a=2), [pk], ["pT"])
        for m in range(16):
            wg, wgk = b.load_w(W["ple_w_gate"][layer, :, m * 128:(m + 1) * 128], 16)
            pa, pak = b.ps(4 + 2 * (m % 2))
            for k in range(16):
                b.mm(pa[:, 0:TB], wg[:, k, :], xn3[:, k, :], k == 0, k == 15, [wgk, "xn"], [pak])
            wp, wpk = b.load_w(W["ple_w_proj"][layer, :, m * 128:(m + 1) * 128], 2)
            pb_, pbk = b.ps(5 + 2 * (m % 2))
            for k in range(2):
                b.mm(pb_[:, 0:TB], wp[:, k, :], pT[:, k, :], k == 0, k == 1, [wpk, "pT"], [pbk])
            b.act(cg, pa[:, 0:TB], AF.Sigmoid, [pak], ["cg"])
            b.tt(cu, pb_[:, 0:TB], cg, ALU.mult, [pbk, "cg"], ["cu"])
            b.tt(h[:, m, 1:TB + 1], h[:, m, 1:TB + 1], cu, ALU.add, ["cu", "h%d" % m], ["h%d" % m])
        if not last:
            b.dma(hA[:, :, t0:t0 + TB].rearrange("c p t -> p c t"), h[:, :, 1:TB + 1], HK, [], q="act")
        else:
            if b.debug:
                b.dma(hA[:, :, t0:t0 + TB].rearrange("c p t -> p c t"), h[:, :, 1:TB + 1], HK, [], q="act")
            xf = b.alloc([16, TB], F32)
            yo = [b.alloc([D], F32) for _ in range(2)]
            rms(1, TB + 1, C_FING, xf, "xf", HK)
            for tt in range(TB // 128):
                o, ok = yo[tt % 2], "yo%d" % (tt % 2)
                for g in range(4):
                    pst, pk = b.ps(4 + (tt * 4 + g) % 4)
                    for j in range(4):
                        ch = g * 4 + j
                        b.tr(pst[:, j * 128:(j + 1) * 128], xf[:, ch, tt * 128:(tt + 1) * 128], ident, ["xf"] + CK, [pk])
                    b.evac(o[:, g * 512:(g + 1) * 512], pst, [pk], [ok])
                b.dma(y_d[t0 + tt * 128:t0 + (tt + 1) * 128, :], o, [ok], [], q="act")
    b.wcache = None


_NC_CACHE = {}


def kernel(**inputs):
    SEG = 2048
    T = 2 * SEG
    if "nc" not in _NC_CACHE:
        _NC_CACHE["nc"] = build_nc(SEG=SEG)[0]
    nc = _NC_CACHE["nc"]
    xp_ = np.asarray(inputs["x_prompt"], dtype=np.float32)
    xs_ = np.asarray(inputs["x_sample"], dtype=np.float32)
    pp_ = np.asarray(inputs["p_prompt"], dtype=np.float32)
    ps_ = np.asarray(inputs["p_sample"], dtype=np.float32)
    wts = {n: np.ascontiguousarray(np.asarray(inputs[n], dtype=np.float32)) for n, _ in WNAMES}
    in_maps = []
    for core in range(8):
        if core < 4:
            x = xp_[core]
            p = pp_[:, core]
            f = 1.0
        else:
            j = 2 * (core - 4)
            x = xs_[j:j + 2].reshape(T, D)
            p = ps_[:, j:j + 2].reshape(2, T, 256)
            f = 0.0
        m = {"x": np.ascontiguousarray(x), "p": np.ascontiguousarray(p),
             "flag": np.full((128, 1), f, np.float32)}
        m.update(wts)
        in_maps.append(m)
    res = run_bass_kernel_spmd(nc, in_maps, core_ids=list(range(8)))
    outs = [np.asarray(r["y"], dtype=np.float32) for r in res.results]
    y_prompt = np.stack(outs[0:4], axis=0)
    y_sample = np.concatenate([o.reshape(2, SEG, D) for o in outs[4:8]], axis=0)
    return (y_prompt, y_sample)
```

```python
import math
from contextlib import ExitStack

import numpy as np
import concourse.bass as bass
import concourse.mybir as mybir
from concourse.bass_utils import run_bass_kernel_spmd

F32 = mybir.dt.float32
BF16 = mybir.dt.bfloat16
AF = mybir.ActivationFunctionType
ALU = mybir.AluOpType
AX = mybir.AxisListType

D = 2048
KD = 16
DPROJ = 16432
DFF = 6144
EPS = 1e-6
NEG = -30000.0

COMPUTE = ("pe", "act", "dve", "pool")
QUEUES = ("pe", "act", "dve", "pool", "sp")
NDSEM = 8


class Op:
    __slots__ = ("eng", "fn", "dma", "deps", "sig", "sem", "sigval", "waited")

    def __init__(self, eng, fn, dma):
        self.eng = eng
        self.fn = fn
        self.dma = dma
        self.deps = []
        self.sig = False
        self.sem = None
        self.sigval = 0
        self.waited = False


class Sched:
    def __init__(self):
        self.q = {e: [] for e in QUEUES}
        self.last_w = {}
        self.readers = {}
        self.dma_hist = {e: [] for e in QUEUES}
        self.all_dmas = []
        self.bar = []
        self.bar_pending = {e: False for e in QUEUES}
        self.dmas_since_bar = []

    def barrier(self):
        deps = [self.q[e][-1] for e in COMPUTE if self.q[e] and self.q[e][-1].fn is not None]
        deps += self.dmas_since_bar
        self.bar = deps
        self.dmas_since_bar = []
        for e in QUEUES:
            self.bar_pending[e] = True

    def add(self, eng, fn, reads=(), writes=(), dma=False):
        op = Op(eng, fn, dma)
        deps = {}

        def consider(d, kind):
            if d is None:
                return
            if (not d.dma) and d.eng == eng:
                if (not dma) and eng == "pe":
                    return
            deps[id(d)] = d

        for k in reads:
            consider(self.last_w.get(k), "raw")
            if k.startswith("ps"):
                for r in self.readers.get(k, ()):
                    if r.eng != eng:
                        deps[id(r)] = r
        for k in writes:
            consider(self.last_w.get(k), "waw")
            for r in self.readers.get(k, ()):
                consider(r, "war")
        if self.bar_pending[eng]:
            self.bar_pending[eng] = False
            for d in self.bar:
                if d.dma or d.eng != eng:
                    deps[id(d)] = d
        if dma:
            hist = self.dma_hist[eng]
            if len(hist) >= NDSEM:
                d = hist[-NDSEM]
                deps[id(d)] = d
            hist.append(op)
            self.all_dmas.append(op)
            self.dmas_since_bar.append(op)
        op.deps = list(deps.values())
        for k in reads:
            self.readers.setdefault(k, []).append(op)
        for k in writes:
            self.last_w[k] = op
            self.readers[k] = []
        self.q[eng].append(op)
        return op

    def pe(self, fn, reads=(), writes=()):
        return self.add("pe", fn, reads, writes)

    def act(self, fn, reads=(), writes=()):
        return self.add("act", fn, reads, writes)

    def dve(self, fn, reads=(), writes=()):
        return self.add("dve", fn, reads, writes)

    def pool(self, fn, reads=(), writes=()):
        return self.add("pool", fn, reads, writes)

    def dma(self, q, fn, reads=(), writes=()):
        return self.add(q, fn, reads, writes, dma=True)

    def emit(self, nc):
        fin = Op("sp", None, False)
        fin.deps = list(self.all_dmas)
        self.q["sp"].append(fin)
        for e in QUEUES:
            for op in self.q[e]:
                for d in op.deps:
                    d.waited = True
        with ExitStack() as es:
            csem = {e: es.enter_context(nc.semaphore("s_" + e)) for e in COMPUTE}
            dsem = {e: [es.enter_context(nc.semaphore("d_%s_%d" % (e, i))) for i in range(NDSEM)]
                    for e in QUEUES}
            for e in QUEUES:
                cnt = 0
                dcnt = [0] * NDSEM
                di = 0
                for op in self.q[e]:
                    if op.fn is None:
                        continue
                    if op.dma:
                        s = di % NDSEM
                        di += 1
                        dcnt[s] += 16
                        op.sem = dsem[e][s]
                        op.sigval = dcnt[s]
                        op.sig = True
                    elif op.waited:
                        cnt += 1
                        op.sem = csem[e]
                        op.sigval = cnt
                        op.sig = True
            block = es.enter_context(nc.Block())
            engs = {"pe": block.tensor, "act": block.scalar, "dve": block.vector,
                    "pool": block.gpsimd, "sp": block.sync}
            self.stats = {}
            for e in QUEUES:
                ops = self.q[e]
                nw = [0]

                def body(eng, ops=ops, nw=nw):
                    known = {}
                    for op in ops:
                        need = {}
                        for d in op.deps:
                            key = d.sem.name
                            if known.get(key, 0) >= d.sigval:
                                continue
                            if key not in need or need[key][1] < d.sigval:
                                need[key] = (d.sem, d.sigval)
                        for key, (sem, val) in need.items():
                            eng.wait_ge(sem, val)
                            known[key] = val
                            nw[0] += 1
                        if op.fn is None:
                            continue
                        ins = op.fn(eng)
                        if op.sig:
                            ins.then_inc(op.sem, 16 if op.dma else 1)

                engs[e](body)
                self.stats[e] = (len(ops), nw[0])


ARENA_WORDS = 52224


class B:
    def __init__(self, SEG, debug=False):
        self.SEG = SEG
        self.T = 2 * SEG
        self.debug = debug
        self.nc = bass.Bass("TRN2", target_bir_lowering=False)
        self.S = Sched()
        self.uid = 0

    def dram_in(self, name, shape):
        return self.nc.dram_tensor(name, list(shape), F32, kind="ExternalInput").ap()

    def scratch(self, name, shape, dt):
        kind = "ExternalOutput" if self.debug else "Internal"
        return self.nc.dram_tensor(name, list(shape), dt, kind=kind).ap()

    def alloc(self, free_shape, dt, parts=128):
        n = 1
        for s in free_shape:
            n *= s
        words = n if dt == F32 else (n + 1) // 2
        words = (words + 7) // 8 * 8
        assert self.off + words <= ARENA_WORDS, ("arena overflow", self.off, words)
        ap = self.arena[0:parts, self.off:self.off + words]
        self.off += words
        if dt == BF16:
            ap = ap.bitcast(BF16)
        ap = ap[:, 0:n]
        if len(free_shape) == 2:
            ap = ap.rearrange("p (a b) -> p a b", a=free_shape[0])
        elif len(free_shape) == 3:
            ap = ap.rearrange("p (a b c) -> p a b c", a=free_shape[0], b=free_shape[1])
        return ap

    def key(self, base):
        self.uid += 1
        return "%s#%d" % (base, self.uid)

    def phase(self):
        self.S.barrier()
        self.off = self.persist_off

    def mm(self, out, lhsT, rhs, start, stop, R, W):
        self.S.pe(lambda e: e.matmul(out, lhsT=lhsT, rhs=rhs, start=start, stop=stop), R, W)

    def tr(self, out, in_, ident, R, W):
        self.S.pe(lambda e: e.transpose(out, in_, ident), R, W)

    def act(self, out, in_, func, R, W, bias=0.0, scale=1.0):
        self.S.act(lambda e: e.activation(out=out, in_=in_, func=func, bias=bias, scale=scale), R, W)

    def tt(self, out, a, b, op, R, W, eng="dve"):
        self.S.add(eng, lambda e: e.tensor_tensor(out=out, in0=a, in1=b, op=op), R, W)

    def ts(self, out, a, s1, s2, op0, op1, R, W, eng="dve"):
        if s2 is None:
            self.S.add(eng, lambda e: e.tensor_scalar(out=out, in0=a, scalar1=s1, scalar2=None, op0=op0), R, W)
        else:
            self.S.add(eng, lambda e: e.tensor_scalar(out=out, in0=a, scalar1=s1, scalar2=s2, op0=op0, op1=op1), R, W)

    def stt(self, out, a, s, b, op0, op1, R, W):
        self.S.dve(lambda e: e.scalar_tensor_tensor(out=out, in0=a, scalar=s, in1=b, op0=op0, op1=op1), R, W)

    def copy(self, out, in_, R, W, eng="dve"):
        if eng == "act":
            self.S.act(lambda e: e.copy(out=out, in_=in_), R, W)
        else:
            self.S.add(eng, lambda e: e.tensor_copy(out=out, in_=in_), R, W)

    def dma(self, out, in_, R, W, q="sp", slow=False):
        if slow:
            self.S.dma(q, lambda e: e.dma_start(out=out, in_=in_, allow_slow_non_contiguous=True), R, W)
        else:
            self.S.dma(q, lambda e: e.dma_start(out=out, in_=in_), R, W)

    def memset(self, ap, val, W, eng="pool"):
        self.S.add(eng, lambda e: e.memset(ap, val), (), W)

    def evac(self, out, in_, R, W):
        self.ev = getattr(self, "ev", 0) + 1
        self.copy(out, in_, R, W, eng=("act" if self.ev % 2 else "dve"))

    def wsetup(self):
        self.wst = [self.alloc([16, 128], F32) for _ in range(2)]
        self.wbf = [self.alloc([16, 128], BF16) for _ in range(3)]
        self.wi = 0

    def load_w(self, src, nk, ncol=128):
        i = self.wi
        self.wi += 1
        st = self.wst[i % 2]
        bf = self.wbf[i % 3]
        ks, kb = "wst%d" % (i % 2), "wbf%d" % (i % 3)
        wc = getattr(self, "wcache", None)
        if wc is not None:
            idx = wc["idx"]
            wc["idx"] += 1
            slot = wc["ap"][idx, :, 0:nk * ncol].rearrange("p (k n) -> p k n", n=ncol)
            if wc["mode"] == "use":
                self.dma(bf[:, 0:nk, 0:ncol], slot, [], [kb])
                return bf, kb
        self.dma(st[:, 0:nk, 0:ncol], src.rearrange("(k p) n -> p k n", p=128), [], [ks])
        self.copy(bf[:, 0:nk, 0:ncol], st[:, 0:nk, 0:ncol], [ks], [kb], eng="pool")
        if wc is not None:
            self.dma(slot, bf[:, 0:nk, 0:ncol], [kb], [], q="pool")
        return bf, kb

    def colblocks(self, n):
        return [(a, min(a + 512, n)) for a in range(0, n, 512)]

    def ps(self, i):
        return self.psum[i // 2][:, (i % 2) * 512:(i % 2) * 512 + 512], "ps%d" % i


WNAMES = [
    ("norm_mix_g", (2, 2048)), ("w_in", (2, 2048, 16432)), ("rglru_conv_w", (2, 4, 1024)),
    ("rglru_conv_b", (2, 1024)), ("rglru_gate_w", (2, 2, 2, 8, 128, 128)), ("rglru_gate_b", (2, 2, 2, 1024)),
    ("rglru_lambda", (2, 2, 1024)), ("mlstm_gate_b", (2, 16)), ("mlstm_norm_g", (2, 1024)),
    ("gdn_conv_w", (2, 4, 3072)), ("gdn_A_log", (2, 2, 8)), ("gdn_dt_bias", (2, 2, 8)),
    ("gdn_norm_g", (2, 1024)), ("w_branch", (2, 3, 1024, 2048)), ("w_out", (2, 2048, 2048)),
    ("norm_ffn_g", (2, 2048)), ("ffn_w_up", (2, 2048, 12288)), ("ffn_conv_w", (2, 3, 12288)),
    ("ffn_w_down", (2, 6144, 2048)), ("norm_ple_g", (2, 2048)), ("ple_w_gate", (2, 2048, 2048)),
    ("ple_w_proj", (2, 256, 2048)), ("norm_final_g", (2048,)),
]

C_MIXG, C_FFNG, C_PLEG, C_FING = 0, 16, 32, 48
C_ACW, C_ACB, C_AGB, C_ALAM = 64, 96, 104, 136
C_BNG, C_CCW, C_CNG, C_FCW = 152, 160, 256, 264
C_AC8 = 552
NCOLS = 568


def build_nc(SEG=2048, debug=False, nlayers=2, stop_after=None):
    b = B(SEG, debug)
    nc, S, T = b.nc, b.S, b.T
    NT = T // 128
    x_d = b.dram_in("x", (T, D))
    p_d = b.dram_in("p", (2, T, 256))
    flag_d = b.dram_in("flag", (128, 1))
    W = {n: b.dram_in(n, s) for n, s in WNAMES}
    y_d = nc.dram_tensor("y", [T, D], F32, kind="ExternalOutput").ap()
    hA = b.scratch("hA", (16, 128, T), F32)
    hM = b.scratch("hM", (16, 128, T), F32)
    Ain = b.scratch("Ain", (16, 128, T), BF16)
    Bqk = b.scratch("Bqk", (16, 128, T), BF16)
    Bo = b.scratch("Bo", (8, 128, T), BF16)
    Bv = b.scratch("Bv", (T, 1024), BF16)
    Bk = b.scratch("Bk", (T, 1024), BF16)
    Gt = b.scratch("Gt", (T, 48), F32)
    Cqkv = b.scratch("Cqkv", (24, 128, T), BF16)
    Cz = b.scratch("Cz", (8, 128, T), BF16)
    MG = b.scratch("MG", (48, 128, T), BF16)
    Y = b.scratch("Y", (24, 128, T), BF16)
    Wc3a = nc.dram_tensor("Wc3a", [64, 128, 2048], BF16, kind="Internal").ap()
    Wc3b = nc.dram_tensor("Wc3b", [208, 128, 2048], BF16, kind="Internal").ap()

    with ExitStack() as es:
        b.arena = es.enter_context(nc.sbuf_tensor("arena", [128, ARENA_WORDS], F32))
        b.psum = [es.enter_context(nc.psum_tensor("psum%d" % i, [128, 1024], F32)) for i in range(4)]
        b.off = 0
        ident = b.alloc([128], F32)
        ones_f = b.alloc([128], F32)
        ones_b = b.alloc([128], BF16)
        Uf = b.alloc([128], F32)
        Ub = b.alloc([128], F32)
        NGf = b.alloc([128], F32)
        NGb = b.alloc([128], F32)
        PSf = b.alloc([128], F32)
        PSb = b.alloc([128], F32)
        flag = b.alloc([1], F32)
        cols = b.alloc([NCOLS], F32)
        gbB = b.alloc([16], F32)
        dtb = b.alloc([16], F32)
        negA = b.alloc([16], F32)
        b.persist_off = b.off

        CK = []

        def aff(ap, src_val, cmp, fill, pattern, cm):
            k = "const%d" % len(CK)
            CK.append(k)
            b.memset(ap, src_val, [k])
            S.pool(lambda e: e.affine_select(out=ap, in_=ap, compare_op=cmp, fill=fill, base=0,
                                              pattern=pattern, channel_multiplier=cm), [k], [k])

        aff(ident, 0.0, ALU.not_equal, 1.0, [[-1, 128]], 1)
        b.memset(ones_f, 1.0, ["const_of"])
        b.memset(ones_b, 1.0, ["const_ob"])
        CK += ["const_of", "const_ob", "const_flag"]
        aff(Uf, 1.0, ALU.is_ge, 0.0, [[1, 128]], -1)
        aff(Ub, 1.0, ALU.is_ge, 0.0, [[-1, 128]], 1)
        aff(NGf, 0.0, ALU.is_ge, NEG, [[1, 128]], -1)
        aff(NGb, 0.0, ALU.is_ge, NEG, [[-1, 128]], 1)
        aff(PSf, 0.0, ALU.is_gt, -NEG, [[-1, 128]], 1)
        aff(PSb, 0.0, ALU.is_gt, -NEG, [[1, 128]], -1)
        b.dma(flag, flag_d, [], ["const_flag"])

        def load_cols(layer):
            b.phase()
            stage = b.alloc([128], F32)
            items = [
                (W["norm_mix_g"][layer].rearrange("(c p) -> c p", p=128), 16, C_MIXG),
                (W["norm_ffn_g"][layer].rearrange("(c p) -> c p", p=128), 16, C_FFNG),
                (W["norm_ple_g"][layer].rearrange("(c p) -> c p", p=128), 16, C_PLEG),
                (W["norm_final_g"].rearrange("(c p) -> c p", p=128), 16, C_FING),
                (W["rglru_conv_w"][layer].rearrange("k (n p) -> (k n) p", p=128), 32, C_ACW),
                (W["rglru_conv_b"][layer].rearrange("(n p) -> n p", p=128), 8, C_ACB),
                (W["rglru_gate_b"][layer].rearrange("e g (n p) -> (e g n) p", p=128), 32, C_AGB),
                (W["rglru_lambda"][layer].rearrange("e (n p) -> (e n) p", p=128), 16, C_ALAM),
                (W["mlstm_norm_g"][layer].rearrange("(n p) -> n p", p=128), 8, C_BNG),
                (W["gdn_conv_w"][layer].rearrange("k (c p) -> (k c) p", p=128), 96, C_CCW),
                (W["gdn_norm_g"][layer].rearrange("(n p) -> n p", p=128), 8, C_CNG),
            ]
            fcw = W["ffn_conv_w"][layer].rearrange("k (c p) -> (k c) p", p=128)
            for q in range(3):
                items.append((fcw[q * 96:(q + 1) * 96, :], 96, C_FCW + q * 96))
            for i, (src, R, c0) in enumerate(items):
                pst, pk = b.ps(i % 2)
                b.dma(stage[0:R, :], src, [], ["lc_stage"])
                b.tr(pst[:, 0:R], stage[0:R, :], ident[0:R, 0:R], ["lc_stage"] + CK, [pk])
                b.copy(cols[:, c0:c0 + R], pst[:, 0:R], [pk], ["cols"], eng="act")
            tmp = b.alloc([16], F32)
            b.act(tmp, cols[:, C_ALAM:C_ALAM + 16], AF.Exp, ["cols"], ["lc_tmp"], scale=-1.0)
            b.act(tmp, tmp, AF.Ln, ["lc_tmp"], ["lc_tmp"], bias=1.0)
            b.S.act(lambda e: e.mul(out=cols[:, C_AC8:C_AC8 + 16], in_=tmp, mul=-8.0), ["lc_tmp"], ["cols"])
            b.dma(gbB, W["mlstm_gate_b"][layer:layer + 1, :].partition_broadcast(128), [], ["bc"])
            b.dma(dtb, W["gdn_dt_bias"][layer:layer + 1].rearrange("a e h -> a (e h)").partition_broadcast(128), [], ["bc"])
            b.dma(negA, W["gdn_A_log"][layer:layer + 1].rearrange("a e h -> a (e h)").partition_broadcast(128), [], ["bc"])
            b.act(negA, negA, AF.Exp, ["bc"], ["bc"])
            b.S.act(lambda e: e.mul(out=negA, in_=negA, mul=-1.0), ["bc"], ["bc"])

        def phase0():
            b.phase()
            xin = [b.alloc([D], F32) for _ in range(2)]
            xo = [b.alloc([16, 128], F32) for _ in range(2)]
            for tt in range(NT):
                xi, xk = xin[tt % 2], "xin%d" % (tt % 2)
                o, ok = xo[tt % 2], "xo%d" % (tt % 2)
                b.dma(xi, x_d[tt * 128:(tt + 1) * 128, :], [], [xk])
                for g in range(4):
                    pst, pk = b.ps(g % 2 + 2 * (tt % 2))
                    for j in range(4):
                        c = g * 4 + j
                        b.tr(pst[:, j * 128:(j + 1) * 128], xi[:, c * 128:(c + 1) * 128], ident, [xk] + CK, [pk])
                    b.evac(o[:, g * 4:(g + 1) * 4, :], pst.rearrange("p (a b) -> p a b", a=4), [pk], [ok])
                b.dma(hA[:, :, tt * 128:(tt + 1) * 128].rearrange("c p t -> p c t"), o, [ok], ["hA"], q="act")

        b.cols, b.ident, b.ones_b, b.ones_f, b.flag, b.CK = cols, ident, ones_b, ones_f, flag, CK
        ctx = dict(locals())
        phase0()
        for layer in range(nlayers):
            load_cols(layer)
            phase1(b, ctx, layer)
            if stop_after == "p1":
                break
            mixerA(b, ctx, layer)
            if stop_after == "mA":
                break
            mixerB(b, ctx, layer)
            if stop_after == "mB":
                break
            mixerC(b, ctx, layer)
            if stop_after == "mC":
                break
            phase3a(b, ctx, layer)
            if stop_after == "p3a":
                break
            phase3b(b, ctx, layer, last=(layer == nlayers - 1))
        S.emit(nc)
    return nc, b


def phase1(b, c, layer):
    T = b.T
    TB = min(T, 2048)
    cols, CK = b.cols, b.CK
    W, hA = c["W"], c["hA"]
    groups = [(0, 16, c["Ain"]), (2048, 16, c["Bqk"]), (5120, 8, c["Bo"]), (6160, 24, c["Cqkv"]),
              (9232, 8, c["Cz"]), (10288, 48, c["MG"])]
    for tb in range(T // TB):
        b.phase()
        t0 = tb * TB
        b.wsetup()
        xn = b.alloc([16, TB], BF16)
        hc = [b.alloc([TB], F32) for _ in range(2)]
        sq = [b.alloc([TB], BF16) for _ in range(2)]
        rstd = b.alloc([TB], F32)
        ost = [b.alloc([TB], BF16) for _ in range(2)]
        cbs = b.colblocks(TB)
        for ch in range(16):
            h, hk = hc[ch % 2], "hc%d" % (ch % 2)
            q, qk = sq[ch % 2], "sq%d" % (ch % 2)
            b.dma(h, hA[ch, :, t0:t0 + TB], [], [hk])
            b.act(q, h, AF.Square, [hk], [qk])
            for i, (a, e_) in enumerate(cbs):
                pst, pk = b.ps(i)
                b.mm(pst[:, 0:e_ - a], b.ones_b, q[:, a:e_], ch == 0, ch == 15, [qk] + CK, [pk])
            b.ts(xn[:, ch, :], h, cols[:, C_MIXG + ch:C_MIXG + ch + 1], None, ALU.mult, None, [hk, "cols"], ["xn"])
        for i, (a, e_) in enumerate(cbs):
            pst, pk = b.ps(i)
            b.act(rstd[:, a:e_], pst[:, 0:e_ - a], AF.Sqrt, [pk], ["rstd"], bias=EPS, scale=1.0 / D)
        b.S.dve(lambda e: e.reciprocal(out=rstd, in_=rstd), ["rstd"], ["rstd"])
        for ch in range(16):
            b.tt(xn[:, ch, :], xn[:, ch, :], rstd, ALU.mult, ["xn", "rstd"], ["xn"])
        pi = 0
        oi = 0
        for col0, nch, dst in groups:
            for j in range(nch):
                wt, wk = b.load_w(W["w_in"][layer, :, col0 + j * 128:col0 + (j + 1) * 128], 16)
                o, ok = ost[oi % 2], "ost%d" % (oi % 2)
                oi += 1
                for (a, e_) in cbs:
                    pst, pk = b.ps(4 + pi % 4)
                    pi += 1
                    for k in range(16):
                        b.mm(pst[:, 0:e_ - a], wt[:, k, :], xn[:, k, a:e_], k == 0, k == 15, [wk, "xn"], [pk])
                    b.evac(o[:, a:e_], pst[:, 0:e_ - a], [pk], [ok])
                b.dma(dst[j, :, t0:t0 + TB], o, [ok], [], q="act")
        wtb = b.alloc([16, 512], BF16)
        otm = [b.alloc([512], BF16) for _ in range(2)]
        for (col0, dst) in [(4096, c["Bv"]), (3072, c["Bk"])]:
            for half in range(2):
                for qd in range(4):
                    cc = col0 + half * 512 + qd * 128
                    wt, wk = b.load_w(W["w_in"][layer, :, cc:cc + 128], 16)
                    b.copy(wtb[:, :, qd * 128:(qd + 1) * 128], wt[:, 0:16, :], [wk], ["wtb"], eng="pool")
                for tt in range(TB // 128):
                    pst, pk = b.ps(4 + pi % 4)
                    pi += 1
                    for k in range(16):
                        b.mm(pst, xn[:, k, tt * 128:(tt + 1) * 128], wtb[:, k, :], k == 0, k == 15, ["xn", "wtb"], [pk])
                    o, ok = otm[tt % 2], "otm%d" % (tt % 2)
                    b.evac(o, pst, [pk], [ok])
                    b.dma(dst[t0 + tt * 128:t0 + (tt + 1) * 128, half * 512:(half + 1) * 512], o, [ok], [], q="act")
        gst = b.alloc([16, 48], F32)
        gbf = b.alloc([16, 48], BF16)
        og = [b.alloc([48], F32) for _ in range(2)]
        b.dma(gst[:, :, 0:16], W["w_in"][layer, :, 6144:6160].rearrange("(k p) n -> p k n", p=128), [], ["gst"])
        b.dma(gst[:, :, 16:48], W["w_in"][layer, :, 10256:10288].rearrange("(k p) n -> p k n", p=128), [], ["gst"])
        b.copy(gbf, gst, ["gst"], ["gbf"], eng="pool")
        for tt in range(TB // 128):
            pst, pk = b.ps(4 + pi % 4)
            pi += 1
            for k in range(16):
                b.mm(pst[:, 0:48], xn[:, k, tt * 128:(tt + 1) * 128], gbf[:, k, :], k == 0, k == 15, ["xn", "gbf"], [pk])
            o, ok = og[tt % 2], "og%d" % (tt % 2)
            b.evac(o, pst[:, 0:48], [pk], [ok])
            b.dma(c["Gt"][t0 + tt * 128:t0 + (tt + 1) * 128, :], o, [ok], [], q="act")


def mixerA(b, c, layer):
    T, SEG = b.T, b.SEG
    cols, CK, flag, S = b.cols, b.CK, b.flag, b.S
    W, Ain, Y = c["W"], c["Ain"], c["Y"]
    b.phase()
    b.wsetup()
    PW = SEG + 3
    xp = b.alloc([2 * PW], BF16)
    ga = b.alloc([T], BF16)
    xcb = b.alloc([T], BF16)
    yb = b.alloc([T], BF16)
    xc = b.alloc([T], F32)
    rf = b.alloc([T], F32)
    uf = b.alloc([T], F32)
    sf = b.alloc([T], F32)
    hf = b.alloc([T], F32)
    hb = b.alloc([T], F32)
    carry = b.alloc([2], F32)
    cbs = b.colblocks(T)
    for n in range(8):
        for s in range(2):
            b.dma(xp[:, s * PW + 2:s * PW + 2 + SEG], Ain[n, :, s * SEG:(s + 1) * SEG], [], ["xp"])
        b.dma(ga, Ain[8 + n, :, :], [], ["ga"])
        b.memset(xp[:, 0:2], 0.0, ["xp"])
        b.memset(xp[:, 2 * PW - 1:2 * PW], 0.0, ["xp"])
        b.ts(xp[:, PW - 1:PW], xp[:, PW + 2:PW + 3], flag, None, ALU.mult, None, ["xp"] + CK, ["xp"])
        b.ts(xp[:, PW:PW + 2], xp[:, SEG:SEG + 2], flag, None, ALU.mult, None, ["xp"] + CK, ["xp"])
        for s in range(2):
            o = s * PW
            dst = xc[:, s * SEG:(s + 1) * SEG]
            b.ts(dst, xp[:, o:o + SEG], cols[:, C_ACW + n:C_ACW + n + 1], cols[:, C_ACB + n:C_ACB + n + 1],
                 ALU.mult, ALU.add, ["xp", "cols"], ["xc"])
            for k in range(1, 4):
                b.stt(dst, xp[:, o + k:o + k + SEG], cols[:, C_ACW + k * 8 + n:C_ACW + k * 8 + n + 1], dst,
                      ALU.mult, ALU.add, ["xp", "cols", "xc"], ["xc"])
        b.copy(xcb, xc, ["xc"], ["xcb"], eng="act")
        for e in range(2):
            wr, wrk = b.load_w(W["rglru_gate_w"][layer, e, 0, n], 1)
            wi_, wik = b.load_w(W["rglru_gate_w"][layer, e, 1, n], 1)
            for (wt, wk, dst, dk, g) in ((wr, wrk, rf, "rf", 0), (wi_, wik, uf, "uf", 1)):
                bias = cols[:, C_AGB + (e * 2 + g) * 8 + n:C_AGB + (e * 2 + g) * 8 + n + 1]
                for ci, (a, e_) in enumerate(cbs):
                    pst, pk = b.ps(ci % 8)
                    b.mm(pst[:, 0:e_ - a], wt[:, 0, :], xcb[:, a:e_], True, True, [wk, "xcb"], [pk])
                    b.act(dst[:, a:e_], pst[:, 0:e_ - a], AF.Sigmoid, [pk, "cols"], [dk], bias=bias)
            c8 = cols[:, C_AC8 + e * 8 + n:C_AC8 + e * 8 + n + 1]
            b.act(rf, rf, AF.Exp, ["rf", "cols"], ["rf"], scale=c8)
            b.act(sf, rf, AF.Square, ["rf"], ["sf"])
            b.act(sf, sf, AF.Sqrt, ["sf"], ["sf"], bias=1.0, scale=-1.0)
            b.tt(uf, uf, xc, ALU.mult, ["uf", "xc"], ["uf"])
            b.tt(uf, uf, sf, ALU.mult, ["uf", "sf"], ["uf"])
            if e == 0:
                S.dve(lambda en: en.tensor_tensor_scan(out=hf[:, 0:SEG], data0=rf[:, 0:SEG], data1=uf[:, 0:SEG],
                                                       initial=0.0, op0=ALU.mult, op1=ALU.add), ["rf", "uf"], ["hf"])
                b.tt(carry[:, 0:1], hf[:, SEG - 1:SEG], flag, ALU.mult, ["hf"] + CK, ["carry0"])
                S.dve(lambda en: en.tensor_tensor_scan(out=hf[:, SEG:T], data0=rf[:, SEG:T], data1=uf[:, SEG:T],
                                                       initial=carry[:, 0:1], op0=ALU.mult, op1=ALU.add),
                      ["rf", "uf", "carry0"], ["hf"])
            else:
                S.dve(lambda en: en.tensor_tensor_scan(out=hb[:, SEG:T][:, ::-1], data0=rf[:, SEG:T][:, ::-1],
                                                       data1=uf[:, SEG:T][:, ::-1], initial=0.0,
                                                       op0=ALU.mult, op1=ALU.add), ["rf", "uf"], ["hb"])
                b.tt(carry[:, 1:2], hb[:, SEG:SEG + 1], flag, ALU.mult, ["hb"] + CK, ["carry1"])
                S.dve(lambda en: en.tensor_tensor_scan(out=hb[:, 0:SEG][:, ::-1], data0=rf[:, 0:SEG][:, ::-1],
                                                       data1=uf[:, 0:SEG][:, ::-1], initial=carry[:, 1:2],
                                                       op0=ALU.mult, op1=ALU.add), ["rf", "uf", "carry1"], ["hb"])
        b.act(sf, ga, AF.Gelu_apprx_tanh, ["ga"], ["sf"])
        b.tt(hf, hf, hb, ALU.add, ["hf", "hb"], ["hf"])
        b.tt(yb, hf, sf, ALU.mult, ["hf", "sf"], ["yb"])
        b.dma(Y[n, :, :], yb, ["yb"], [], q="act")


def bc_mid(ap2, n):
    a = ap2.ap
    return bass.AP(ap2.tensor, ap2.offset, [list(a[0]), [0, n]] + [list(x) for x in a[1:]])


def bc_l3(ap3, n):
    a = ap3.ap
    return bass.AP(ap3.tensor, ap3.offset, [list(a[0]), list(a[1]), [0, n]])


def bc_last(ap1, n):
    a = ap1.ap
    return bass.AP(ap1.tensor, ap1.offset, [list(a[0]), [0, n]])


def mixerB(b, c, layer):
    T, SEG = b.T, b.SEG
    NCH, NCS = T // 128, SEG // 128
    cols, CK, flag, S, ident = b.cols, b.CK, b.flag, b.S, b.ident
    Uf, Ub, NGf, NGb, gbB = c["Uf"], c["Ub"], c["NGf"], c["NGb"], c["gbB"]
    Bqk, Bv, Bk, Bo, Gt, Y = c["Bqk"], c["Bv"], c["Bk"], c["Bo"], c["Gt"], c["Y"]
    b.phase()
    G = b.alloc([NCH, 16], F32)
    lf = b.alloc([NCH, 8], F32)
    sc = b.alloc([NCH, 8], F32)
    dec = b.alloc([NCH, 8], F32)
    b.dma(G, Gt[:, 0:16].rearrange("(c p) n -> p c n", p=128), [], ["G"])
    b.tt(G, G, bc_mid(gbB, NCH), ALU.add, ["G", "bc"], ["G"])
    b.act(lf[:, :, 0:4], G[:, :, 4:8], AF.Exp, ["G"], ["lf"], scale=-1.0)
    b.act(lf[:, :, 4:8], G[:, :, 12:16], AF.Exp, ["G"], ["lf"], scale=-1.0)
    b.act(lf, lf, AF.Ln, ["lf"], ["lf"], bias=1.0)
    S.act(lambda e: e.mul(out=lf, in_=lf, mul=-1.0), ["lf"], ["lf"])
    p0, k0 = b.ps(0)
    p1, k1 = b.ps(1)
    p2, k2 = b.ps(2)
    v3 = lambda p, n, w: p[:, 0:n * w].rearrange("p (a b) -> p a b", b=w)
    b.mm(v3(p0, NCH, 4), Uf, lf[:, :, 0:4], True, True, ["lf"] + CK, [k0])
    b.mm(v3(p1, NCH, 4), Ub, lf[:, :, 4:8], True, True, ["lf"] + CK, [k1])
    b.tt(sc[:, :, 0:4], G[:, :, 0:4], v3(p0, NCH, 4), ALU.subtract, ["G", k0], ["sc"])
    b.tt(sc[:, :, 4:8], G[:, :, 8:12], v3(p1, NCH, 4), ALU.subtract, ["G", k1], ["sc"])
    b.mm(v3(p2, NCH, 8), b.ones_f, lf, True, True, ["lf"] + CK, [k2])
    b.act(dec, v3(p2, NCH, 8), AF.Exp, [k2], ["dec"])

    qT = b.alloc([2, T], BF16)
    kT = b.alloc([2, T], BF16)
    ogT = b.alloc([2, T], BF16)
    yB = b.alloc([2, T], BF16)
    vaug = b.alloc([NCH, 260], BF16)
    ktm = b.alloc([NCH, 256], BF16)
    hsum = b.alloc([NCH, 256], F32)
    Cm2 = [b.alloc([2, 260], F32) for _ in range(2)]
    Cb2 = [b.alloc([2, 260], BF16) for _ in range(2)]
    Wrow2 = [b.alloc([128], F32) for _ in range(2)]
    DT2 = [b.alloc([128], F32) for _ in range(2)]
    ST2 = [b.alloc([128], BF16) for _ in range(2)]
    qw2 = [b.alloc([2, 128], BF16) for _ in range(2)]
    kw2 = [b.alloc([256], BF16) for _ in range(2)]
    rden2 = [b.alloc([1], F32) for _ in range(2)]
    sqc = b.alloc([256], F32)
    hn = b.alloc([256], F32)
    ssq = b.alloc([1], F32)
    sgo = b.alloc([2, 128], F32)
    for hd in range(4):
        b.dma(qT, Bqk[2 * hd:2 * hd + 2, :, :].rearrange("c p t -> p c t"), [], ["qT"])
        b.dma(kT, Bqk[8 + 2 * hd:8 + 2 * hd + 2, :, :].rearrange("c p t -> p c t"), [], ["kT"])
        b.dma(ogT, Bo[2 * hd:2 * hd + 2, :, :].rearrange("c p t -> p c t"), [], ["ogT"])
        b.dma(vaug[:, :, 0:256], Bv[:, hd * 256:(hd + 1) * 256].rearrange("(c p) e -> p c e", p=128), [], ["vaug"])
        b.dma(ktm, Bk[:, hd * 256:(hd + 1) * 256].rearrange("(c p) e -> p c e", p=128), [], ["ktm"])
        b.memset(vaug[:, :, 256:257], 1.0, ["vaug"])
        b.ts(qT, qT, 1.0 / 16.0, None, ALU.mult, None, ["qT"], ["qT"])
        b.memset(hsum, 0.0, ["hsum"])
        for e in range(2):
            b.memset(Cm2[e], 0.0, ["Cm%d" % e])
            b.memset(Cb2[e], 0.0, ["Cb%d" % e])
        for idx in range(NCH):
            for e in range(2):
                U, NG = (Uf, NGf) if e == 0 else (Ub, NGb)
                ch = idx if e == 0 else NCH - 1 - idx
                Cm, Cb, Wrow, DT, ST, qw, kw, rden = Cm2[e], Cb2[e], Wrow2[e], DT2[e], ST2[e], qw2[e], kw2[e], rden2[e]
                kCm, kCb, kW, kDT, kST, kqw, kkw, krd = ("%s%d" % (n_, e) for n_ in ("Cm", "Cb", "Wrow", "DT", "ST", "qw", "kw", "rden"))
                if idx == NCS:
                    b.ts(Cm, Cm, flag, None, ALU.mult, None, [kCm] + CK, [kCm])
                    b.copy(Cb, Cm, [kCm], [kCb], eng="act")
                col = e * 4 + hd
                cs = slice(ch * 128, (ch + 1) * 128)
                lfb = bc_last(lf[:, ch, col:col + 1], 128)
                pB, pBk = b.ps(0)
                pD, pDk = b.ps(1)
                pS, pSk = b.ps(2)
                pO, pOk = b.ps(3 + 2 * e)
                pC, pCk = b.ps(4 + 2 * e)
                pN, pNk = b.ps(7)
                b.mm(pB[:, 0:128], lfb, U, True, True, ["lf"] + CK, [pBk])
                b.act(Wrow, pB[:, 0:128], AF.Exp, [pBk], [kW])
                b.mm(pD[:, 0:128], lfb, U, True, False, ["lf"] + CK, [pDk])
                b.mm(pD[:, 0:128], ident, NG, False, True, CK, [pDk])
                b.act(DT, pD[:, 0:128], AF.Exp, [pDk, "sc"], [kDT], bias=sc[:, ch, col:col + 1])
                b.mm(pS[:, 0:128], kT[:, 0, cs], qT[:, 0, cs], True, False, ["kT", "qT"], [pSk])
                b.mm(pS[:, 0:128], kT[:, 1, cs], qT[:, 1, cs], False, True, ["kT", "qT"], [pSk])
                b.tt(ST, pS[:, 0:128], DT, ALU.mult, [pSk, kDT], [kST])
                for dc in range(2):
                    b.tt(qw[:, dc, :], qT[:, dc, cs], Wrow, ALU.mult, ["qT", kW], [kqw], eng="pool")
                b.mm(pO[:, 0:257], ST, vaug[:, ch, 0:257], True, False, [kST, "vaug"], [pOk])
                b.mm(pO[:, 0:257], qw[:, 0, :], Cb[:, 0, 0:257], False, False, [kqw, kCb], [pOk])
                b.mm(pO[:, 0:257], qw[:, 1, :], Cb[:, 1, 0:257], False, True, [kqw, kCb], [pOk])
                b.act(rden, pO[:, 256:257], AF.Abs, [pOk], [krd])
                b.ts(rden, rden, 1.0, None, ALU.max, None, [krd], [krd])
                S.dve(lambda en, rden=rden: en.reciprocal(out=rden, in_=rden), [krd], [krd])
                b.stt(hsum[:, ch, :], pO[:, 0:256], rden, hsum[:, ch, :], ALU.mult, ALU.add, [pOk, krd, "hsum"], ["hsum"])
                wcol = DT[:, 127:128] if e == 0 else DT[:, 0:1]
                b.ts(kw, ktm[:, ch, :], wcol, None, ALU.mult, None, ["ktm", kDT], [kkw], eng="pool")
                for dc in range(2):
                    b.mm(pC[:, dc * 256:(dc + 1) * 256], kw[:, dc * 128:(dc + 1) * 128], vaug[:, ch, 0:256], True, True,
                         [kkw, "vaug"], [pCk])
                    b.mm(pN[:, 2 * e + dc:2 * e + dc + 1], kw[:, dc * 128:(dc + 1) * 128], vaug[:, ch, 256:257], True, True,
                         [kkw, "vaug"], [pNk])
                dcol = dec[:, ch, col:col + 1]
                b.stt(Cm[:, :, 0:256], Cm[:, :, 0:256], dcol, pC.rearrange("p (a b) -> p a b", a=2), ALU.mult, ALU.add,
                      [kCm, "dec", pCk], [kCm])
                b.stt(Cm[:, :, 256:257], Cm[:, :, 256:257], dcol, pN[:, 2 * e:2 * e + 2].rearrange("p (a b) -> p a b", b=1),
                      ALU.mult, ALU.add, [kCm, "dec", pNk], [kCm])
                b.copy(Cb[:, :, 0:257], Cm[:, :, 0:257], [kCm], [kCb], eng="act")
        for ch in range(NCH):
            cs = slice(ch * 128, (ch + 1) * 128)
            pT, pTk = b.ps(ch % 2)
            b.act(sqc, hsum[:, ch, :], AF.Square, ["hsum"], ["sqc"])
            S.dve(lambda en, ch=ch: en.reduce_sum(out=ssq, in_=sqc, axis=AX.X), ["sqc"], ["ssq"])
            b.act(ssq, ssq, AF.Sqrt, ["ssq"], ["ssq"], bias=EPS, scale=1.0 / 256.0)
            S.dve(lambda en: en.reciprocal(out=ssq, in_=ssq), ["ssq"], ["ssq"])
            b.ts(hn, hsum[:, ch, :], ssq, None, ALU.mult, None, ["hsum", "ssq"], ["hn"])
            for ec in range(2):
                b.tr(pT[:, ec * 128:(ec + 1) * 128], hn[:, ec * 128:(ec + 1) * 128], ident, ["hn"] + CK, [pTk])
            b.act(sgo, ogT[:, :, cs], AF.Sigmoid, ["ogT"], ["sgo"])
            for ec in range(2):
                gcol = cols[:, C_BNG + 2 * hd + ec:C_BNG + 2 * hd + ec + 1]
                b.stt(yB[:, ec, cs], pT[:, ec * 128:(ec + 1) * 128], gcol, sgo[:, ec, :], ALU.mult, ALU.mult,
                      [pTk, "cols", "sgo"], ["yB"])
        for ec in range(2):
            b.dma(Y[8 + 2 * hd + ec, :, :], yB[:, ec, :], ["yB"], [], q="act")


def mixerC(b, c, layer):
    T, SEG = b.T, b.SEG
    NC, NCS = T // 64, SEG // 64
    cols, CK, flag, S, ident = b.cols, b.CK, b.flag, b.S, b.ident
    Uf, Ub, NGf, NGb, PSf, PSb, dtb, negA = (c[k] for k in ("Uf", "Ub", "NGf", "NGb", "PSf", "PSb", "dtb", "negA"))
    Cqkv, Cz, Gt, Y = c["Cqkv"], c["Cz"], c["Gt"], c["Y"]
    b.phase()
    H = 64
    a64 = lambda shape, dt: b.alloc(shape, dt)[0:H]
    gp = a64([NC, 32], F32)
    g = a64([NC, 16], F32)
    be = a64([NC, 16], F32)
    Gc = a64([NC, 16], F32)
    nG = a64([NC, 16], F32)
    eg = a64([NC, 16], F32)
    ed = a64([NC, 16], F32)
    bk = a64([NC, 16], F32)
    gL = b.alloc([NC, 16], F32)
    b.dma(gp, Gt[:, 16:48].rearrange("(c p) n -> p c n", p=H), [], ["gpTt"])
    b.tt(g[:, :, 0:8], gp[:, :, 0:8], bc_mid(dtb[0:H, 0:8], NC), ALU.add, ["gpTt", "bc"], ["g"])
    b.tt(g[:, :, 8:16], gp[:, :, 16:24], bc_mid(dtb[0:H, 8:16], NC), ALU.add, ["gpTt", "bc"], ["g"])
    b.act(g, g, AF.Exp, ["g"], ["g"])
    b.act(g, g, AF.Ln, ["g"], ["g"], bias=1.0)
    b.tt(g, g, bc_mid(negA[0:H, :], NC), ALU.mult, ["g", "bc"], ["g"])
    b.act(be[:, :, 0:8], gp[:, :, 8:16], AF.Sigmoid, ["gpTt"], ["be"])
    b.act(be[:, :, 8:16], gp[:, :, 24:32], AF.Sigmoid, ["gpTt"], ["be"])
    v3 = lambda p, parts, n, w: p[0:parts, 0:n * w].rearrange("p (a b) -> p a b", b=w)
    for e in range(2):
        U = Uf if e == 0 else Ub
        pa, ka = b.ps(0 + e)
        pb_, kb_ = b.ps(2 + e)
        pc, kc = b.ps(4 + e)
        gs = g[:, :, e * 8:(e + 1) * 8]
        b.mm(v3(pa, H, NC, 8), U[0:H, 0:H], gs, True, True, ["g"] + CK, [ka])
        b.copy(Gc[:, :, e * 8:(e + 1) * 8], v3(pa, H, NC, 8), [ka], ["Gc"], eng="act")
        b.mm(v3(pb_, H, NC, 8), b.ones_f[0:H, 0:H], gs, True, True, ["g"] + CK, [kb_])
        b.tt(ed[:, :, e * 8:(e + 1) * 8], v3(pb_, H, NC, 8), Gc[:, :, e * 8:(e + 1) * 8], ALU.subtract, [kb_, "Gc"], ["ed"])
        b.mm(v3(pc, 128, NC, 8), b.ones_f[0:H, :], gs, True, True, ["g"] + CK, [kc])
        b.act(gL[:, :, e * 8:(e + 1) * 8], v3(pc, 128, NC, 8), AF.Exp, [kc], ["gL"])
    b.act(ed, ed, AF.Exp, ["ed"], ["ed"])
    b.act(eg, Gc, AF.Exp, ["Gc"], ["eg"])
    b.tt(bk, be, eg, ALU.mult, ["be", "eg"], ["bk"])
    S.act(lambda en: en.mul(out=nG, in_=Gc, mul=-1.0), ["Gc"], ["nG"])

    PW = SEG + 3
    xp = b.alloc([2 * PW], BF16)
    cvs = b.alloc([T], F32)
    sqb = b.alloc([T], BF16)
    yC = sqb
    qT = b.alloc([T], BF16)
    kT = b.alloc([T], BF16)
    zs = xp[:, 0:T]
    rn = b.alloc([512], F32)
    ktm = a64([NC, 128], BF16)
    vtm = a64([NC, 128], BF16)
    osum = a64([NC, 128], F32)
    Ttb2 = [a64([NC, 64], BF16) for _ in range(2)]
    qkb2 = [a64([NC, 64], BF16) for _ in range(2)]
    Sm2 = [b.alloc([128], F32) for _ in range(2)]
    Sb2 = [b.alloc([128], BF16) for _ in range(2)]
    G = min(8, NC)
    gsrc = cvs if T >= 8 * G * 64 else b.alloc([8 * G * 64], F32)
    gbuf = lambda i: gsrc[0:H, i * G * 64:(i + 1) * G * 64].rearrange("p (a b) -> p a b", b=64)
    DTg, DAg = gbuf(0), gbuf(1)
    Ng = [gbuf(2), gbuf(3)]
    Mg = [gbuf(4), gbuf(5)]
    Pg = [gbuf(6), gbuf(7)]
    sqg = gsrc[0:H, 0:G * 128].rearrange("p (a b) -> p a b", b=128)
    ssg = a64([G], F32)
    vb2 = [a64([128], F32) for _ in range(2)]
    negr2 = [a64([128], BF16) for _ in range(2)]
    vnew2 = [a64([128], BF16) for _ in range(2)]
    kdec2 = [a64([128], BF16) for _ in range(2)]
    o12 = [a64([128], F32) for _ in range(2)]
    o22 = [a64([128], F32) for _ in range(2)]
    sqc = a64([128], F32)
    on = a64([128], F32)
    ssq = a64([1], F32)
    I64 = ident[0:H, 0:H]
    cbs = b.colblocks(T)

    def conv_silu(ci):
        for s in range(2):
            b.dma(xp[:, s * PW + 2:s * PW + 2 + SEG], Cqkv[ci, :, s * SEG:(s + 1) * SEG], [], ["xp"])
        b.memset(xp[:, 0:2], 0.0, ["xp"])
        b.memset(xp[:, 2 * PW - 1:2 * PW], 0.0, ["xp"])
        b.ts(xp[:, PW - 1:PW], xp[:, PW + 2:PW + 3], flag, None, ALU.mult, None, ["xp"] + CK, ["xp"])
        b.ts(xp[:, PW:PW + 2], xp[:, SEG:SEG + 2], flag, None, ALU.mult, None, ["xp"] + CK, ["xp"])
        for s in range(2):
            o = s * PW
            dst = cvs[:, s * SEG:(s + 1) * SEG]
            wc = lambda k: cols[:, C_CCW + k * 24 + ci:C_CCW + k * 24 + ci + 1]
            b.ts(dst, xp[:, o:o + SEG], wc(0), None, ALU.mult, None, ["xp", "cols"], ["cvs"])
            for k in range(1, 4):
                b.stt(dst, xp[:, o + k:o + k + SEG], wc(k), dst, ALU.mult, ALU.add, ["xp", "cols", "cvs"], ["cvs"])
        b.act(cvs, cvs, AF.Silu, ["cvs"], ["cvs"])

    def l2norm(dstT, dkey, scale):
        b.act(sqb, cvs, AF.Square, ["cvs"], ["sqb"])
        for ci_, (a, e_) in enumerate(cbs):
            pst, pk = b.ps(ci_ % 2)
            b.mm(pst[:, 0:e_ - a], b.ones_b, sqb[:, a:e_], True, True, ["sqb"] + CK, [pk])
            b.act(rn[:, 0:e_ - a], pst[:, 0:e_ - a], AF.Sqrt, [pk], ["rn"], bias=EPS)
            S.dve(lambda en, w=e_ - a: en.reciprocal(out=rn[:, 0:w], in_=rn[:, 0:w]), ["rn"], ["rn"])
            b.stt(cvs[:, a:e_], cvs[:, a:e_], scale, rn[:, 0:e_ - a], ALU.mult, ALU.mult, ["cvs", "rn"], ["cvs"])
        b.copy(dstT, cvs, ["cvs"], [dkey], eng="act")

    def to_tm(dst, dkey):
        for ch in range(NC):
            pst, pk = b.ps(2 + (ch // 4) % 2)
            sub = pst[0:H, (ch % 4) * 128:(ch % 4 + 1) * 128]
            b.tr(sub, cvs[:, ch * 64:(ch + 1) * 64], ident, ["cvs"] + CK, [pk])
            if ch % 4 == 3 or ch == NC - 1:
                n = ch % 4 + 1
                c0 = ch - n + 1
                b.evac(dst[:, c0:c0 + n, :], pst[0:H, 0:n * 128].rearrange("p (a b) -> p a b", b=128), [pk], [dkey])

    for hd in range(8):
        S.barrier()
        conv_silu(hd)
        l2norm(qT, "qT", 128.0 ** -0.5)
        conv_silu(8 + hd)
        l2norm(kT, "kT", 1.0)
        to_tm(ktm, "ktm")
        conv_silu(16 + hd)
        to_tm(vtm, "vtm")
        b.dma(zs, Cz[hd, :, :], [], ["xp"])
        b.act(zs, zs, AF.Silu, ["xp"], ["xp"])
        S.barrier()
        for e in range(2):
            col = e * 8 + hd
            U, NGm, PSm = (Uf, NGf, PSf) if e == 0 else (Ub, NGb, PSb)
            U64, NG64, PS64 = U[0:H, 0:H], NGm[0:H, 0:H], PSm[0:H, 0:H]
            for c0 in range(0, NC, G):
                P = [b.ps(i) for i in range(8)]
                V = lambda i: P[i][0][0:H, 0:G * 64].rearrange("p (a b) -> p a b", b=64)
                slot = lambda i, j: P[i][0][0:H, j * 64:(j + 1) * 64]
                PK = lambda i: P[i][1]
                for j in range(G):
                    gb = bc_last(g[:, c0 + j, col:col + 1], H)
                    b.mm(slot(0, j), gb, U64, True, False, ["g"] + CK, [PK(0)])
                    b.mm(slot(0, j), I64, NG64, False, True, CK, [PK(0)])
                    b.mm(slot(1, j), gb, U64, True, False, ["g"] + CK, [PK(1)])
                    b.mm(slot(1, j), I64, PS64, False, True, CK, [PK(1)])
                b.tt(DTg, V(0), bc_l3(nG[:, c0:c0 + G, col:col + 1], 64), ALU.add, [PK(0), "nG"], ["DTg"])
                b.act(DTg, DTg, AF.Exp, ["DTg"], ["DTg"])
                b.tt(DAg, bc_l3(Gc[:, c0:c0 + G, col:col + 1], 64), V(1), ALU.subtract, [PK(1), "Gc"], ["DAg"])
                b.act(DAg, DAg, AF.Exp, ["DAg"], ["DAg"])
                b.tt(DAg, DAg, bc_l3(be[:, c0:c0 + G, col:col + 1], 64), ALU.mult, ["DAg", "be"], ["DAg"], eng="pool")
                for j in range(G):
                    cs = slice((c0 + j) * 64, (c0 + j + 1) * 64)
                    b.mm(slot(2, j), kT[:, cs], kT[:, cs], True, True, ["kT"], [PK(2)])
                    b.mm(slot(3, j), kT[:, cs], qT[:, cs], True, True, ["kT", "qT"], [PK(3)])
                b.tt(Ng[0], V(2), DAg, ALU.mult, [PK(2), "DAg"], ["N0"])
                b.tt(qkb2[e][:, c0:c0 + G, :], V(3), DTg, ALU.mult, [PK(3), "DTg"], ["qkb%d" % e])
                for j in range(G):
                    b.mm(slot(4, j), Ng[0][:, j, :], I64, True, True, ["N0"] + CK, [PK(4)])
                b.copy(Mg[0], V(4), [PK(4)], ["M0"], eng="act")
                b.tt(Pg[0], bc_mid(I64, G), V(4), ALU.subtract, [PK(4)] + CK, ["P0"])
                for lv in range(1, 6):
                    pi_, ci_ = (lv - 1) % 2, lv % 2
                    for j in range(G):
                        b.mm(slot(5, j), Mg[pi_][:, j, :], Ng[pi_][:, j, :], True, True, ["M%d" % pi_, "N%d" % pi_], [PK(5)])
                    b.copy(Ng[ci_], V(5), [PK(5)], ["N%d" % ci_], eng="act")
                    if lv < 5:
                        for j in range(G):
                            b.mm(slot(6, j), Ng[pi_][:, j, :], Mg[pi_][:, j, :], True, True, ["M%d" % pi_, "N%d" % pi_], [PK(6)])
                        b.copy(Mg[ci_], V(6), [PK(6)], ["M%d" % ci_], eng="dve")
                    for j in range(G):
                        b.mm(slot(7, j), Ng[ci_][:, j, :], Pg[pi_][:, j, :], True, True, ["N%d" % ci_, "P%d" % pi_], [PK(7)])
                    if lv < 5:
                        b.tt(Pg[ci_], Pg[pi_], V(7), ALU.add, ["P%d" % pi_, PK(7)], ["P%d" % ci_])
                    else:
                        b.tt(Ttb2[e][:, c0:c0 + G, :], Pg[pi_], V(7), ALU.add, ["P%d" % pi_, PK(7)], ["Ttb%d" % e])
        b.memset(osum, 0.0, ["osum"])
        for e in range(2):
            b.memset(Sm2[e], 0.0, ["Sm%d" % e])
            b.memset(Sb2[e], 0.0, ["Sb%d" % e])
        for idx in range(NC):
            for e in range(2):
                ch = idx if e == 0 else NC - 1 - idx
                col = e * 8 + hd
                Sm, Sb, vb, negr, vnew, kdec, o1, o2 = Sm2[e], Sb2[e], vb2[e], negr2[e], vnew2[e], kdec2[e], o12[e], o22[e]
                kS, kSb, kvb, knr, kvn, kkd, ko1, ko2 = ("%s%d" % (n_, e) for n_ in ("Sm", "Sb", "vb", "negr", "vnew", "kdec", "o1", "o2"))
                if idx == NCS:
                    b.ts(Sm, Sm, flag, None, ALU.mult, None, [kS] + CK, [kS])
                    b.copy(Sb, Sm, [kS], [kSb], eng="act")
                cs = slice(ch * 64, (ch + 1) * 64)
                pA, pAk = b.ps(4 * e + 0)
                pB, pBk = b.ps(4 * e + 1)
                pC, pCk = b.ps(4 * e + 2)
                pD, pDk = b.ps(4 * e + 3)
                b.mm(pA[0:H, 0:128], kT[:, cs], Sb, True, True, ["kT", kSb], [pAk])
                b.mm(pA[0:H, 128:256], qT[:, cs], Sb, True, True, ["qT", kSb], [pAk])
                b.ts(vb, vtm[:, ch, :], be[:, ch, col:col + 1], None, ALU.mult, None, ["vtm", "be"], [kvb], eng="pool")
                b.stt(negr, pA[0:H, 0:128], bk[:, ch, col:col + 1], vb, ALU.mult, ALU.subtract, [pAk, "bk", kvb], [knr])
                b.mm(pB[0:H, 0:128], Ttb2[e][:, ch, :], negr, True, True, ["Ttb%d" % e, knr], [pBk])
                S.act(lambda en, pB=pB, vnew=vnew: en.mul(out=vnew, in_=pB[0:H, 0:128], mul=-1.0), [pBk], [kvn])
                b.mm(pC[0:H, 0:128], qkb2[e][:, ch, :], vnew, True, True, ["qkb%d" % e, kvn], [pCk])
                b.ts(kdec, ktm[:, ch, :], ed[:, ch, col:col + 1], None, ALU.mult, None, ["ktm", "ed"], [kkd], eng="pool")
                b.mm(pD[:, 0:128], kdec, vnew, True, True, [kkd, kvn], [pDk])
                b.copy(o1, pC[0:H, 0:128], [pCk], [ko1], eng="act")
                b.stt(o2, pA[0:H, 128:256], eg[:, ch, col:col + 1], o1, ALU.mult, ALU.add, [pAk, "eg", ko1], [ko2])
                b.tt(osum[:, ch, :], osum[:, ch, :], o2, ALU.add, ["osum", ko2], ["osum"])
                b.stt(Sm, Sm, gL[:, ch, col:col + 1], pD[:, 0:128], ALU.mult, ALU.add, [kS, "gL", pDk], [kS])
                b.copy(Sb, Sm, [kS], [kSb], eng="act")
        S.barrier()
        for gi, c0 in enumerate(range(0, NC, G)):
            pT, pTk = b.ps(gi % 2)
            og = osum[:, c0:c0 + G, :]
            b.act(sqg, og, AF.Square, ["osum"], ["sqg"])
            S.dve(lambda en: en.reduce_sum(out=ssg, in_=sqg, axis=AX.X), ["sqg"], ["ssg"])
            b.act(ssg, ssg, AF.Sqrt, ["ssg"], ["ssg"], bias=EPS, scale=1.0 / 128.0)
            S.dve(lambda en: en.reciprocal(out=ssg, in_=ssg), ["ssg"], ["ssg"])
            b.tt(sqg, og, bc_l3(ssg.rearrange("p (a b) -> p a b", b=1), 128), ALU.mult, ["osum", "ssg"], ["sqg"])
            for j in range(G):
                b.tr(pT[:, j * 64:(j + 1) * 64], sqg[:, j, :], I64, ["sqg"] + CK, [pTk])
            cs = slice(c0 * 64, (c0 + G) * 64)
            b.stt(yC[:, cs], pT[:, 0:G * 64], cols[:, C_CNG + hd:C_CNG + hd + 1], zs[:, cs], ALU.mult, ALU.mult,
                  [pTk, "cols", "xp"], ["sqb"])
        b.dma(Y[16 + hd, :, :], yC, ["sqb"], [], q="act")


def phase3a(b, c, layer):
    T = b.T
    TB = min(T, 512)
    W, Y, MG, hA, hM = c["W"], c["Y"], c["MG"], c["hA"], c["hM"]
    for tb in range(T // TB):
        b.phase()
        t0 = tb * TB
        b.wsetup()
        b.wcache = {"ap": c["Wc3a"], "idx": 0, "mode": "fill" if tb == 0 else "use"}
        Yt = b.alloc([24, TB], BF16)
        h = b.alloc([16, TB], F32)
        mrg = b.alloc([16, TB], BF16)
        mgt = [b.alloc([3, TB], BF16) for _ in range(2)]
        sg = [b.alloc([TB], F32) for _ in range(3)]
        acc = b.alloc([TB], F32)
        tmp = b.alloc([TB], F32)
        b.dma(Yt, Y[:, :, t0:t0 + TB].rearrange("c p t -> p c t"), [], ["Yt"])
        b.dma(h, hA[:, :, t0:t0 + TB].rearrange("c p t -> p c t"), [], ["h%d" % i for i in range(16)])
        for m in range(16):
            mg_ = mgt[m % 2]
            pks = []
            for g in range(3):
                mk = "mgt%d_%d" % (m % 2, g)
                b.dma(mg_[:, g, :], MG[g * 16 + m, :, t0:t0 + TB], [], [mk])
                wt, wk = b.load_w(W["w_branch"][layer, g, :, m * 128:(m + 1) * 128], 8)
                pst, pk = b.ps(g + 4 * (m % 2))
                pks.append((pst, pk))
                for k in range(8):
                    b.mm(pst[:, 0:TB], wt[:, k, :], Yt[:, g * 8 + k, :], k == 0, k == 7, [wk, "Yt"], [pk])
                b.act(sg[g], mg_[:, g, :], AF.Sigmoid, [mk], ["sg%d" % g])
            b.tt(acc, pks[0][0][:, 0:TB], sg[0], ALU.mult, [pks[0][1], "sg0"], ["acc"])
            b.tt(tmp, pks[1][0][:, 0:TB], sg[1], ALU.mult, [pks[1][1], "sg1"], ["tmp"])
            b.tt(acc, acc, tmp, ALU.add, ["acc", "tmp"], ["acc"])
            b.tt(tmp, pks[2][0][:, 0:TB], sg[2], ALU.mult, [pks[2][1], "sg2"], ["tmp"])
            b.tt(mrg[:, m, :], acc, tmp, ALU.add, ["acc", "tmp"], ["mrg"])
        for m in range(16):
            wt, wk = b.load_w(W["w_out"][layer, :, m * 128:(m + 1) * 128], 16)
            pst, pk = b.ps(3 + 4 * (m % 2))
            for k in range(16):
                b.mm(pst[:, 0:TB], wt[:, k, :], mrg[:, k, :], k == 0, k == 15, [wk, "mrg"], [pk])
            b.tt(h[:, m, :], h[:, m, :], pst[:, 0:TB], ALU.add, [pk, "h%d" % m], ["h%d" % m])
        b.dma(hM[:, :, t0:t0 + TB].rearrange("c p t -> p c t"), h, ["h%d" % i for i in range(16)], [], q="act")
    b.wcache = None


def phase3b(b, c, layer, last):
    T, SEG = b.T, b.SEG
    TB = min(SEG, 512)
    NE = TB + 2
    cols, CK, flag, S, ident = b.cols, b.CK, b.flag, b.S, b.ident
    W, hA, hM, p_d, y_d = c["W"], c["hA"], c["hM"], c["p_d"], c["y_d"]
    HK = ["h%d" % i for i in range(16)]
    for tb in range(T // TB):
        b.phase()
        t0 = tb * TB
        b.wsetup()
        b.wcache = {"ap": c["Wc3b"], "idx": 0, "mode": "fill" if tb == 0 else "use"}
        h = b.alloc([16, NE], F32)
        xn = b.alloc([16, NE], BF16)
        sq = [b.alloc([NE], BF16) for _ in range(2)]
        rstd = b.alloc([NE], F32)
        act_ = b.alloc([48, TB], BF16)
        cg = b.alloc([TB], F32)
        cu = b.alloc([TB], F32)
        pin = b.alloc([256], F32)
        pT = b.alloc([2, TB], BF16)
        b.dma(h[:, :, 1:TB + 1], hM[:, :, t0:t0 + TB].rearrange("c p t -> p c t"), [], HK)
        if t0 > 0:
            b.dma(h[:, :, 0:1], hM[:, :, t0 - 1:t0].rearrange("c p t -> p c t"), [], ["hl"], slow=True)
        else:
            b.memset(h[:, :, 0:1], 0.0, ["hl"])
        if t0 + TB < T:
            b.dma(h[:, :, TB + 1:TB + 2], hM[:, :, t0 + TB:t0 + TB + 1].rearrange("c p t -> p c t"), [], ["hr"], slow=True)
        else:
            b.memset(h[:, :, TB + 1:TB + 2], 0.0, ["hr"])
        HALL = HK + ["hl", "hr"]

        def rms(lo, hi, gcol0, out, okey, hkeys):
            n = hi - lo
            cb = [(a + lo, e_ + lo) for (a, e_) in b.colblocks(n)]
            for ch in range(16):
                q, qk = sq[ch % 2], "sq%d" % (ch % 2)
                b.act(q[:, lo:hi], h[:, ch, lo:hi], AF.Square, hkeys, [qk])
                for i, (a, e_) in enumerate(cb):
                    pst, pk = b.ps(i)
                    b.mm(pst[:, 0:e_ - a], b.ones_b, q[:, a:e_], ch == 0, ch == 15, [qk] + CK, [pk])
            for i, (a, e_) in enumerate(cb):
                pst, pk = b.ps(i)
                b.act(rstd[:, a:e_], pst[:, 0:e_ - a], AF.Sqrt, [pk], ["rstd"], bias=EPS, scale=1.0 / D)
            S.dve(lambda en: en.reciprocal(out=rstd[:, lo:hi], in_=rstd[:, lo:hi]), ["rstd"], ["rstd"])
            for ch in range(16):
                b.stt(out[:, ch, :], h[:, ch, lo:hi], cols[:, gcol0 + ch:gcol0 + ch + 1], rstd[:, lo:hi],
                      ALU.mult, ALU.mult, hkeys + ["cols", "rstd"], [okey])

        rms(0, NE, C_FFNG, xn, "xn", HALL)
        if t0 == SEG:
            b.ts(xn[:, :, 0:1], xn[:, :, 0:1], flag, None, ALU.mult, None, ["xn"] + CK, ["xn"])
        if t0 + TB == SEG:
            b.ts(xn[:, :, TB + 1:TB + 2], xn[:, :, TB + 1:TB + 2], flag, None, ALU.mult, None, ["xn"] + CK, ["xn"])
        cbs = b.colblocks(NE)
        for j in range(48):
            wg, wgk = b.load_w(W["ffn_w_up"][layer, :, j * 128:(j + 1) * 128], 16)
            pg = b.psum[2 * (j % 2)]
            pgk = ["ps%d" % (4 * (j % 2)), "ps%d" % (4 * (j % 2) + 1)]
            for (a, e_) in cbs:
                for k in range(16):
                    b.mm(pg[:, a:e_], wg[:, k, :], xn[:, k, a:e_], k == 0, k == 15, [wgk, "xn"], pgk)
            wu, wuk = b.load_w(W["ffn_w_up"][layer, :, DFF + j * 128:DFF + (j + 1) * 128], 16)
            pu = b.psum[2 * (j % 2) + 1]
            puk = ["ps%d" % (4 * (j % 2) + 2), "ps%d" % (4 * (j % 2) + 3)]
            for (a, e_) in cbs:
                for k in range(16):
                    b.mm(pu[:, a:e_], wu[:, k, :], xn[:, k, a:e_], k == 0, k == 15, [wuk, "xn"], puk)
            for (pp, ppk, dst, dk, ci) in ((pg, pgk, cg, "cg", j), (pu, puk, cu, "cu", 48 + j)):
                wc = lambda k: cols[:, C_FCW + k * 96 + ci:C_FCW + k * 96 + ci + 1]
                b.ts(dst, pp[:, 1:TB + 1], wc(1), None, ALU.mult, None, ppk + ["cols"], [dk])
                b.stt(dst, pp[:, 0:TB], wc(0), dst, ALU.mult, ALU.add, ppk + ["cols", dk], [dk])
                b.stt(dst, pp[:, 2:TB + 2], wc(2), dst, ALU.mult, ALU.add, ppk + ["cols", dk], [dk])
            b.act(cg, cg, AF.Gelu_apprx_tanh, ["cg"], ["cg"])
            b.tt(act_[:, j, :], cg, cu, ALU.mult, ["cg", "cu"], ["act"])
        for m in range(16):
            pst, pk = b.ps(m % 2)
            for q_ in range(3):
                wt, wk = b.load_w(W["ffn_w_down"][layer, q_ * 2048:(q_ + 1) * 2048, m * 128:(m + 1) * 128], 16)
                for k in range(16):
                    b.mm(pst[:, 0:TB], wt[:, k, :], act_[:, q_ * 16 + k, :], q_ == 0 and k == 0, q_ == 2 and k == 15,
                         [wk, "act"], [pk])
            b.tt(h[:, m, 1:TB + 1], h[:, m, 1:TB + 1], pst[:, 0:TB], ALU.add, [pk, "h%d" % m], ["h%d" % m])
        xn3 = xn[:, :, 1:TB + 1]
        rms(1, TB + 1, C_PLEG, xn3, "xn", HK)
        for tt in range(TB // 128):
            b.dma(pin, p_d[layer, t0 + tt * 128:t0 + (tt + 1) * 128, :], [], ["pin"])
            pst, pk = b.ps(2 + tt % 2)
            for k in range(2):
                b.tr(pst[:, k * 128:(k + 1) * 128], pin[:, k * 128:(k + 1) * 128], ident, ["pin"] + CK, [pk])
            b.evac(pT[:, :, tt * 128:(tt + 1) * 128], pst[:, 0:256].rearrange("p (a b) -> p a b", a=2), [pk], ["pT"])
        for m in range(16):
            wg, wgk = b.load_w(W["ple_w_gate"][layer, :, m * 128:(m + 1) * 128], 16)
            pa, pak = b.ps(4 + 2 * (m % 2))
            for k in range(16):
                b.mm(pa[:, 0:TB], wg[:, k, :], xn3[:, k, :], k == 0, k == 15, [wgk, "xn"], [pak])
            wp, wpk = b.load_w(W["ple_w_proj"][layer, :, m * 128:(m + 1) * 128], 2)
            pb_, pbk = b.ps(5 + 2 * (m % 2))
            for k in range(2):
                b.mm(pb_[:, 0:TB], wp[:, k, :], pT[:, k, :], k == 0, k == 1, [wpk, "pT"], [pbk])
            b.act(cg, pa[:, 0:TB], AF.Sigmoid, [pak], ["cg"])
            b.tt(cu, pb_[:, 0:TB], cg, ALU.mult, [pbk, "cg"], ["cu"])
            b.tt(h[:, m, 1:TB + 1], h[:, m, 1:TB + 1], cu, ALU.add, ["cu", "h%d" % m], ["h%d" % m])
        if not last:
            b.dma(hA[:, :, t0:t0 + TB].rearrange("c p t -> p c t"), h[:, :, 1:TB + 1], HK, [], q="act")
        else:
            if b.debug:
                b.dma(hA[:, :, t0:t0 + TB].rearrange("c p t -> p c t"), h[:, :, 1:TB + 1], HK, [], q="act")
            xf = b.alloc([16, TB], F32)
            yo = [b.alloc([D], F32) for _ in range(2)]
            rms(1, TB + 1, C_FING, xf, "xf", HK)
            for tt in range(TB // 128):
                o, ok = yo[tt % 2], "yo%d" % (tt % 2)
                for g in range(4):
                    pst, pk = b.ps(4 + (tt * 4 + g) % 4)
                    for j in range(4):
                        ch = g * 4 + j
                        b.tr(pst[:, j * 128:(j + 1) * 128], xf[:, ch, tt * 128:(tt + 1) * 128], ident, ["xf"] + CK, [pk])
                    b.evac(o[:, g * 512:(g + 1) * 512], pst, [pk], [ok])
                b.dma(y_d[t0 + tt * 128:t0 + (tt + 1) * 128, :], o, [ok], [], q="act")
    b.wcache = None


_NC_CACHE = {}


def kernel(**inputs):
    SEG = 2048
    T = 2 * SEG
    if "nc" not in _NC_CACHE:
        _NC_CACHE["nc"] = build_nc(SEG=SEG)[0]
    nc = _NC_CACHE["nc"]
    xp_ = np.asarray(inputs["x_prompt"], dtype=np.float32)
    xs_ = np.asarray(inputs["x_sample"], dtype=np.float32)
    pp_ = np.asarray(inputs["p_prompt"], dtype=np.float32)
    ps_ = np.asarray(inputs["p_sample"], dtype=np.float32)
    wts = {n: np.ascontiguousarray(np.asarray(inputs[n], dtype=np.float32)) for n, _ in WNAMES}
    in_maps = []
    for core in range(8):
        if core < 4:
            x = xp_[core]
            p = pp_[:, core]
            f = 1.0
        else:
            j = 2 * (core - 4)
            x = xs_[j:j + 2].reshape(T, D)
            p = ps_[:, j:j + 2].reshape(2, T, 256)
            f = 0.0
        m = {"x": np.ascontiguousarray(x), "p": np.ascontiguousarray(p),
             "flag": np.full((128, 1), f, np.float32)}
        m.update(wts)
        in_maps.append(m)
    res = run_bass_kernel_spmd(nc, in_maps, core_ids=list(range(8)))
    outs = [np.asarray(r["y"], dtype=np.float32) for r in res.results]
    y_prompt = np.stack(outs[0:4], axis=0)
    y_sample = np.concatenate([o.reshape(2, SEG, D) for o in outs[4:8]], axis=0)
    return (y_prompt, y_sample)
```

```python
import math
from contextlib import ExitStack

import numpy as np
import concourse.bass as bass
import concourse.mybir as mybir
from concourse.bass_utils import run_bass_kernel_spmd

F32 = mybir.dt.float32
BF16 = mybir.dt.bfloat16
AF = mybir.ActivationFunctionType
ALU = mybir.AluOpType
AX = mybir.AxisListType

D = 2048
KD = 16
DPROJ = 16432
DFF = 6144
EPS = 1e-6
NEG = -30000.0

COMPUTE = ("pe", "act", "dve", "pool")
QUEUES = ("pe", "act", "dve", "pool", "sp")
NDSEM = 8


class Op:
    __slots__ = ("eng", "fn", "dma", "deps", "sig", "sem", "sigval", "waited")

    def __init__(self, eng, fn, dma):
        self.eng = eng
        self.fn = fn
        self.dma = dma
        self.deps = []
        self.sig = False
        self.sem = None
        self.sigval = 0
        self.waited = False


class Sched:
    def __init__(self):
        self.q = {e: [] for e in QUEUES}
        self.last_w = {}
        self.readers = {}
        self.dma_hist = {e: [] for e in QUEUES}
        self.all_dmas = []
        self.bar = []
        self.bar_pending = {e: False for e in QUEUES}
        self.dmas_since_bar = []

    def barrier(self):
        deps = [self.q[e][-1] for e in COMPUTE if self.q[e] and self.q[e][-1].fn is not None]
        deps += self.dmas_since_bar
        self.bar = deps
        self.dmas_since_bar = []
        for e in QUEUES:
            self.bar_pending[e] = True

    def add(self, eng, fn, reads=(), writes=(), dma=False):
        op = Op(eng, fn, dma)
        deps = {}

        def consider(d, kind):
            if d is None:
                return
            if (not d.dma) and d.eng == eng:
                if (not dma) and eng == "pe":
                    return
            deps[id(d)] = d

        for k in reads:
            consider(self.last_w.get(k), "raw")
            if k.startswith("ps"):
                for r in self.readers.get(k, ()):
                    if r.eng != eng:
                        deps[id(r)] = r
        for k in writes:
            consider(self.last_w.get(k), "waw")
            for r in self.readers.get(k, ()):
                consider(r, "war")
        if self.bar_pending[eng]:
            self.bar_pending[eng] = False
            for d in self.bar:
                if d.dma or d.eng != eng:
                    deps[id(d)] = d
        if dma:
            hist = self.dma_hist[eng]
            if len(hist) >= NDSEM:
                d = hist[-NDSEM]
                deps[id(d)] = d
            hist.append(op)
            self.all_dmas.append(op)
            self.dmas_since_bar.append(op)
        op.deps = list(deps.values())
        for k in reads:
            self.readers.setdefault(k, []).append(op)
        for k in writes:
            self.last_w[k] = op
            self.readers[k] = []
        self.q[eng].append(op)
        return op

    def pe(self, fn, reads=(), writes=()):
        return self.add("pe", fn, reads, writes)

    def act(self, fn, reads=(), writes=()):
        return self.add("act", fn, reads, writes)

    def dve(self, fn, reads=(), writes=()):
        return self.add("dve", fn, reads, writes)

    def pool(self, fn, reads=(), writes=()):
        return self.add("pool", fn, reads, writes)

    def dma(self, q, fn, reads=(), writes=()):
        return self.add(q, fn, reads, writes, dma=True)

    def emit(self, nc):
        fin = Op("sp", None, False)
        fin.deps = list(self.all_dmas)
        self.q["sp"].append(fin)
        for e in QUEUES:
            for op in self.q[e]:
                for d in op.deps:
                    d.waited = True
        with ExitStack() as es:
            csem = {e: es.enter_context(nc.semaphore("s_" + e)) for e in COMPUTE}
            dsem = {e: [es.enter_context(nc.semaphore("d_%s_%d" % (e, i))) for i in range(NDSEM)]
                    for e in QUEUES}
            for e in QUEUES:
                cnt = 0
                dcnt = [0] * NDSEM
                di = 0
                for op in self.q[e]:
                    if op.fn is None:
                        continue
                    if op.dma:
                        s = di % NDSEM
                        di += 1
                        dcnt[s] += 16
                        op.sem = dsem[e][s]
                        op.sigval = dcnt[s]
                        op.sig = True
                    elif op.waited:
                        cnt += 1
                        op.sem = csem[e]
                        op.sigval = cnt
                        op.sig = True
            block = es.enter_context(nc.Block())
            engs = {"pe": block.tensor, "act": block.scalar, "dve": block.vector,
                    "pool": block.gpsimd, "sp": block.sync}
            self.stats = {}
            for e in QUEUES:
                ops = self.q[e]
                nw = [0]

                def body(eng, ops=ops, nw=nw):
                    known = {}
                    for op in ops:
                        need = {}
                        for d in op.deps:
                            key = d.sem.name
                            if known.get(key, 0) >= d.sigval:
                                continue
                            if key not in need or need[key][1] < d.sigval:
                                need[key] = (d.sem, d.sigval)
                        for key, (sem, val) in need.items():
                            eng.wait_ge(sem, val)
                            known[key] = val
                            nw[0] += 1
                        if op.fn is None:
                            continue
                        ins = op.fn(eng)
                        if op.sig:
                            ins.then_inc(op.sem, 16 if op.dma else 1)

                engs[e](body)
                self.stats[e] = (len(ops), nw[0])


ARENA_WORDS = 52224


class B:
    def __init__(self, SEG, debug=False):
        self.SEG = SEG
        self.T = 2 * SEG
        self.debug = debug
        self.nc = bass.Bass("TRN2", target_bir_lowering=False)
        self.S = Sched()
        self.uid = 0

    def dram_in(self, name, shape):
        return self.nc.dram_tensor(name, list(shape), F32, kind="ExternalInput").ap()

    def scratch(self, name, shape, dt):
        kind = "ExternalOutput" if self.debug else "Internal"
        return self.nc.dram_tensor(name, list(shape), dt, kind=kind).ap()

    def alloc(self, free_shape, dt, parts=128):
        n = 1
        for s in free_shape:
            n *= s
        words = n if dt == F32 else (n + 1) // 2
        words = (words + 7) // 8 * 8
        assert self.off + words <= ARENA_WORDS, ("arena overflow", self.off, words)
        ap = self.arena[0:parts, self.off:self.off + words]
        self.off += words
        if dt == BF16:
            ap = ap.bitcast(BF16)
        ap = ap[:, 0:n]
        if len(free_shape) == 2:
            ap = ap.rearrange("p (a b) -> p a b", a=free_shape[0])
        elif len(free_shape) == 3:
            ap = ap.rearrange("p (a b c) -> p a b c", a=free_shape[0], b=free_shape[1])
        return ap

    def key(self, base):
        self.uid += 1
        return "%s#%d" % (base, self.uid)

    def phase(self):
        self.S.barrier()
        self.off = self.persist_off

    def mm(self, out, lhsT, rhs, start, stop, R, W):
        self.S.pe(lambda e: e.matmul(out, lhsT=lhsT, rhs=rhs, start=start, stop=stop), R, W)

    def tr(self, out, in_, ident, R, W):
        self.S.pe(lambda e: e.transpose(out, in_, ident), R, W)

    def act(self, out, in_, func, R, W, bias=0.0, scale=1.0):
        self.S.act(lambda e: e.activation(out=out, in_=in_, func=func, bias=bias, scale=scale), R, W)

    def tt(self, out, a, b, op, R, W, eng="dve"):
        self.S.add(eng, lambda e: e.tensor_tensor(out=out, in0=a, in1=b, op=op), R, W)

    def ts(self, out, a, s1, s2, op0, op1, R, W, eng="dve"):
        if s2 is None:
            self.S.add(eng, lambda e: e.tensor_scalar(out=out, in0=a, scalar1=s1, scalar2=None, op0=op0), R, W)
        else:
            self.S.add(eng, lambda e: e.tensor_scalar(out=out, in0=a, scalar1=s1, scalar2=s2, op0=op0, op1=op1), R, W)

    def stt(self, out, a, s, b, op0, op1, R, W):
        self.S.dve(lambda e: e.scalar_tensor_tensor(out=out, in0=a, scalar=s, in1=b, op0=op0, op1=op1), R, W)

    def copy(self, out, in_, R, W, eng="dve"):
        if eng == "act":
            self.S.act(lambda e: e.copy(out=out, in_=in_), R, W)
        else:
            self.S.add(eng, lambda e: e.tensor_copy(out=out, in_=in_), R, W)

    def dma(self, out, in_, R, W, q="sp", slow=False):
        if slow:
            self.S.dma(q, lambda e: e.dma_start(out=out, in_=in_, allow_slow_non_contiguous=True), R, W)
        else:
            self.S.dma(q, lambda e: e.dma_start(out=out, in_=in_), R, W)

    def memset(self, ap, val, W, eng="pool"):
        self.S.add(eng, lambda e: e.memset(ap, val), (), W)

    def evac(self, out, in_, R, W):
        self.ev = getattr(self, "ev", 0) + 1
        self.copy(out, in_, R, W, eng=("act" if self.ev % 2 else "dve"))

    def wsetup(self):
        self.wst = [self.alloc([16, 128], F32) for _ in range(2)]
        self.wbf = [self.alloc([16, 128], BF16) for _ in range(3)]
        self.wi = 0

    def load_w(self, src, nk, ncol=128):
        i = self.wi
        self.wi += 1
        st = self.wst[i % 2]
        bf = self.wbf[i % 3]
        ks, kb = "wst%d" % (i % 2), "wbf%d" % (i % 3)
        wc = getattr(self, "wcache", None)
        if wc is not None:
            idx = wc["idx"]
            wc["idx"] += 1
            slot = wc["ap"][idx, :, 0:nk * ncol].rearrange("p (k n) -> p k n", n=ncol)
            if wc["mode"] == "use":
                self.dma(bf[:, 0:nk, 0:ncol], slot, [], [kb])
                return bf, kb
        self.dma(st[:, 0:nk, 0:ncol], src.rearrange("(k p) n -> p k n", p=128), [], [ks])
        self.copy(bf[:, 0:nk, 0:ncol], st[:, 0:nk, 0:ncol], [ks], [kb], eng="pool")
        if wc is not None:
            self.dma(slot, bf[:, 0:nk, 0:ncol], [kb], [], q="pool")
        return bf, kb

    def colblocks(self, n):
        return [(a, min(a + 512, n)) for a in range(0, n, 512)]

    def ps(self, i):
        return self.psum[i // 2][:, (i % 2) * 512:(i % 2) * 512 + 512], "ps%d" % i


WNAMES = [
    ("norm_mix_g", (2, 2048)), ("w_in", (2, 2048, 16432)), ("rglru_conv_w", (2, 4, 1024)),
    ("rglru_conv_b", (2, 1024)), ("rglru_gate_w", (2, 2, 2, 8, 128, 128)), ("rglru_gate_b", (2, 2, 2, 1024)),
    ("rglru_lambda", (2, 2, 1024)), ("mlstm_gate_b", (2, 16)), ("mlstm_norm_g", (2, 1024)),
    ("gdn_conv_w", (2, 4, 3072)), ("gdn_A_log", (2, 2, 8)), ("gdn_dt_bias", (2, 2, 8)),
    ("gdn_norm_g", (2, 1024)), ("w_branch", (2, 3, 1024, 2048)), ("w_out", (2, 2048, 2048)),
    ("norm_ffn_g", (2, 2048)), ("ffn_w_up", (2, 2048, 12288)), ("ffn_conv_w", (2, 3, 12288)),
    ("ffn_w_down", (2, 6144, 2048)), ("norm_ple_g", (2, 2048)), ("ple_w_gate", (2, 2048, 2048)),
    ("ple_w_proj", (2, 256, 2048)), ("norm_final_g", (2048,)),
]

C_MIXG, C_FFNG, C_PLEG, C_FING = 0, 16, 32, 48
C_ACW, C_ACB, C_AGB, C_ALAM = 64, 96, 104, 136
C_BNG, C_CCW, C_CNG, C_FCW = 152, 160, 256, 264
C_AC8 = 552
NCOLS = 568


def build_nc(SEG=2048, debug=False, nlayers=2, stop_after=None):
    b = B(SEG, debug)
    nc, S, T = b.nc, b.S, b.T
    NT = T // 128
    x_d = b.dram_in("x", (T, D))
    p_d = b.dram_in("p", (2, T, 256))
    flag_d = b.dram_in("flag", (128, 1))
    W = {n: b.dram_in(n, s) for n, s in WNAMES}
    y_d = nc.dram_tensor("y", [T, D], F32, kind="ExternalOutput").ap()
    hA = b.scratch("hA", (16, 128, T), F32)
    hM = b.scratch("hM", (16, 128, T), F32)
    Ain = b.scratch("Ain", (16, 128, T), BF16)
    Bqk = b.scratch("Bqk", (16, 128, T), BF16)
    Bo = b.scratch("Bo", (8, 128, T), BF16)
    Bv = b.scratch("Bv", (T, 1024), BF16)
    Bk = b.scratch("Bk", (T, 1024), BF16)
    Gt = b.scratch("Gt", (T, 48), F32)
    Cqkv = b.scratch("Cqkv", (24, 128, T), BF16)
    Cz = b.scratch("Cz", (8, 128, T), BF16)
    MG = b.scratch("MG", (48, 128, T), BF16)
    Y = b.scratch("Y", (24, 128, T), BF16)
    Wc3a = nc.dram_tensor("Wc3a", [64, 128, 2048], BF16, kind="Internal").ap()
    Wc3b = nc.dram_tensor("Wc3b", [208, 128, 2048], BF16, kind="Internal").ap()

    with ExitStack() as es:
        b.arena = es.enter_context(nc.sbuf_tensor("arena", [128, ARENA_WORDS], F32))
        b.psum = [es.enter_context(nc.psum_tensor("psum%d" % i, [128, 1024], F32)) for i in range(4)]
        b.off = 0
        ident = b.alloc([128], F32)
        ones_f = b.alloc([128], F32)
        ones_b = b.alloc([128], BF16)
        Uf = b.alloc([128], F32)
        Ub = b.alloc([128], F32)
        NGf = b.alloc([128], F32)
        NGb = b.alloc([128], F32)
        PSf = b.alloc([128], F32)
        PSb = b.alloc([128], F32)
        flag = b.alloc([1], F32)
        cols = b.alloc([NCOLS], F32)
        gbB = b.alloc([16], F32)
        dtb = b.alloc([16], F32)
        negA = b.alloc([16], F32)
        b.persist_off = b.off

        CK = []

        def aff(ap, src_val, cmp, fill, pattern, cm):
            k = "const%d" % len(CK)
            CK.append(k)
            b.memset(ap, src_val, [k])
            S.pool(lambda e: e.affine_select(out=ap, in_=ap, compare_op=cmp, fill=fill, base=0,
                                              pattern=pattern, channel_multiplier=cm), [k], [k])

        aff(ident, 0.0, ALU.not_equal, 1.0, [[-1, 128]], 1)
        b.memset(ones_f, 1.0, ["const_of"])
        b.memset(ones_b, 1.0, ["const_ob"])
        CK += ["const_of", "const_ob", "const_flag"]
        aff(Uf, 1.0, ALU.is_ge, 0.0, [[1, 128]], -1)
        aff(Ub, 1.0, ALU.is_ge, 0.0, [[-1, 128]], 1)
        aff(NGf, 0.0, ALU.is_ge, NEG, [[1, 128]], -1)
        aff(NGb, 0.0, ALU.is_ge, NEG, [[-1, 128]], 1)
        aff(PSf, 0.0, ALU.is_gt, -NEG, [[-1, 128]], 1)
        aff(PSb, 0.0, ALU.is_gt, -NEG, [[1, 128]], -1)
        b.dma(flag, flag_d, [], ["const_flag"])

        def load_cols(layer):
            b.phase()
            stage = b.alloc([128], F32)
            items = [
                (W["norm_mix_g"][layer].rearrange("(c p) -> c p", p=128), 16, C_MIXG),
                (W["norm_ffn_g"][layer].rearrange("(c p) -> c p", p=128), 16, C_FFNG),
                (W["norm_ple_g"][layer].rearrange("(c p) -> c p", p=128), 16, C_PLEG),
                (W["norm_final_g"].rearrange("(c p) -> c p", p=128), 16, C_FING),
                (W["rglru_conv_w"][layer].rearrange("k (n p) -> (k n) p", p=128), 32, C_ACW),
                (W["rglru_conv_b"][layer].rearrange("(n p) -> n p", p=128), 8, C_ACB),
                (W["rglru_gate_b"][layer].rearrange("e g (n p) -> (e g n) p", p=128), 32, C_AGB),
                (W["rglru_lambda"][layer].rearrange("e (n p) -> (e n) p", p=128), 16, C_ALAM),
                (W["mlstm_norm_g"][layer].rearrange("(n p) -> n p", p=128), 8, C_BNG),
                (W["gdn_conv_w"][layer].rearrange("k (c p) -> (k c) p", p=128), 96, C_CCW),
                (W["gdn_norm_g"][layer].rearrange("(n p) -> n p", p=128), 8, C_CNG),
            ]
            fcw = W["ffn_conv_w"][layer].rearrange("k (c p) -> (k c) p", p=128)
            for q in range(3):
                items.append((fcw[q * 96:(q + 1) * 96, :], 96, C_FCW + q * 96))
            for i, (src, R, c0) in enumerate(items):
                pst, pk = b.ps(i % 2)
                b.dma(stage[0:R, :], src, [], ["lc_stage"])
                b.tr(pst[:, 0:R], stage[0:R, :], ident[0:R, 0:R], ["lc_stage"] + CK, [pk])
                b.copy(cols[:, c0:c0 + R], pst[:, 0:R], [pk], ["cols"], eng="act")
            tmp = b.alloc([16], F32)
            b.act(tmp, cols[:, C_ALAM:C_ALAM + 16], AF.Exp, ["cols"], ["lc_tmp"], scale=-1.0)
            b.act(tmp, tmp, AF.Ln, ["lc_tmp"], ["lc_tmp"], bias=1.0)
            b.S.act(lambda e: e.mul(out=cols[:, C_AC8:C_AC8 + 16], in_=tmp, mul=-8.0), ["lc_tmp"], ["cols"])
            b.dma(gbB, W["mlstm_gate_b"][layer:layer + 1, :].partition_broadcast(128), [], ["bc"])
            b.dma(dtb, W["gdn_dt_bias"][layer:layer + 1].rearrange("a e h -> a (e h)").partition_broadcast(128), [], ["bc"])
            b.dma(negA, W["gdn_A_log"][layer:layer + 1].rearrange("a e h -> a (e h)").partition_broadcast(128), [], ["bc"])
            b.act(negA, negA, AF.Exp, ["bc"], ["bc"])
            b.S.act(lambda e: e.mul(out=negA, in_=negA, mul=-1.0), ["bc"], ["bc"])

        def phase0():
            b.phase()
            xin = [b.alloc([D], F32) for _ in range(2)]
            xo = [b.alloc([16, 128], F32) for _ in range(2)]
            for tt in range(NT):
                xi, xk = xin[tt % 2], "xin%d" % (tt % 2)
                o, ok = xo[tt % 2], "xo%d" % (tt % 2)
                b.dma(xi, x_d[tt * 128:(tt + 1) * 128, :], [], [xk])
                for g in range(4):
                    pst, pk = b.ps(g % 2 + 2 * (tt % 2))
                    for j in range(4):
                        c = g * 4 + j
                        b.tr(pst[:, j * 128:(j + 1) * 128], xi[:, c * 128:(c + 1) * 128], ident, [xk] + CK, [pk])
                    b.evac(o[:, g * 4:(g + 1) * 4, :], pst.rearrange("p (a b) -> p a b", a=4), [pk], [ok])
                b.dma(hA[:, :, tt * 128:(tt + 1) * 128].rearrange("c p t -> p c t"), o, [ok], ["hA"], q="act")

        b.cols, b.ident, b.ones_b, b.ones_f, b.flag, b.CK = cols, ident, ones_b, ones_f, flag, CK
        ctx = dict(locals())
        phase0()
        for layer in range(nlayers):
            load_cols(layer)
            phase1(b, ctx, layer)
            if stop_after == "p1":
                break
            mixerA(b, ctx, layer)
            if stop_after == "mA":
                break
            mixerB(b, ctx, layer)
            if stop_after == "mB":
                break
            mixerC(b, ctx, layer)
            if stop_after == "mC":
                break
            phase3a(b, ctx, layer)
            if stop_after == "p3a":
                break
            phase3b(b, ctx, layer, last=(layer == nlayers - 1))
        S.emit(nc)
    return nc, b


def phase1(b, c, layer):
    T = b.T
    TB = min(T, 2048)
    cols, CK = b.cols, b.CK
    W, hA = c["W"], c["hA"]
    groups = [(0, 16, c["Ain"]), (2048, 16, c["Bqk"]), (5120, 8, c["Bo"]), (6160, 24, c["Cqkv"]),
              (9232, 8, c["Cz"]), (10288, 48, c["MG"])]
    for tb in range(T // TB):
        b.phase()
        t0 = tb * TB
        b.wsetup()
        xn = b.alloc([16, TB], BF16)
        hc = [b.alloc([TB], F32) for _ in range(2)]
        sq = [b.alloc([TB], BF16) for _ in range(2)]
        rstd = b.alloc([TB], F32)
        ost = [b.alloc([TB], BF16) for _ in range(2)]
        cbs = b.colblocks(TB)
        for ch in range(16):
            h, hk = hc[ch % 2], "hc%d" % (ch % 2)
            q, qk = sq[ch % 2], "sq%d" % (ch % 2)
            b.dma(h, hA[ch, :, t0:t0 + TB], [], [hk])
            b.act(q, h, AF.Square, [hk], [qk])
            for i, (a, e_) in enumerate(cbs):
                pst, pk = b.ps(i)
                b.mm(pst[:, 0:e_ - a], b.ones_b, q[:, a:e_], ch == 0, ch == 15, [qk] + CK, [pk])
            b.ts(xn[:, ch, :], h, cols[:, C_MIXG + ch:C_MIXG + ch + 1], None, ALU.mult, None, [hk, "cols"], ["xn"])
        for i, (a, e_) in enumerate(cbs):
            pst, pk = b.ps(i)
            b.act(rstd[:, a:e_], pst[:, 0:e_ - a], AF.Sqrt, [pk], ["rstd"], bias=EPS, scale=1.0 / D)
        b.S.dve(lambda e: e.reciprocal(out=rstd, in_=rstd), ["rstd"], ["rstd"])
        for ch in range(16):
            b.tt(xn[:, ch, :], xn[:, ch, :], rstd, ALU.mult, ["xn", "rstd"], ["xn"])
        pi = 0
        oi = 0
        for col0, nch, dst in groups:
            for j in range(nch):
                wt, wk = b.load_w(W["w_in"][layer, :, col0 + j * 128:col0 + (j + 1) * 128], 16)
                o, ok = ost[oi % 2], "ost%d" % (oi % 2)
                oi += 1
                for (a, e_) in cbs:
                    pst, pk = b.ps(4 + pi % 4)
                    pi += 1
                    for k in range(16):
                        b.mm(pst[:, 0:e_ - a], wt[:, k, :], xn[:, k, a:e_], k == 0, k == 15, [wk, "xn"], [pk])
                    b.evac(o[:, a:e_], pst[:, 0:e_ - a], [pk], [ok])
                b.dma(dst[j, :, t0:t0 + TB], o, [ok], [], q="act")
        wtb = b.alloc([16, 512], BF16)
        otm = [b.alloc([512], BF16) for _ in range(2)]
        for (col0, dst) in [(4096, c["Bv"]), (3072, c["Bk"])]:
            for half in range(2):
                for qd in range(4):
                    cc = col0 + half * 512 + qd * 128
                    wt, wk = b.load_w(W["w_in"][layer, :, cc:cc + 128], 16)
                    b.copy(wtb[:, :, qd * 128:(qd + 1) * 128], wt[:, 0:16, :], [wk], ["wtb"], eng="pool")
                for tt in range(TB // 128):
                    pst, pk = b.ps(4 + pi % 4)
                    pi += 1
                    for k in range(16):
                        b.mm(pst, xn[:, k, tt * 128:(tt + 1) * 128], wtb[:, k, :], k == 0, k == 15, ["xn", "wtb"], [pk])
                    o, ok = otm[tt % 2], "otm%d" % (tt % 2)
                    b.evac(o, pst, [pk], [ok])
                    b.dma(dst[t0 + tt * 128:t0 + (tt + 1) * 128, half * 512:(half + 1) * 512], o, [ok], [], q="act")
        gst = b.alloc([16, 48], F32)
        gbf = b.alloc([16, 48], BF16)
        og = [b.alloc([48], F32) for _ in range(2)]
        b.dma(gst[:, :, 0:16], W["w_in"][layer, :, 6144:6160].rearrange("(k p) n -> p k n", p=128), [], ["gst"])
        b.dma(gst[:, :, 16:48], W["w_in"][layer, :, 10256:10288].rearrange("(k p) n -> p k n", p=128), [], ["gst"])
        b.copy(gbf, gst, ["gst"], ["gbf"], eng="pool")
        for tt in range(TB // 128):
            pst, pk = b.ps(4 + pi % 4)
            pi += 1
            for k in range(16):
                b.mm(pst[:, 0:48], xn[:, k, tt * 128:(tt + 1) * 128], gbf[:, k, :], k == 0, k == 15, ["xn", "gbf"], [pk])
            o, ok = og[tt % 2], "og%d" % (tt % 2)
            b.evac(o, pst[:, 0:48], [pk], [ok])
            b.dma(c["Gt"][t0 + tt * 128:t0 + (tt + 1) * 128, :], o, [ok], [], q="act")


def mixerA(b, c, layer):
    T, SEG = b.T, b.SEG
    cols, CK, flag, S = b.cols, b.CK, b.flag, b.S
    W, Ain, Y = c["W"], c["Ain"], c["Y"]
    b.phase()
    b.wsetup()
    PW = SEG + 3
    xp = b.alloc([2 * PW], BF16)
    ga = b.alloc([T], BF16)
    xcb = b.alloc([T], BF16)
    yb = b.alloc([T], BF16)
    xc = b.alloc([T], F32)
    rf = b.alloc([T], F32)
    uf = b.alloc([T], F32)
    sf = b.alloc([T], F32)
    hf = b.alloc([T], F32)
    hb = b.alloc([T], F32)
    carry = b.alloc([2], F32)
    cbs = b.colblocks(T)
    for n in range(8):
        for s in range(2):
            b.dma(xp[:, s * PW + 2:s * PW + 2 + SEG], Ain[n, :, s * SEG:(s + 1) * SEG], [], ["xp"])
        b.dma(ga, Ain[8 + n, :, :], [], ["ga"])
        b.memset(xp[:, 0:2], 0.0, ["xp"])
        b.memset(xp[:, 2 * PW - 1:2 * PW], 0.0, ["xp"])
        b.ts(xp[:, PW - 1:PW], xp[:, PW + 2:PW + 3], flag, None, ALU.mult, None, ["xp"] + CK, ["xp"])
        b.ts(xp[:, PW:PW + 2], xp[:, SEG:SEG + 2], flag, None, ALU.mult, None, ["xp"] + CK, ["xp"])
        for s in range(2):
            o = s * PW
            dst = xc[:, s * SEG:(s + 1) * SEG]
            b.ts(dst, xp[:, o:o + SEG], cols[:, C_ACW + n:C_ACW + n + 1], cols[:, C_ACB + n:C_ACB + n + 1],
                 ALU.mult, ALU.add, ["xp", "cols"], ["xc"])
            for k in range(1, 4):
                b.stt(dst, xp[:, o + k:o + k + SEG], cols[:, C_ACW + k * 8 + n:C_ACW + k * 8 + n + 1], dst,
                      ALU.mult, ALU.add, ["xp", "cols", "xc"], ["xc"])
        b.copy(xcb, xc, ["xc"], ["xcb"], eng="act")
        for e in range(2):
            wr, wrk = b.load_w(W["rglru_gate_w"][layer, e, 0, n], 1)
            wi_, wik = b.load_w(W["rglru_gate_w"][layer, e, 1, n], 1)
            for (wt, wk, dst, dk, g) in ((wr, wrk, rf, "rf", 0), (wi_, wik, uf, "uf", 1)):
                bias = cols[:, C_AGB + (e * 2 + g) * 8 + n:C_AGB + (e * 2 + g) * 8 + n + 1]
                for ci, (a, e_) in enumerate(cbs):
                    pst, pk = b.ps(ci % 8)
                    b.mm(pst[:, 0:e_ - a], wt[:, 0, :], xcb[:, a:e_], True, True, [wk, "xcb"], [pk])
                    b.act(dst[:, a:e_], pst[:, 0:e_ - a], AF.Sigmoid, [pk, "cols"], [dk], bias=bias)
            c8 = cols[:, C_AC8 + e * 8 + n:C_AC8 + e * 8 + n + 1]
            b.act(rf, rf, AF.Exp, ["rf", "cols"], ["rf"], scale=c8)
            b.act(sf, rf, AF.Square, ["rf"], ["sf"])
            b.act(sf, sf, AF.Sqrt, ["sf"], ["sf"], bias=1.0, scale=-1.0)
            b.tt(uf, uf, xc, ALU.mult, ["uf", "xc"], ["uf"])
            b.tt(uf, uf, sf, ALU.mult, ["uf", "sf"], ["uf"])
            if e == 0:
                S.dve(lambda en: en.tensor_tensor_scan(out=hf[:, 0:SEG], data0=rf[:, 0:SEG], data1=uf[:, 0:SEG],
                                                       initial=0.0, op0=ALU.mult, op1=ALU.add), ["rf", "uf"], ["hf"])
                b.tt(carry[:, 0:1], hf[:, SEG - 1:SEG], flag, ALU.mult, ["hf"] + CK, ["carry0"])
                S.dve(lambda en: en.tensor_tensor_scan(out=hf[:, SEG:T], data0=rf[:, SEG:T], data1=uf[:, SEG:T],
                                                       initial=carry[:, 0:1], op0=ALU.mult, op1=ALU.add),
                      ["rf", "uf", "carry0"], ["hf"])
            else:
                S.dve(lambda en: en.tensor_tensor_scan(out=hb[:, SEG:T][:, ::-1], data0=rf[:, SEG:T][:, ::-1],
                                                       data1=uf[:, SEG:T][:, ::-1], initial=0.0,
                                                       op0=ALU.mult, op1=ALU.add), ["rf", "uf"], ["hb"])
                b.tt(carry[:, 1:2], hb[:, SEG:SEG + 1], flag, ALU.mult, ["hb"] + CK, ["carry1"])
                S.dve(lambda en: en.tensor_tensor_scan(out=hb[:, 0:SEG][:, ::-1], data0=rf[:, 0:SEG][:, ::-1],
                                                       data1=uf[:, 0:SEG][:, ::-1], initial=carry[:, 1:2],
                                                       op0=ALU.mult, op1=ALU.add), ["rf", "uf", "carry1"], ["hb"])
        b.act(sf, ga, AF.Gelu_apprx_tanh, ["ga"], ["sf"])
        b.tt(hf, hf, hb, ALU.add, ["hf", "hb"], ["hf"])
        b.tt(yb, hf, sf, ALU.mult, ["hf", "sf"], ["yb"])
        b.dma(Y[n, :, :], yb, ["yb"], [], q="act")


def bc_mid(ap2, n):
    a = ap2.ap
    return bass.AP(ap2.tensor, ap2.offset, [list(a[0]), [0, n]] + [list(x) for x in a[1:]])


def bc_l3(ap3, n):
    a = ap3.ap
    return bass.AP(ap3.tensor, ap3.offset, [list(a[0]), list(a[1]), [0, n]])


def bc_last(ap1, n):
    a = ap1.ap
    return bass.AP(ap1.tensor, ap1.offset, [list(a[0]), [0, n]])


def mixerB(b, c, layer):
    T, SEG = b.T, b.SEG
    NCH, NCS = T // 128, SEG // 128
    cols, CK, flag, S, ident = b.cols, b.CK, b.flag, b.S, b.ident
    Uf, Ub, NGf, NGb, gbB = c["Uf"], c["Ub"], c["NGf"], c["NGb"], c["gbB"]
    Bqk, Bv, Bk, Bo, Gt, Y = c["Bqk"], c["Bv"], c["Bk"], c["Bo"], c["Gt"], c["Y"]
    b.phase()
    G = b.alloc([NCH, 16], F32)
    lf = b.alloc([NCH, 8], F32)
    sc = b.alloc([NCH, 8], F32)
    dec = b.alloc([NCH, 8], F32)
    b.dma(G, Gt[:, 0:16].rearrange("(c p) n -> p c n", p=128), [], ["G"])
    b.tt(G, G, bc_mid(gbB, NCH), ALU.add, ["G", "bc"], ["G"])
    b.act(lf[:, :, 0:4], G[:, :, 4:8], AF.Exp, ["G"], ["lf"], scale=-1.0)
    b.act(lf[:, :, 4:8], G[:, :, 12:16], AF.Exp, ["G"], ["lf"], scale=-1.0)
    b.act(lf, lf, AF.Ln, ["lf"], ["lf"], bias=1.0)
    S.act(lambda e: e.mul(out=lf, in_=lf, mul=-1.0), ["lf"], ["lf"])
    p0, k0 = b.ps(0)
    p1, k1 = b.ps(1)
    p2, k2 = b.ps(2)
    v3 = lambda p, n, w: p[:, 0:n * w].rearrange("p (a b) -> p a b", b=w)
    b.mm(v3(p0, NCH, 4), Uf, lf[:, :, 0:4], True, True, ["lf"] + CK, [k0])
    b.mm(v3(p1, NCH, 4), Ub, lf[:, :, 4:8], True, True, ["lf"] + CK, [k1])
    b.tt(sc[:, :, 0:4], G[:, :, 0:4], v3(p0, NCH, 4), ALU.subtract, ["G", k0], ["sc"])
    b.tt(sc[:, :, 4:8], G[:, :, 8:12], v3(p1, NCH, 4), ALU.subtract, ["G", k1], ["sc"])
    b.mm(v3(p2, NCH, 8), b.ones_f, lf, True, True, ["lf"] + CK, [k2])
    b.act(dec, v3(p2, NCH, 8), AF.Exp, [k2], ["dec"])

    qT = b.alloc([2, T], BF16)
    kT = b.alloc([2, T], BF16)
    ogT = b.alloc([2, T], BF16)
    yB = b.alloc([2, T], BF16)
    vaug = b.alloc([NCH, 260], BF16)
    ktm = b.alloc([NCH, 256], BF16)
    hsum = b.alloc([NCH, 256], F32)
    Cm2 = [b.alloc([2, 260], F32) for _ in range(2)]
    Cb2 = [b.alloc([2, 260], BF16) for _ in range(2)]
    Wrow2 = [b.alloc([128], F32) for _ in range(2)]
    DT2 = [b.alloc([128], F32) for _ in range(2)]
    ST2 = [b.alloc([128], BF16) for _ in range(2)]
    qw2 = [b.alloc([2, 128], BF16) for _ in range(2)]
    kw2 = [b.alloc([256], BF16) for _ in range(2)]
    rden2 = [b.alloc([1], F32) for _ in range(2)]
    sqc = b.alloc([256], F32)
    hn = b.alloc([256], F32)
    ssq = b.alloc([1], F32)
    sgo = b.alloc([2, 128], F32)
    for hd in range(4):
        b.dma(qT, Bqk[2 * hd:2 * hd + 2, :, :].rearrange("c p t -> p c t"), [], ["qT"])
        b.dma(kT, Bqk[8 + 2 * hd:8 + 2 * hd + 2, :, :].rearrange("c p t -> p c t"), [], ["kT"])
        b.dma(ogT, Bo[2 * hd:2 * hd + 2, :, :].rearrange("c p t -> p c t"), [], ["ogT"])
        b.dma(vaug[:, :, 0:256], Bv[:, hd * 256:(hd + 1) * 256].rearrange("(c p) e -> p c e", p=128), [], ["vaug"])
        b.dma(ktm, Bk[:, hd * 256:(hd + 1) * 256].rearrange("(c p) e -> p c e", p=128), [], ["ktm"])
        b.memset(vaug[:, :, 256:257], 1.0, ["vaug"])
        b.ts(qT, qT, 1.0 / 16.0, None, ALU.mult, None, ["qT"], ["qT"])
        b.memset(hsum, 0.0, ["hsum"])
        for e in range(2):
            b.memset(Cm2[e], 0.0, ["Cm%d" % e])
            b.memset(Cb2[e], 0.0, ["Cb%d" % e])
        for idx in range(NCH):
            for e in range(2):
                U, NG = (Uf, NGf) if e == 0 else (Ub, NGb)
                ch = idx if e == 0 else NCH - 1 - idx
                Cm, Cb, Wrow, DT, ST, qw, kw, rden = Cm2[e], Cb2[e], Wrow2[e], DT2[e], ST2[e], qw2[e], kw2[e], rden2[e]
                kCm, kCb, kW, kDT, kST, kqw, kkw, krd = ("%s%d" % (n_, e) for n_ in ("Cm", "Cb", "Wrow", "DT", "ST", "qw", "kw", "rden"))
                if idx == NCS:
                    b.ts(Cm, Cm, flag, None, ALU.mult, None, [kCm] + CK, [kCm])
                    b.copy(Cb, Cm, [kCm], [kCb], eng="act")
                col = e * 4 + hd
                cs = slice(ch * 128, (ch + 1) * 128)
                lfb = bc_last(lf[:, ch, col:col + 1], 128)
                pB, pBk = b.ps(0)
                pD, pDk = b.ps(1)
                pS, pSk = b.ps(2)
                pO, pOk = b.ps(3 + 2 * e)
                pC, pCk = b.ps(4 + 2 * e)
                pN, pNk = b.ps(7)
                b.mm(pB[:, 0:128], lfb, U, True, True, ["lf"] + CK, [pBk])
                b.act(Wrow, pB[:, 0:128], AF.Exp, [pBk], [kW])
                b.mm(pD[:, 0:128], lfb, U, True, False, ["lf"] + CK, [pDk])
                b.mm(pD[:, 0:128], ident, NG, False, True, CK, [pDk])
                b.act(DT, pD[:, 0:128], AF.Exp, [pDk, "sc"], [kDT], bias=sc[:, ch, col:col + 1])
                b.mm(pS[:, 0:128], kT[:, 0, cs], qT[:, 0, cs], True, False, ["kT", "qT"], [pSk])
                b.mm(pS[:, 0:128], kT[:, 1, cs], qT[:, 1, cs], False, True, ["kT", "qT"], [pSk])
                b.tt(ST, pS[:, 0:128], DT, ALU.mult, [pSk, kDT], [kST])
                for dc in range(2):
                    b.tt(qw[:, dc, :], qT[:, dc, cs], Wrow, ALU.mult, ["qT", kW], [kqw], eng="pool")
                b.mm(pO[:, 0:257], ST, vaug[:, ch, 0:257], True, False, [kST, "vaug"], [pOk])
                b.mm(pO[:, 0:257], qw[:, 0, :], Cb[:, 0, 0:257], False, False, [kqw, kCb], [pOk])
                b.mm(pO[:, 0:257], qw[:, 1, :], Cb[:, 1, 0:257], False, True, [kqw, kCb], [pOk])
                b.act(rden, pO[:, 256:257], AF.Abs, [pOk], [krd])
                b.ts(rden, rden, 1.0, None, ALU.max, None, [krd], [krd])
                S.dve(lambda en, rden=rden: en.reciprocal(out=rden, in_=rden), [krd], [krd])
                b.stt(hsum[:, ch, :], pO[:, 0:256], rden, hsum[:, ch, :], ALU.mult, ALU.add, [pOk, krd, "hsum"], ["hsum"])
                wcol = DT[:, 127:128] if e == 0 else DT[:, 0:1]
                b.ts(kw, ktm[:, ch, :], wcol, None, ALU.mult, None, ["ktm", kDT], [kkw], eng="pool")
                for dc in range(2):
                    b.mm(pC[:, dc * 256:(dc + 1) * 256], kw[:, dc * 128:(dc + 1) * 128], vaug[:, ch, 0:256], True, True,
                         [kkw, "vaug"], [pCk])
                    b.mm(pN[:, 2 * e + dc:2 * e + dc + 1], kw[:, dc * 128:(dc + 1) * 128], vaug[:, ch, 256:257], True, True,
                         [kkw, "vaug"], [pNk])
                dcol = dec[:, ch, col:col + 1]
                b.stt(Cm[:, :, 0:256], Cm[:, :, 0:256], dcol, pC.rearrange("p (a b) -> p a b", a=2), ALU.mult, ALU.add,
                      [kCm, "dec", pCk], [kCm])
                b.stt(Cm[:, :, 256:257], Cm[:, :, 256:257], dcol, pN[:, 2 * e:2 * e + 2].rearrange("p (a b) -> p a b", b=1),
                      ALU.mult, ALU.add, [kCm, "dec", pNk], [kCm])
                b.copy(Cb[:, :, 0:257], Cm[:, :, 0:257], [kCm], [kCb], eng="act")
        for ch in range(NCH):
            cs = slice(ch * 128, (ch + 1) * 128)
            pT, pTk = b.ps(ch % 2)
            b.act(sqc, hsum[:, ch, :], AF.Square, ["hsum"], ["sqc"])
            S.dve(lambda en, ch=ch: en.reduce_sum(out=ssq, in_=sqc, axis=AX.X), ["sqc"], ["ssq"])
            b.act(ssq, ssq, AF.Sqrt, ["ssq"], ["ssq"], bias=EPS, scale=1.0 / 256.0)
            S.dve(lambda en: en.reciprocal(out=ssq, in_=ssq), ["ssq"], ["ssq"])
            b.ts(hn, hsum[:, ch, :], ssq, None, ALU.mult, None, ["hsum", "ssq"], ["hn"])
            for ec in range(2):
                b.tr(pT[:, ec * 128:(ec + 1) * 128], hn[:, ec * 128:(ec + 1) * 128], ident, ["hn"] + CK, [pTk])
            b.act(sgo, ogT[:, :, cs], AF.Sigmoid, ["ogT"], ["sgo"])
            for ec in range(2):
                gcol = cols[:, C_BNG + 2 * hd + ec:C_BNG + 2 * hd + ec + 1]
                b.stt(yB[:, ec, cs], pT[:, ec * 128:(ec + 1) * 128], gcol, sgo[:, ec, :], ALU.mult, ALU.mult,
                      [pTk, "cols", "sgo"], ["yB"])
        for ec in range(2):
            b.dma(Y[8 + 2 * hd + ec, :, :], yB[:, ec, :], ["yB"], [], q="act")


def mixerC(b, c, layer):
    T, SEG = b.T, b.SEG
    NC, NCS = T // 64, SEG // 64
    cols, CK, flag, S, ident = b.cols, b.CK, b.flag, b.S, b.ident
    Uf, Ub, NGf, NGb, PSf, PSb, dtb, negA = (c[k] for k in ("Uf", "Ub", "NGf", "NGb", "PSf", "PSb", "dtb", "negA"))
    Cqkv, Cz, Gt, Y = c["Cqkv"], c["Cz"], c["Gt"], c["Y"]
    b.phase()
    H = 64
    a64 = lambda shape, dt: b.alloc(shape, dt)[0:H]
    gp = a64([NC, 32], F32)
    g = a64([NC, 16], F32)
    be = a64([NC, 16], F32)
    Gc = a64([NC, 16], F32)
    nG = a64([NC, 16], F32)
    eg = a64([NC, 16], F32)
    ed = a64([NC, 16], F32)
    bk = a64([NC, 16], F32)
    gL = b.alloc([NC, 16], F32)
    b.dma(gp, Gt[:, 16:48].rearrange("(c p) n -> p c n", p=H), [], ["gpTt"])
    b.tt(g[:, :, 0:8], gp[:, :, 0:8], bc_mid(dtb[0:H, 0:8], NC), ALU.add, ["gpTt", "bc"], ["g"])
    b.tt(g[:, :, 8:16], gp[:, :, 16:24], bc_mid(dtb[0:H, 8:16], NC), ALU.add, ["gpTt", "bc"], ["g"])
    b.act(g, g, AF.Exp, ["g"], ["g"])
    b.act(g, g, AF.Ln, ["g"], ["g"], bias=1.0)
    b.tt(g, g, bc_mid(negA[0:H, :], NC), ALU.mult, ["g", "bc"], ["g"])
    b.act(be[:, :, 0:8], gp[:, :, 8:16], AF.Sigmoid, ["gpTt"], ["be"])
    b.act(be[:, :, 8:16], gp[:, :, 24:32], AF.Sigmoid, ["gpTt"], ["be"])
    v3 = lambda p, parts, n, w: p[0:parts, 0:n * w].rearrange("p (a b) -> p a b", b=w)
    for e in range(2):
        U = Uf if e == 0 else Ub
        pa, ka = b.ps(0 + e)
        pb_, kb_ = b.ps(2 + e)
        pc, kc = b.ps(4 + e)
        gs = g[:, :, e * 8:(e + 1) * 8]
        b.mm(v3(pa, H, NC, 8), U[0:H, 0:H], gs, True, True, ["g"] + CK, [ka])
        b.copy(Gc[:, :, e * 8:(e + 1) * 8], v3(pa, H, NC, 8), [ka], ["Gc"], eng="act")
        b.mm(v3(pb_, H, NC, 8), b.ones_f[0:H, 0:H], gs, True, True, ["g"] + CK, [kb_])
        b.tt(ed[:, :, e * 8:(e + 1) * 8], v3(pb_, H, NC, 8), Gc[:, :, e * 8:(e + 1) * 8], ALU.subtract, [kb_, "Gc"], ["ed"])
        b.mm(v3(pc, 128, NC, 8), b.ones_f[0:H, :], gs, True, True, ["g"] + CK, [kc])
        b.act(gL[:, :, e * 8:(e + 1) * 8], v3(pc, 128, NC, 8), AF.Exp, [kc], ["gL"])
    b.act(ed, ed, AF.Exp, ["ed"], ["ed"])
    b.act(eg, Gc, AF.Exp, ["Gc"], ["eg"])
    b.tt(bk, be, eg, ALU.mult, ["be", "eg"], ["bk"])
    S.act(lambda en: en.mul(out=nG, in_=Gc, mul=-1.0), ["Gc"], ["nG"])

    PW = SEG + 3
    xp = b.alloc([2 * PW], BF16)
    cvs = b.alloc([T], F32)
    sqb = b.alloc([T], BF16)
    yC = sqb
    qT = b.alloc([T], BF16)
    kT = b.alloc([T], BF16)
    zs = xp[:, 0:T]
    rn = b.alloc([512], F32)
    ktm = a64([NC, 128], BF16)
    vtm = a64([NC, 128], BF16)
    osum = a64([NC, 128], F32)
    Ttb2 = [a64([NC, 64], BF16) for _ in range(2)]
    qkb2 = [a64([NC, 64], BF16) for _ in range(2)]
    Sm2 = [b.alloc([128], F32) for _ in range(2)]
    Sb2 = [b.alloc([128], BF16) for _ in range(2)]
    G = min(8, NC)
    gsrc = cvs if T >= 8 * G * 64 else b.alloc([8 * G * 64], F32)
    gbuf = lambda i: gsrc[0:H, i * G * 64:(i + 1) * G * 64].rearrange("p (a b) -> p a b", b=64)
    DTg, DAg = gbuf(0), gbuf(1)
    Ng = [gbuf(2), gbuf(3)]
    Mg = [gbuf(4), gbuf(5)]
    Pg = [gbuf(6), gbuf(7)]
    sqg = gsrc[0:H, 0:G * 128].rearrange("p (a b) -> p a b", b=128)
    ssg = a64([G], F32)
    vb2 = [a64([128], F32) for _ in range(2)]
    negr2 = [a64([128], BF16) for _ in range(2)]
    vnew2 = [a64([128], BF16) for _ in range(2)]
    kdec2 = [a64([128], BF16) for _ in range(2)]
    o12 = [a64([128], F32) for _ in range(2)]
    o22 = [a64([128], F32) for _ in range(2)]
    sqc = a64([128], F32)
    on = a64([128], F32)
    ssq = a64([1], F32)
    I64 = ident[0:H, 0:H]
    cbs = b.colblocks(T)

    def conv_silu(ci):
        for s in range(2):
            b.dma(xp[:, s * PW + 2:s * PW + 2 + SEG], Cqkv[ci, :, s * SEG:(s + 1) * SEG], [], ["xp"])
        b.memset(xp[:, 0:2], 0.0, ["xp"])
        b.memset(xp[:, 2 * PW - 1:2 * PW], 0.0, ["xp"])
        b.ts(xp[:, PW - 1:PW], xp[:, PW + 2:PW + 3], flag, None, ALU.mult, None, ["xp"] + CK, ["xp"])
        b.ts(xp[:, PW:PW + 2], xp[:, SEG:SEG + 2], flag, None, ALU.mult, None, ["xp"] + CK, ["xp"])
        for s in range(2):
            o = s * PW
            dst = cvs[:, s * SEG:(s + 1) * SEG]
            wc = lambda k: cols[:, C_CCW + k * 24 + ci:C_CCW + k * 24 + ci + 1]
            b.ts(dst, xp[:, o:o + SEG], wc(0), None, ALU.mult, None, ["xp", "cols"], ["cvs"])
            for k in range(1, 4):
                b.stt(dst, xp[:, o + k:o + k + SEG], wc(k), dst, ALU.mult, ALU.add, ["xp", "cols", "cvs"], ["cvs"])
        b.act(cvs, cvs, AF.Silu, ["cvs"], ["cvs"])

    def l2norm(dstT, dkey, scale):
        b.act(sqb, cvs, AF.Square, ["cvs"], ["sqb"])
        for ci_, (a, e_) in enumerate(cbs):
            pst, pk = b.ps(ci_ % 2)
            b.mm(pst[:, 0:e_ - a], b.ones_b, sqb[:, a:e_], True, True, ["sqb"] + CK, [pk])
            b.act(rn[:, 0:e_ - a], pst[:, 0:e_ - a], AF.Sqrt, [pk], ["rn"], bias=EPS)
            S.dve(lambda en, w=e_ - a: en.reciprocal(out=rn[:, 0:w], in_=rn[:, 0:w]), ["rn"], ["rn"])
            b.stt(cvs[:, a:e_], cvs[:, a:e_], scale, rn[:, 0:e_ - a], ALU.mult, ALU.mult, ["cvs", "rn"], ["cvs"])
        b.copy(dstT, cvs, ["cvs"], [dkey], eng="act")

    def to_tm(dst, dkey):
        for ch in range(NC):
            pst, pk = b.ps(2 + (ch // 4) % 2)
            sub = pst[0:H, (ch % 4) * 128:(ch % 4 + 1) * 128]
            b.tr(sub, cvs[:, ch * 64:(ch + 1) * 64], ident, ["cvs"] + CK, [pk])
            if ch % 4 == 3 or ch == NC - 1:
                n = ch % 4 + 1
                c0 = ch - n + 1
                b.evac(dst[:, c0:c0 + n, :], pst[0:H, 0:n * 128].rearrange("p (a b) -> p a b", b=128), [pk], [dkey])

    for hd in range(8):
        S.barrier()
        conv_silu(hd)
        l2norm(qT, "qT", 128.0 ** -0.5)
        conv_silu(8 + hd)
        l2norm(kT, "kT", 1.0)
        to_tm(ktm, "ktm")
        conv_silu(16 + hd)
        to_tm(vtm, "vtm")
        b.dma(zs, Cz[hd, :, :], [], ["xp"])
        b.act(zs, zs, AF.Silu, ["xp"], ["xp"])
        S.barrier()
        def pre_gen(e, c0):
            col = e * 8 + hd
            U, NGm, PSm = (Uf, NGf, PSf) if e == 0 else (Ub, NGb, PSb)
            U64, NG64, PS64 = U[0:H, 0:H], NGm[0:H, 0:H], PSm[0:H, 0:H]
            gk = "%d_%d" % (e, c0 // G)
            P = [b.ps(i % 4) for i in range(8)]
            V = lambda i: P[i][0][0:H, 0:G * 64].rearrange("p (a b) -> p a b", b=64)
            slot = lambda i, j: P[i][0][0:H, j * 64:(j + 1) * 64]
            PK = lambda i: P[i][1]
            for j in range(G):
                gb = bc_last(g[:, c0 + j, col:col + 1], H)
                b.mm(slot(0, j), gb, U64, True, False, ["g"] + CK, [PK(0)])
                b.mm(slot(0, j), I64, NG64, False, True, CK, [PK(0)])
                b.mm(slot(1, j), gb, U64, True, False, ["g"] + CK, [PK(1)])
                b.mm(slot(1, j), I64, PS64, False, True, CK, [PK(1)])
                yield
            b.tt(DTg, V(0), bc_l3(nG[:, c0:c0 + G, col:col + 1], 64), ALU.add, [PK(0), "nG"], ["DTg"])
            b.act(DTg, DTg, AF.Exp, ["DTg"], ["DTg"])
            yield
            b.tt(DAg, bc_l3(Gc[:, c0:c0 + G, col:col + 1], 64), V(1), ALU.subtract, [PK(1), "Gc"], ["DAg"])
            b.act(DAg, DAg, AF.Exp, ["DAg"], ["DAg"])
            b.tt(DAg, DAg, bc_l3(be[:, c0:c0 + G, col:col + 1], 64), ALU.mult, ["DAg", "be"], ["DAg"], eng="pool")
            yield
            for j in range(G):
                cs = slice((c0 + j) * 64, (c0 + j + 1) * 64)
                b.mm(slot(2, j), kT[:, cs], kT[:, cs], True, True, ["kT"], [PK(2)])
                b.mm(slot(3, j), kT[:, cs], qT[:, cs], True, True, ["kT", "qT"], [PK(3)])
                yield
            b.tt(Ng[0], V(2), DAg, ALU.mult, [PK(2), "DAg"], ["N0"])
            b.tt(qkb2[e][:, c0:c0 + G, :], V(3), DTg, ALU.mult, [PK(3), "DTg"], ["qkb" + gk])
            yield
            for j in range(G):
                b.mm(slot(4, j), Ng[0][:, j, :], I64, True, True, ["N0"] + CK, [PK(4)])
            yield
            b.copy(Mg[0], V(4), [PK(4)], ["M0"], eng="act")
            b.tt(Pg[0], bc_mid(I64, G), V(4), ALU.subtract, [PK(4)] + CK, ["P0"])
            yield
            for lv in range(1, 6):
                pi_, ci_ = (lv - 1) % 2, lv % 2
                for j in range(G):
                    b.mm(slot(5, j), Mg[pi_][:, j, :], Ng[pi_][:, j, :], True, True, ["M%d" % pi_, "N%d" % pi_], [PK(5)])
                yield
                b.copy(Ng[ci_], V(5), [PK(5)], ["N%d" % ci_], eng="act")
                if lv < 5:
                    for j in range(G):
                        b.mm(slot(6, j), Ng[pi_][:, j, :], Mg[pi_][:, j, :], True, True, ["M%d" % pi_, "N%d" % pi_], [PK(6)])
                    yield
                    b.copy(Mg[ci_], V(6), [PK(6)], ["M%d" % ci_], eng="dve")
                for j in range(G):
                    b.mm(slot(7, j), Ng[ci_][:, j, :], Pg[pi_][:, j, :], True, True, ["N%d" % ci_, "P%d" % pi_], [PK(7)])
                yield
                if lv < 5:
                    b.tt(Pg[ci_], Pg[pi_], V(7), ALU.add, ["P%d" % pi_, PK(7)], ["P%d" % ci_])
                else:
                    b.tt(Ttb2[e][:, c0:c0 + G, :], Pg[pi_], V(7), ALU.add, ["P%d" % pi_, PK(7)], ["Ttb" + gk])
                yield

        def drain(gen):
            for _ in gen:
                pass

        def chain_step(e, ch, idx):
            col = e * 8 + hd
            gk = "%d_%d" % (e, ch // G)
            Sm, Sb, vb, negr, vnew, kdec, o1, o2 = Sm2[e], Sb2[e], vb2[e], negr2[e], vnew2[e], kdec2[e], o12[e], o22[e]
            kS, kSb, kvb, knr, kvn, kkd, ko1, ko2 = ("%s%d" % (n_, e) for n_ in ("Sm", "Sb", "vb", "negr", "vnew", "kdec", "o1", "o2"))
            if idx == NCS:
                b.ts(Sm, Sm, flag, None, ALU.mult, None, [kS] + CK, [kS])
                b.copy(Sb, Sm, [kS], [kSb], eng="act")
            cs = slice(ch * 64, (ch + 1) * 64)
            pX, pXk = b.ps(4 + 2 * e)
            pD, pDk = b.ps(5 + 2 * e)
            b.mm(pX[0:H, 0:128], kT[:, cs], Sb, True, True, ["kT", kSb], [pXk])
            b.mm(pX[0:H, 128:256], qT[:, cs], Sb, True, True, ["qT", kSb], [pXk])
            b.ts(vb, vtm[:, ch, :], be[:, ch, col:col + 1], None, ALU.mult, None, ["vtm", "be"], [kvb], eng="pool")
            b.stt(negr, pX[0:H, 0:128], bk[:, ch, col:col + 1], vb, ALU.mult, ALU.subtract, [pXk, "bk", kvb], [knr])
            b.mm(pX[0:H, 256:384], Ttb2[e][:, ch, :], negr, True, True, ["Ttb" + gk, knr], [pXk])
            S.act(lambda en: en.mul(out=vnew, in_=pX[0:H, 256:384], mul=-1.0), [pXk], [kvn])
            b.mm(pX[0:H, 384:512], qkb2[e][:, ch, :], vnew, True, True, ["qkb" + gk, kvn], [pXk])
            b.ts(kdec, ktm[:, ch, :], ed[:, ch, col:col + 1], None, ALU.mult, None, ["ktm", "ed"], [kkd], eng="pool")
            b.mm(pD[:, 0:128], kdec, vnew, True, True, [kkd, kvn], [pDk])
            b.stt(Sm, Sm, gL[:, ch, col:col + 1], pD[:, 0:128], ALU.mult, ALU.add, [kS, "gL", pDk], [kS])
            b.copy(Sb, Sm, [kS], [kSb], eng="act")
            b.copy(o1, pX[0:H, 384:512], [pXk], [ko1], eng="act")
            b.stt(o2, pX[0:H, 128:256], eg[:, ch, col:col + 1], o1, ALU.mult, ALU.add, [pXk, "eg", ko1], [ko2])
            b.tt(osum[:, ch, :], osum[:, ch, :], o2, ALU.add, ["osum", ko2], ["osum"])

        NGR = NC // G
        b.memset(osum, 0.0, ["osum"])
        for e in range(2):
            b.memset(Sm2[e], 0.0, ["Sm%d" % e])
            b.memset(Sb2[e], 0.0, ["Sb%d" % e])
        drain(pre_gen(0, 0))
        drain(pre_gen(1, (NGR - 1) * G))
        UNITS = 18 * G + 30
        for gi in range(NGR):
            gens = []
            if gi + 1 < NGR:
                gens = [pre_gen(0, (gi + 1) * G), pre_gen(1, (NGR - 2 - gi) * G)]
            per = (2 * UNITS) // G + 1
            for k in range(G):
                idx = gi * G + k
                for e in range(2):
                    chain_step(e, idx if e == 0 else NC - 1 - idx, idx)
                n = per
                while gens and n > 0:
                    try:
                        next(gens[0])
                        n -= 1
                    except StopIteration:
                        gens.pop(0)
            for gen in gens:
                drain(gen)
        S.barrier()
        for gi, c0 in enumerate(range(0, NC, G)):
            pT, pTk = b.ps(gi % 2)
            og = osum[:, c0:c0 + G, :]
            b.act(sqg, og, AF.Square, ["osum"], ["sqg"])
            S.dve(lambda en: en.reduce_sum(out=ssg, in_=sqg, axis=AX.X), ["sqg"], ["ssg"])
            b.act(ssg, ssg, AF.Sqrt, ["ssg"], ["ssg"], bias=EPS, scale=1.0 / 128.0)
            S.dve(lambda en: en.reciprocal(out=ssg, in_=ssg), ["ssg"], ["ssg"])
            b.tt(sqg, og, bc_l3(ssg.rearrange("p (a b) -> p a b", b=1), 128), ALU.mult, ["osum", "ssg"], ["sqg"])
            for j in range(G):
                b.tr(pT[:, j * 64:(j + 1) * 64], sqg[:, j, :], I64, ["sqg"] + CK, [pTk])
            cs = slice(c0 * 64, (c0 + G) * 64)
            b.stt(yC[:, cs], pT[:, 0:G * 64], cols[:, C_CNG + hd:C_CNG + hd + 1], zs[:, cs], ALU.mult, ALU.mult,
                  [pTk, "cols", "xp"], ["sqb"])
        b.dma(Y[16 + hd, :, :], yC, ["sqb"], [], q="act")


def phase3a(b, c, layer):
    T = b.T
    TB = min(T, 512)
    W, Y, MG, hA, hM = c["W"], c["Y"], c["MG"], c["hA"], c["hM"]
    for tb in range(T // TB):
        b.phase()
        t0 = tb * TB
        b.wsetup()
        b.wcache = {"ap": c["Wc3a"], "idx": 0, "mode": "fill" if tb == 0 else "use"}
        Yt = b.alloc([24, TB], BF16)
        h = b.alloc([16, TB], F32)
        mrg = b.alloc([16, TB], BF16)
        mgt = [b.alloc([3, TB], BF16) for _ in range(2)]
        sg = [b.alloc([TB], F32) for _ in range(3)]
        acc = b.alloc([TB], F32)
        tmp = b.alloc([TB], F32)
        b.dma(Yt, Y[:, :, t0:t0 + TB].rearrange("c p t -> p c t"), [], ["Yt"])
        b.dma(h, hA[:, :, t0:t0 + TB].rearrange("c p t -> p c t"), [], ["h%d" % i for i in range(16)])
        for m in range(16):
            mg_ = mgt[m % 2]
            pks = []
            for g in range(3):
                mk = "mgt%d_%d" % (m % 2, g)
                b.dma(mg_[:, g, :], MG[g * 16 + m, :, t0:t0 + TB], [], [mk])
                wt, wk = b.load_w(W["w_branch"][layer, g, :, m * 128:(m + 1) * 128], 8)
                pst, pk = b.ps(g + 4 * (m % 2))
                pks.append((pst, pk))
                for k in range(8):
                    b.mm(pst[:, 0:TB], wt[:, k, :], Yt[:, g * 8 + k, :], k == 0, k == 7, [wk, "Yt"], [pk])
                b.act(sg[g], mg_[:, g, :], AF.Sigmoid, [mk], ["sg%d" % g])
            b.tt(acc, pks[0][0][:, 0:TB], sg[0], ALU.mult, [pks[0][1], "sg0"], ["acc"])
            b.tt(tmp, pks[1][0][:, 0:TB], sg[1], ALU.mult, [pks[1][1], "sg1"], ["tmp"])
            b.tt(acc, acc, tmp, ALU.add, ["acc", "tmp"], ["acc"])
            b.tt(tmp, pks[2][0][:, 0:TB], sg[2], ALU.mult, [pks[2][1], "sg2"], ["tmp"])
            b.tt(mrg[:, m, :], acc, tmp, ALU.add, ["acc", "tmp"], ["mrg"])
        for m in range(16):
            wt, wk = b.load_w(W["w_out"][layer, :, m * 128:(m + 1) * 128], 16)
            pst, pk = b.ps(3 + 4 * (m % 2))
            for k in range(16):
                b.mm(pst[:, 0:TB], wt[:, k, :], mrg[:, k, :], k == 0, k == 15, [wk, "mrg"], [pk])
            b.tt(h[:, m, :], h[:, m, :], pst[:, 0:TB], ALU.add, [pk, "h%d" % m], ["h%d" % m])
        b.dma(hM[:, :, t0:t0 + TB].rearrange("c p t -> p c t"), h, ["h%d" % i for i in range(16)], [], q="act")
    b.wcache = None


def phase3b(b, c, layer, last):
    T, SEG = b.T, b.SEG
    TB = min(SEG, 512)
    NE = TB + 2
    cols, CK, flag, S, ident = b.cols, b.CK, b.flag, b.S, b.ident
    W, hA, hM, p_d, y_d = c["W"], c["hA"], c["hM"], c["p_d"], c["y_d"]
    HK = ["h%d" % i for i in range(16)]
    for tb in range(T // TB):
        b.phase()
        t0 = tb * TB
        b.wsetup()
        b.wcache = {"ap": c["Wc3b"], "idx": 0, "mode": "fill" if tb == 0 else "use"}
        h = b.alloc([16, NE], F32)
        xn = b.alloc([16, NE], BF16)
        sq = [b.alloc([NE], BF16) for _ in range(2)]
        rstd = b.alloc([NE], F32)
        act_ = b.alloc([48, TB], BF16)
        cg = b.alloc([TB], F32)
        cu = b.alloc([TB], F32)
        pin = b.alloc([256], F32)
        pT = b.alloc([2, TB], BF16)
        b.dma(h[:, :, 1:TB + 1], hM[:, :, t0:t0 + TB].rearrange("c p t -> p c t"), [], HK)
        if t0 > 0:
            b.dma(h[:, :, 0:1], hM[:, :, t0 - 1:t0].rearrange("c p t -> p c t"), [], ["hl"], slow=True)
        else:
            b.memset(h[:, :, 0:1], 0.0, ["hl"])
        if t0 + TB < T:
            b.dma(h[:, :, TB + 1:TB + 2], hM[:, :, t0 + TB:t0 + TB + 1].rearrange("c p t -> p c t"), [], ["hr"], slow=True)
        else:
            b.memset(h[:, :, TB + 1:TB + 2], 0.0, ["hr"])
        HALL = HK + ["hl", "hr"]

        def rms(lo, hi, gcol0, out, okey, hkeys):
            n = hi - lo
            cb = [(a + lo, e_ + lo) for (a, e_) in b.colblocks(n)]
            for ch in range(16):
                q, qk = sq[ch % 2], "sq%d" % (ch % 2)
                b.act(q[:, lo:hi], h[:, ch, lo:hi], AF.Square, hkeys, [qk])
                for i, (a, e_) in enumerate(cb):
                    pst, pk = b.ps(i)
                    b.mm(pst[:, 0:e_ - a], b.ones_b, q[:, a:e_], ch == 0, ch == 15, [qk] + CK, [pk])
            for i, (a, e_) in enumerate(cb):
                pst, pk = b.ps(i)
                b.act(rstd[:, a:e_], pst[:, 0:e_ - a], AF.Sqrt, [pk], ["rstd"], bias=EPS, scale=1.0 / D)
            S.dve(lambda en: en.reciprocal(out=rstd[:, lo:hi], in_=rstd[:, lo:hi]), ["rstd"], ["rstd"])
            for ch in range(16):
                b.stt(out[:, ch, :], h[:, ch, lo:hi], cols[:, gcol0 + ch:gcol0 + ch + 1], rstd[:, lo:hi],
                      ALU.mult, ALU.mult, hkeys + ["cols", "rstd"], [okey])

        rms(0, NE, C_FFNG, xn, "xn", HALL)
        if t0 == SEG:
            b.ts(xn[:, :, 0:1], xn[:, :, 0:1], flag, None, ALU.mult, None, ["xn"] + CK, ["xn"])
        if t0 + TB == SEG:
            b.ts(xn[:, :, TB + 1:TB + 2], xn[:, :, TB + 1:TB + 2], flag, None, ALU.mult, None, ["xn"] + CK, ["xn"])
        cbs = b.colblocks(NE)
        for j in range(48):
            wg, wgk = b.load_w(W["ffn_w_up"][layer, :, j * 128:(j + 1) * 128], 16)
            pg = b.psum[2 * (j % 2)]
            pgk = ["ps%d" % (4 * (j % 2)), "ps%d" % (4 * (j % 2) + 1)]
            for (a, e_) in cbs:
                for k in range(16):
                    b.mm(pg[:, a:e_], wg[:, k, :], xn[:, k, a:e_], k == 0, k == 15, [wgk, "xn"], pgk)
            wu, wuk = b.load_w(W["ffn_w_up"][layer, :, DFF + j * 128:DFF + (j + 1) * 128], 16)
            pu = b.psum[2 * (j % 2) + 1]
            puk = ["ps%d" % (4 * (j % 2) + 2), "ps%d" % (4 * (j % 2) + 3)]
            for (a, e_) in cbs:
                for k in range(16):
                    b.mm(pu[:, a:e_], wu[:, k, :], xn[:, k, a:e_], k == 0, k == 15, [wuk, "xn"], puk)
            for (pp, ppk, dst, dk, ci) in ((pg, pgk, cg, "cg", j), (pu, puk, cu, "cu", 48 + j)):
                wc = lambda k: cols[:, C_FCW + k * 96 + ci:C_FCW + k * 96 + ci + 1]
                b.ts(dst, pp[:, 1:TB + 1], wc(1), None, ALU.mult, None, ppk + ["cols"], [dk])
                b.stt(dst, pp[:, 0:TB], wc(0), dst, ALU.mult, ALU.add, ppk + ["cols", dk], [dk])
                b.stt(dst, pp[:, 2:TB + 2], wc(2), dst, ALU.mult, ALU.add, ppk + ["cols", dk], [dk])
            b.act(cg, cg, AF.Gelu_apprx_tanh, ["cg"], ["cg"])
            b.tt(act_[:, j, :], cg, cu, ALU.mult, ["cg", "cu"], ["act"])
        for m in range(16):
            pst, pk = b.ps(m % 2)
            for q_ in range(3):
                wt, wk = b.load_w(W["ffn_w_down"][layer, q_ * 2048:(q_ + 1) * 2048, m * 128:(m + 1) * 128], 16)
                for k in range(16):
                    b.mm(pst[:, 0:TB], wt[:, k, :], act_[:, q_ * 16 + k, :], q_ == 0 and k == 0, q_ == 2 and k == 15,
                         [wk, "act"], [pk])
            b.tt(h[:, m, 1:TB + 1], h[:, m, 1:TB + 1], pst[:, 0:TB], ALU.add, [pk, "h%d" % m], ["h%d" % m])
        xn3 = xn[:, :, 1:TB + 1]
        rms(1, TB + 1, C_PLEG, xn3, "xn", HK)
        for tt in range(TB // 128):
            b.dma(pin, p_d[layer, t0 + tt * 128:t0 + (tt + 1) * 128, :], [], ["pin"])
            pst, pk = b.ps(2 + tt % 2)
            for k in range(2):
                b.tr(pst[:, k * 128:(k + 1) * 128], pin[:, k * 128:(k + 1) * 128], ident, ["pin"] + CK, [pk])
            b.evac(pT[:, :, tt * 128:(tt + 1) * 128], pst[:, 0:256].rearrange("p (a b) -> p a b", a=2), [pk], ["pT"])
        for m in range(16):
            wg, wgk = b.load_w(W["ple_w_gate"][layer, :, m * 128:(m + 1) * 128], 16)
            pa, pak = b.ps(4 + 2 * (m % 2))
            for k in range(16):
                b.mm(pa[:, 0:TB], wg[:, k, :], xn3[:, k, :], k == 0, k == 15, [wgk, "xn"], [pak])
            wp, wpk = b.load_w(W["ple_w_proj"][layer, :, m * 128:(m + 1) * 128], 2)
            pb_, pbk = b.ps(5 + 2 * (m % 2))
            for k in range(2):
                b.mm(pb_[:, 0:TB], wp[:, k, :], pT[:, k, :], k == 0, k == 1, [wpk, "pT"], [pbk])
            b.act(cg, pa[:, 0:TB], AF.Sigmoid, [pak], ["cg"])
            b.tt(cu, pb_[:, 0:TB], cg, ALU.mult, [pbk, "cg"], ["cu"])
            b.tt(h[:, m, 1:TB + 1], h[:, m, 1:TB + 1], cu, ALU.add, ["cu", "h%d" % m], ["h%d" % m])
        if not last:
            b.dma(hA[:, :, t0:t0 + TB].rearrange("c p t -> p c t"), h[:, :, 1:TB + 1], HK, [], q="act")
        else:
            if b.debug:
                b.dma(hA[:, :, t0:t0 + TB].rearrange("c p t -> p c t"), h[:, :, 1:TB + 1], HK, [], q="act")
            xf = b.alloc([16, TB], F32)
            yo = [b.alloc([D], F32) for _ in range(2)]
            rms(1, TB + 1, C_FING, xf, "xf", HK)
            for tt in range(TB // 128):
                o, ok = yo[tt % 2], "yo%d" % (tt % 2)
                for g in range(4):
                    pst, pk = b.ps(4 + (tt * 4 + g) % 4)
                    for j in range(4):
                        ch = g * 4 + j
                        b.tr(pst[:, j * 128:(j + 1) * 128], xf[:, ch, tt * 128:(tt + 1) * 128], ident, ["xf"] + CK, [pk])
                    b.evac(o[:, g * 512:(g + 1) * 512], pst, [pk], [ok])
                b.dma(y_d[t0 + tt * 128:t0 + (tt + 1) * 128, :], o, [ok], [], q="act")
    b.wcache = None


_NC_CACHE = {}


def kernel(**inputs):
    SEG = 2048
    T = 2 * SEG
    if "nc" not in _NC_CACHE:
        _NC_CACHE["nc"] = build_nc(SEG=SEG)[0]
    nc = _NC_CACHE["nc"]
    xp_ = np.asarray(inputs["x_prompt"], dtype=np.float32)
    xs_ = np.asarray(inputs["x_sample"], dtype=np.float32)
    pp_ = np.asarray(inputs["p_prompt"], dtype=np.float32)
    ps_ = np.asarray(inputs["p_sample"], dtype=np.float32)
    wts = {n: np.ascontiguousarray(np.asarray(inputs[n], dtype=np.float32)) for n, _ in WNAMES}
    in_maps = []
    for core in range(8):
        if core < 4:
            x = xp_[core]
            p = pp_[:, core]
            f = 1.0
        else:
            j = 2 * (core - 4)
            x = xs_[j:j + 2].reshape(T, D)
            p = ps_[:, j:j + 2].reshape(2, T, 256)
            f = 0.0
        m = {"x": np.ascontiguousarray(x), "p": np.ascontiguousarray(p),
             "flag": np.full((128, 1), f, np.float32)}
        m.update(wts)
        in_maps.append(m)
    res = run_bass_kernel_spmd(nc, in_maps, core_ids=list(range(8)))
    outs = [np.asarray(r["y"], dtype=np.float32) for r in res.results]
    y_prompt = np.stack(outs[0:4], axis=0)
    y_sample = np.concatenate([o.reshape(2, SEG, D) for o in outs[4:8]], axis=0)
    return (y_prompt, y_sample)
```

```python
import math
from contextlib import ExitStack

import numpy as np
import concourse.bass as bass
import concourse.mybir as mybir
from concourse.bass_utils import run_bass_kernel_spmd

F32 = mybir.dt.float32
BF16 = mybir.dt.bfloat16
AF = mybir.ActivationFunctionType
ALU = mybir.AluOpType
AX = mybir.AxisListType

D = 2048
KD = 16
DPROJ = 16432
DFF = 6144
EPS = 1e-6
NEG = -30000.0

COMPUTE = ("pe", "act", "dve", "pool")
QUEUES = ("pe", "act", "dve", "pool", "sp")
NDSEM = 8


class Op:
    __slots__ = ("eng", "fn", "dma", "deps", "sig", "sem", "sigval", "waited")

    def __init__(self, eng, fn, dma):
        self.eng = eng
        self.fn = fn
        self.dma = dma
        self.deps = []
        self.sig = False
        self.sem = None
        self.sigval = 0
        self.waited = False


class Sched:
    def __init__(self):
        self.q = {e: [] for e in QUEUES}
        self.last_w = {}
        self.readers = {}
        self.dma_hist = {e: [] for e in QUEUES}
        self.all_dmas = []
        self.bar = []
        self.bar_pending = {e: False for e in QUEUES}
        self.dmas_since_bar = []

    def barrier(self):
        deps = [self.q[e][-1] for e in COMPUTE if self.q[e] and self.q[e][-1].fn is not None]
        deps += self.dmas_since_bar
        self.bar = deps
        self.dmas_since_bar = []
        for e in QUEUES:
            self.bar_pending[e] = True

    def add(self, eng, fn, reads=(), writes=(), dma=False):
        op = Op(eng, fn, dma)
        deps = {}

        def consider(d, kind):
            if d is None:
                return
            if (not d.dma) and d.eng == eng:
                if (not dma) and eng == "pe":
                    return
            deps[id(d)] = d

        for k in reads:
            consider(self.last_w.get(k), "raw")
            if k.startswith("ps"):
                for r in self.readers.get(k, ()):
                    if r.eng != eng:
                        deps[id(r)] = r
        for k in writes:
            consider(self.last_w.get(k), "waw")
            for r in self.readers.get(k, ()):
                consider(r, "war")
        if self.bar_pending[eng]:
            self.bar_pending[eng] = False
            for d in self.bar:
                if d.dma or d.eng != eng:
                    deps[id(d)] = d
        if dma:
            hist = self.dma_hist[eng]
            if len(hist) >= NDSEM:
                d = hist[-NDSEM]
                deps[id(d)] = d
            hist.append(op)
            self.all_dmas.append(op)
            self.dmas_since_bar.append(op)
        op.deps = list(deps.values())
        for k in reads:
            self.readers.setdefault(k, []).append(op)
        for k in writes:
            self.last_w[k] = op
            self.readers[k] = []
        self.q[eng].append(op)
        return op

    def pe(self, fn, reads=(), writes=()):
        return self.add("pe", fn, reads, writes)

    def act(self, fn, reads=(), writes=()):
        return self.add("act", fn, reads, writes)

    def dve(self, fn, reads=(), writes=()):
        return self.add("dve", fn, reads, writes)

    def pool(self, fn, reads=(), writes=()):
        return self.add("pool", fn, reads, writes)

    def dma(self, q, fn, reads=(), writes=()):
        return self.add(q, fn, reads, writes, dma=True)

    def emit(self, nc):
        fin = Op("sp", None, False)
        fin.deps = list(self.all_dmas)
        self.q["sp"].append(fin)
        for e in QUEUES:
            for op in self.q[e]:
                for d in op.deps:
                    d.waited = True
        with ExitStack() as es:
            csem = {e: es.enter_context(nc.semaphore("s_" + e)) for e in COMPUTE}
            dsem = {e: [es.enter_context(nc.semaphore("d_%s_%d" % (e, i))) for i in range(NDSEM)]
                    for e in QUEUES}
            for e in QUEUES:
                cnt = 0
                dcnt = [0] * NDSEM
                di = 0
                for op in self.q[e]:
                    if op.fn is None:
                        continue
                    if op.dma:
                        s = di % NDSEM
                        di += 1
                        dcnt[s] += 16
                        op.sem = dsem[e][s]
                        op.sigval = dcnt[s]
                        op.sig = True
                    elif op.waited:
                        cnt += 1
                        op.sem = csem[e]
                        op.sigval = cnt
                        op.sig = True
            block = es.enter_context(nc.Block())
            engs = {"pe": block.tensor, "act": block.scalar, "dve": block.vector,
                    "pool": block.gpsimd, "sp": block.sync}
            self.stats = {}
            for e in QUEUES:
                ops = self.q[e]
                nw = [0]

                def body(eng, ops=ops, nw=nw):
                    known = {}
                    for op in ops:
                        need = {}
                        for d in op.deps:
                            key = d.sem.name
                            if known.get(key, 0) >= d.sigval:
                                continue
                            if key not in need or need[key][1] < d.sigval:
                                need[key] = (d.sem, d.sigval)
                        for key, (sem, val) in need.items():
                            eng.wait_ge(sem, val)
                            known[key] = val
                            nw[0] += 1
                        if op.fn is None:
                            continue
                        ins = op.fn(eng)
                        if op.sig:
                            ins.then_inc(op.sem, 16 if op.dma else 1)

                engs[e](body)
                self.stats[e] = (len(ops), nw[0])


ARENA_WORDS = 52224


class B:
    def __init__(self, SEG, debug=False):
        self.SEG = SEG
        self.T = 2 * SEG
        self.debug = debug
        self.nc = bass.Bass("TRN2", target_bir_lowering=False)
        self.S = Sched()
        self.uid = 0

    def dram_in(self, name, shape):
        return self.nc.dram_tensor(name, list(shape), F32, kind="ExternalInput").ap()

    def scratch(self, name, shape, dt):
        kind = "ExternalOutput" if self.debug else "Internal"
        return self.nc.dram_tensor(name, list(shape), dt, kind=kind).ap()

    def alloc(self, free_shape, dt, parts=128):
        n = 1
        for s in free_shape:
            n *= s
        words = n if dt == F32 else (n + 1) // 2
        words = (words + 7) // 8 * 8
        assert self.off + words <= ARENA_WORDS, ("arena overflow", self.off, words)
        ap = self.arena[0:parts, self.off:self.off + words]
        self.off += words
        if dt == BF16:
            ap = ap.bitcast(BF16)
        ap = ap[:, 0:n]
        if len(free_shape) == 2:
            ap = ap.rearrange("p (a b) -> p a b", a=free_shape[0])
        elif len(free_shape) == 3:
            ap = ap.rearrange("p (a b c) -> p a b c", a=free_shape[0], b=free_shape[1])
        return ap

    def key(self, base):
        self.uid += 1
        return "%s#%d" % (base, self.uid)

    def phase(self):
        self.S.barrier()
        self.off = self.persist_off

    def mm(self, out, lhsT, rhs, start, stop, R, W):
        self.S.pe(lambda e: e.matmul(out, lhsT=lhsT, rhs=rhs, start=start, stop=stop), R, W)

    def tr(self, out, in_, ident, R, W):
        self.S.pe(lambda e: e.transpose(out, in_, ident), R, W)

    def act(self, out, in_, func, R, W, bias=0.0, scale=1.0):
        self.S.act(lambda e: e.activation(out=out, in_=in_, func=func, bias=bias, scale=scale), R, W)

    def tt(self, out, a, b, op, R, W, eng="dve"):
        self.S.add(eng, lambda e: e.tensor_tensor(out=out, in0=a, in1=b, op=op), R, W)

    def ts(self, out, a, s1, s2, op0, op1, R, W, eng="dve"):
        if s2 is None:
            self.S.add(eng, lambda e: e.tensor_scalar(out=out, in0=a, scalar1=s1, scalar2=None, op0=op0), R, W)
        else:
            self.S.add(eng, lambda e: e.tensor_scalar(out=out, in0=a, scalar1=s1, scalar2=s2, op0=op0, op1=op1), R, W)

    def stt(self, out, a, s, b, op0, op1, R, W):
        self.S.dve(lambda e: e.scalar_tensor_tensor(out=out, in0=a, scalar=s, in1=b, op0=op0, op1=op1), R, W)

    def copy(self, out, in_, R, W, eng="dve"):
        if eng == "act":
            self.S.act(lambda e: e.copy(out=out, in_=in_), R, W)
        else:
            self.S.add(eng, lambda e: e.tensor_copy(out=out, in_=in_), R, W)

    def dma(self, out, in_, R, W, q="sp", slow=False):
        if slow:
            self.S.dma(q, lambda e: e.dma_start(out=out, in_=in_, allow_slow_non_contiguous=True), R, W)
        else:
            self.S.dma(q, lambda e: e.dma_start(out=out, in_=in_), R, W)

    def memset(self, ap, val, W, eng="pool"):
        self.S.add(eng, lambda e: e.memset(ap, val), (), W)

    def evac(self, out, in_, R, W):
        self.ev = getattr(self, "ev", 0) + 1
        self.copy(out, in_, R, W, eng=("act" if self.ev % 2 else "dve"))

    def wsetup(self):
        self.wst = [self.alloc([16, 128], F32) for _ in range(2)]
        self.wbf = [self.alloc([16, 128], BF16) for _ in range(3)]
        self.wi = 0

    def load_w(self, src, nk, ncol=128):
        i = self.wi
        self.wi += 1
        st = self.wst[i % 2]
        bf = self.wbf[i % 3]
        ks, kb = "wst%d" % (i % 2), "wbf%d" % (i % 3)
        wc = getattr(self, "wcache", None)
        if wc is not None:
            idx = wc["idx"]
            wc["idx"] += 1
            slot = wc["ap"][idx, :, 0:nk * ncol].rearrange("p (k n) -> p k n", n=ncol)
            if wc["mode"] == "use":
                self.dma(bf[:, 0:nk, 0:ncol], slot, [], [kb])
                return bf, kb
        self.dma(st[:, 0:nk, 0:ncol], src.rearrange("(k p) n -> p k n", p=128), [], [ks])
        self.copy(bf[:, 0:nk, 0:ncol], st[:, 0:nk, 0:ncol], [ks], [kb], eng="pool")
        if wc is not None:
            self.dma(slot, bf[:, 0:nk, 0:ncol], [kb], [], q="pool")
        return bf, kb

    def colblocks(self, n):
        return [(a, min(a + 512, n)) for a in range(0, n, 512)]

    def ps(self, i):
        return self.psum[i // 2][:, (i % 2) * 512:(i % 2) * 512 + 512], "ps%d" % i


WNAMES = [
    ("norm_mix_g", (2, 2048)), ("w_in", (2, 2048, 16432)), ("rglru_conv_w", (2, 4, 1024)),
    ("rglru_conv_b", (2, 1024)), ("rglru_gate_w", (2, 2, 2, 8, 128, 128)), ("rglru_gate_b", (2, 2, 2, 1024)),
    ("rglru_lambda", (2, 2, 1024)), ("mlstm_gate_b", (2, 16)), ("mlstm_norm_g", (2, 1024)),
    ("gdn_conv_w", (2, 4, 3072)), ("gdn_A_log", (2, 2, 8)), ("gdn_dt_bias", (2, 2, 8)),
    ("gdn_norm_g", (2, 1024)), ("w_branch", (2, 3, 1024, 2048)), ("w_out", (2, 2048, 2048)),
    ("norm_ffn_g", (2, 2048)), ("ffn_w_up", (2, 2048, 12288)), ("ffn_conv_w", (2, 3, 12288)),
    ("ffn_w_down", (2, 6144, 2048)), ("norm_ple_g", (2, 2048)), ("ple_w_gate", (2, 2048, 2048)),
    ("ple_w_proj", (2, 256, 2048)), ("norm_final_g", (2048,)),
]

C_MIXG, C_FFNG, C_PLEG, C_FING = 0, 16, 32, 48
C_ACW, C_ACB, C_AGB, C_ALAM = 64, 96, 104, 136
C_BNG, C_CCW, C_CNG, C_FCW = 152, 160, 256, 264
C_AC8 = 552
NCOLS = 568


def build_nc(SEG=2048, debug=False, nlayers=2, stop_after=None):
    b = B(SEG, debug)
    nc, S, T = b.nc, b.S, b.T
    NT = T // 128
    x_d = b.dram_in("x", (T, D))
    p_d = b.dram_in("p", (2, T, 256))
    flag_d = b.dram_in("flag", (128, 1))
    W = {n: b.dram_in(n, s) for n, s in WNAMES}
    y_d = nc.dram_tensor("y", [T, D], F32, kind="ExternalOutput").ap()
    hA = b.scratch("hA", (16, 128, T), F32)
    hM = b.scratch("hM", (16, 128, T), F32)
    Ain = b.scratch("Ain", (16, 128, T), BF16)
    Bqk = b.scratch("Bqk", (16, 128, T), BF16)
    Bo = b.scratch("Bo", (8, 128, T), BF16)
    Bv = b.scratch("Bv", (T, 1024), BF16)
    Bk = b.scratch("Bk", (T, 1024), BF16)
    Gt = b.scratch("Gt", (T, 48), F32)
    Cqkv = b.scratch("Cqkv", (24, 128, T), BF16)
    Cz = b.scratch("Cz", (8, 128, T), BF16)
    MG = b.scratch("MG", (48, 128, T), BF16)
    Y = b.scratch("Y", (24, 128, T), BF16)
    Wc3a = nc.dram_tensor("Wc3a", [64, 128, 2048], BF16, kind="Internal").ap()
    Wc3b = nc.dram_tensor("Wc3b", [208, 128, 2048], BF16, kind="Internal").ap()

    with ExitStack() as es:
        b.arena = es.enter_context(nc.sbuf_tensor("arena", [128, ARENA_WORDS], F32))
        b.psum = [es.enter_context(nc.psum_tensor("psum%d" % i, [128, 1024], F32)) for i in range(4)]
        b.off = 0
        ident = b.alloc([128], F32)
        ones_f = b.alloc([128], F32)
        ones_b = b.alloc([128], BF16)
        Uf = b.alloc([128], F32)
        Ub = b.alloc([128], F32)
        NGf = b.alloc([128], F32)
        NGb = b.alloc([128], F32)
        PSf = b.alloc([128], F32)
        PSb = b.alloc([128], F32)
        flag = b.alloc([1], F32)
        ident_b = b.alloc([128], BF16)
        cols = b.alloc([NCOLS], F32)
        gbB = b.alloc([16], F32)
        dtb = b.alloc([16], F32)
        negA = b.alloc([16], F32)
        b.persist_off = b.off

        CK = []

        def aff(ap, src_val, cmp, fill, pattern, cm):
            k = "const%d" % len(CK)
            CK.append(k)
            b.memset(ap, src_val, [k])
            S.pool(lambda e: e.affine_select(out=ap, in_=ap, compare_op=cmp, fill=fill, base=0,
                                              pattern=pattern, channel_multiplier=cm), [k], [k])

        aff(ident, 0.0, ALU.not_equal, 1.0, [[-1, 128]], 1)
        b.copy(ident_b, ident, [CK[0]], ["const_ib"], eng="pool")
        CK.append("const_ib")
        b.memset(ones_f, 1.0, ["const_of"])
        b.memset(ones_b, 1.0, ["const_ob"])
        CK += ["const_of", "const_ob", "const_flag"]
        aff(Uf, 1.0, ALU.is_ge, 0.0, [[1, 128]], -1)
        aff(Ub, 1.0, ALU.is_ge, 0.0, [[-1, 128]], 1)
        aff(NGf, 0.0, ALU.is_ge, NEG, [[1, 128]], -1)
        aff(NGb, 0.0, ALU.is_ge, NEG, [[-1, 128]], 1)
        aff(PSf, 0.0, ALU.is_gt, -NEG, [[-1, 128]], 1)
        aff(PSb, 0.0, ALU.is_gt, -NEG, [[1, 128]], -1)
        b.dma(flag, flag_d, [], ["const_flag"])

        def load_cols(layer):
            b.phase()
            stage = b.alloc([128], F32)
            items = [
                (W["norm_mix_g"][layer].rearrange("(c p) -> c p", p=128), 16, C_MIXG),
                (W["norm_ffn_g"][layer].rearrange("(c p) -> c p", p=128), 16, C_FFNG),
                (W["norm_ple_g"][layer].rearrange("(c p) -> c p", p=128), 16, C_PLEG),
                (W["norm_final_g"].rearrange("(c p) -> c p", p=128), 16, C_FING),
                (W["rglru_conv_w"][layer].rearrange("k (n p) -> (k n) p", p=128), 32, C_ACW),
                (W["rglru_conv_b"][layer].rearrange("(n p) -> n p", p=128), 8, C_ACB),
                (W["rglru_gate_b"][layer].rearrange("e g (n p) -> (e g n) p", p=128), 32, C_AGB),
                (W["rglru_lambda"][layer].rearrange("e (n p) -> (e n) p", p=128), 16, C_ALAM),
                (W["mlstm_norm_g"][layer].rearrange("(n p) -> n p", p=128), 8, C_BNG),
                (W["gdn_conv_w"][layer].rearrange("k (c p) -> (k c) p", p=128), 96, C_CCW),
                (W["gdn_norm_g"][layer].rearrange("(n p) -> n p", p=128), 8, C_CNG),
            ]
            fcw = W["ffn_conv_w"][layer].rearrange("k (c p) -> (k c) p", p=128)
            for q in range(3):
                items.append((fcw[q * 96:(q + 1) * 96, :], 96, C_FCW + q * 96))
            for i, (src, R, c0) in enumerate(items):
                pst, pk = b.ps(i % 2)
                b.dma(stage[0:R, :], src, [], ["lc_stage"])
                b.tr(pst[:, 0:R], stage[0:R, :], ident[0:R, 0:R], ["lc_stage"] + CK, [pk])
                b.copy(cols[:, c0:c0 + R], pst[:, 0:R], [pk], ["cols"], eng="act")
            tmp = b.alloc([16], F32)
            b.act(tmp, cols[:, C_ALAM:C_ALAM + 16], AF.Exp, ["cols"], ["lc_tmp"], scale=-1.0)
            b.act(tmp, tmp, AF.Ln, ["lc_tmp"], ["lc_tmp"], bias=1.0)
            b.S.act(lambda e: e.mul(out=cols[:, C_AC8:C_AC8 + 16], in_=tmp, mul=-8.0), ["lc_tmp"], ["cols"])
            b.dma(gbB, W["mlstm_gate_b"][layer:layer + 1, :].partition_broadcast(128), [], ["bc"])
            b.dma(dtb, W["gdn_dt_bias"][layer:layer + 1].rearrange("a e h -> a (e h)").partition_broadcast(128), [], ["bc"])
            b.dma(negA, W["gdn_A_log"][layer:layer + 1].rearrange("a e h -> a (e h)").partition_broadcast(128), [], ["bc"])
            b.act(negA, negA, AF.Exp, ["bc"], ["bc"])
            b.S.act(lambda e: e.mul(out=negA, in_=negA, mul=-1.0), ["bc"], ["bc"])

        def phase0():
            b.phase()
            xin = [b.alloc([D], F32) for _ in range(2)]
            xo = [b.alloc([16, 128], F32) for _ in range(2)]
            for tt in range(NT):
                xi, xk = xin[tt % 2], "xin%d" % (tt % 2)
                o, ok = xo[tt % 2], "xo%d" % (tt % 2)
                b.dma(xi, x_d[tt * 128:(tt + 1) * 128, :], [], [xk])
                for g in range(4):
                    pst, pk = b.ps(g % 2 + 2 * (tt % 2))
                    for j in range(4):
                        c = g * 4 + j
                        b.tr(pst[:, j * 128:(j + 1) * 128], xi[:, c * 128:(c + 1) * 128], ident, [xk] + CK, [pk])
                    b.evac(o[:, g * 4:(g + 1) * 4, :], pst.rearrange("p (a b) -> p a b", a=4), [pk], [ok])
                b.dma(hA[:, :, tt * 128:(tt + 1) * 128].rearrange("c p t -> p c t"), o, [ok], ["hA"], q="act")

        b.cols, b.ident, b.ones_b, b.ones_f, b.flag, b.CK = cols, ident, ones_b, ones_f, flag, CK
        ctx = dict(locals())
        phase0()
        for layer in range(nlayers):
            load_cols(layer)
            phase1(b, ctx, layer)
            if stop_after == "p1":
                break
            mixerA(b, ctx, layer)
            if stop_after == "mA":
                break
            mixerB(b, ctx, layer)
            if stop_after == "mB":
                break
            mixerC(b, ctx, layer)
            if stop_after == "mC":
                break
            phase3a(b, ctx, layer)
            if stop_after == "p3a":
                break
            phase3b(b, ctx, layer, last=(layer == nlayers - 1))
        S.emit(nc)
    return nc, b


def phase1(b, c, layer):
    T = b.T
    TB = min(T, 2048)
    cols, CK = b.cols, b.CK
    W, hA = c["W"], c["hA"]
    groups = [(0, 16, c["Ain"]), (2048, 16, c["Bqk"]), (5120, 8, c["Bo"]), (6160, 24, c["Cqkv"]),
              (9232, 8, c["Cz"]), (10288, 48, c["MG"])]
    for tb in range(T // TB):
        b.phase()
        t0 = tb * TB
        b.wsetup()
        xn = b.alloc([16, TB], BF16)
        hc = [b.alloc([TB], F32) for _ in range(2)]
        sq = [b.alloc([TB], BF16) for _ in range(2)]
        rstd = b.alloc([TB], F32)
        ost = [b.alloc([TB], BF16) for _ in range(2)]
        cbs = b.colblocks(TB)
        for ch in range(16):
            h, hk = hc[ch % 2], "hc%d" % (ch % 2)
            q, qk = sq[ch % 2], "sq%d" % (ch % 2)
            b.dma(h, hA[ch, :, t0:t0 + TB], [], [hk])
            b.act(q, h, AF.Square, [hk], [qk])
            for i, (a, e_) in enumerate(cbs):
                pst, pk = b.ps(i)
                b.mm(pst[:, 0:e_ - a], b.ones_b, q[:, a:e_], ch == 0, ch == 15, [qk] + CK, [pk])
            b.ts(xn[:, ch, :], h, cols[:, C_MIXG + ch:C_MIXG + ch + 1], None, ALU.mult, None, [hk, "cols"], ["xn"])
        for i, (a, e_) in enumerate(cbs):
            pst, pk = b.ps(i)
            b.act(rstd[:, a:e_], pst[:, 0:e_ - a], AF.Sqrt, [pk], ["rstd"], bias=EPS, scale=1.0 / D)
        b.S.dve(lambda e: e.reciprocal(out=rstd, in_=rstd), ["rstd"], ["rstd"])
        for ch in range(16):
            b.tt(xn[:, ch, :], xn[:, ch, :], rstd, ALU.mult, ["xn", "rstd"], ["xn"])
        pi = 0
        oi = 0
        for col0, nch, dst in groups:
            for j in range(nch):
                wt, wk = b.load_w(W["w_in"][layer, :, col0 + j * 128:col0 + (j + 1) * 128], 16)
                o, ok = ost[oi % 2], "ost%d" % (oi % 2)
                oi += 1
                for (a, e_) in cbs:
                    pst, pk = b.ps(4 + pi % 4)
                    pi += 1
                    for k in range(16):
                        b.mm(pst[:, 0:e_ - a], wt[:, k, :], xn[:, k, a:e_], k == 0, k == 15, [wk, "xn"], [pk])
                    b.evac(o[:, a:e_], pst[:, 0:e_ - a], [pk], [ok])
                b.dma(dst[j, :, t0:t0 + TB], o, [ok], [], q="act")
        wtb = b.alloc([16, 512], BF16)
        otm = [b.alloc([512], BF16) for _ in range(2)]
        for (col0, dst) in [(4096, c["Bv"]), (3072, c["Bk"])]:
            for half in range(2):
                for qd in range(4):
                    cc = col0 + half * 512 + qd * 128
                    wt, wk = b.load_w(W["w_in"][layer, :, cc:cc + 128], 16)
                    b.copy(wtb[:, :, qd * 128:(qd + 1) * 128], wt[:, 0:16, :], [wk], ["wtb"], eng="pool")
                for tt in range(TB // 128):
                    pst, pk = b.ps(4 + pi % 4)
                    pi += 1
                    for k in range(16):
                        b.mm(pst, xn[:, k, tt * 128:(tt + 1) * 128], wtb[:, k, :], k == 0, k == 15, ["xn", "wtb"], [pk])
                    o, ok = otm[tt % 2], "otm%d" % (tt % 2)
                    b.evac(o, pst, [pk], [ok])
                    b.dma(dst[t0 + tt * 128:t0 + (tt + 1) * 128, half * 512:(half + 1) * 512], o, [ok], [], q="act")
        gst = b.alloc([16, 48], F32)
        gbf = b.alloc([16, 48], BF16)
        og = [b.alloc([48], F32) for _ in range(2)]
        b.dma(gst[:, :, 0:16], W["w_in"][layer, :, 6144:6160].rearrange("(k p) n -> p k n", p=128), [], ["gst"])
        b.dma(gst[:, :, 16:48], W["w_in"][layer, :, 10256:10288].rearrange("(k p) n -> p k n", p=128), [], ["gst"])
        b.copy(gbf, gst, ["gst"], ["gbf"], eng="pool")
        for tt in range(TB // 128):
            pst, pk = b.ps(4 + pi % 4)
            pi += 1
            for k in range(16):
                b.mm(pst[:, 0:48], xn[:, k, tt * 128:(tt + 1) * 128], gbf[:, k, :], k == 0, k == 15, ["xn", "gbf"], [pk])
            o, ok = og[tt % 2], "og%d" % (tt % 2)
            b.evac(o, pst[:, 0:48], [pk], [ok])
            b.dma(c["Gt"][t0 + tt * 128:t0 + (tt + 1) * 128, :], o, [ok], [], q="act")


def mixerA(b, c, layer):
    T, SEG = b.T, b.SEG
    cols, CK, flag, S = b.cols, b.CK, b.flag, b.S
    W, Ain, Y = c["W"], c["Ain"], c["Y"]
    b.phase()
    b.wsetup()
    PW = SEG + 3
    xp = b.alloc([2 * PW], BF16)
    ga = b.alloc([T], BF16)
    xcb = b.alloc([T], BF16)
    yb = b.alloc([T], BF16)
    xc = b.alloc([T], F32)
    rf = b.alloc([T], F32)
    uf = b.alloc([T], F32)
    sf = b.alloc([T], F32)
    hf = b.alloc([T], F32)
    hb = b.alloc([T], F32)
    carry = b.alloc([2], F32)
    cbs = b.colblocks(T)
    for n in range(8):
        for s in range(2):
            b.dma(xp[:, s * PW + 2:s * PW + 2 + SEG], Ain[n, :, s * SEG:(s + 1) * SEG], [], ["xp"])
        b.dma(ga, Ain[8 + n, :, :], [], ["ga"])
        b.memset(xp[:, 0:2], 0.0, ["xp"])
        b.memset(xp[:, 2 * PW - 1:2 * PW], 0.0, ["xp"])
        b.ts(xp[:, PW - 1:PW], xp[:, PW + 2:PW + 3], flag, None, ALU.mult, None, ["xp"] + CK, ["xp"])
        b.ts(xp[:, PW:PW + 2], xp[:, SEG:SEG + 2], flag, None, ALU.mult, None, ["xp"] + CK, ["xp"])
        for s in range(2):
            o = s * PW
            dst = xc[:, s * SEG:(s + 1) * SEG]
            b.ts(dst, xp[:, o:o + SEG], cols[:, C_ACW + n:C_ACW + n + 1], cols[:, C_ACB + n:C_ACB + n + 1],
                 ALU.mult, ALU.add, ["xp", "cols"], ["xc"])
            for k in range(1, 4):
                b.stt(dst, xp[:, o + k:o + k + SEG], cols[:, C_ACW + k * 8 + n:C_ACW + k * 8 + n + 1], dst,
                      ALU.mult, ALU.add, ["xp", "cols", "xc"], ["xc"])
        b.copy(xcb, xc, ["xc"], ["xcb"], eng="act")
        for e in range(2):
            wr, wrk = b.load_w(W["rglru_gate_w"][layer, e, 0, n], 1)
            wi_, wik = b.load_w(W["rglru_gate_w"][layer, e, 1, n], 1)
            for (wt, wk, dst, dk, g) in ((wr, wrk, rf, "rf", 0), (wi_, wik, uf, "uf", 1)):
                bias = cols[:, C_AGB + (e * 2 + g) * 8 + n:C_AGB + (e * 2 + g) * 8 + n + 1]
                for ci, (a, e_) in enumerate(cbs):
                    pst, pk = b.ps(ci % 8)
                    b.mm(pst[:, 0:e_ - a], wt[:, 0, :], xcb[:, a:e_], True, True, [wk, "xcb"], [pk])
                    b.act(dst[:, a:e_], pst[:, 0:e_ - a], AF.Sigmoid, [pk, "cols"], [dk], bias=bias)
            c8 = cols[:, C_AC8 + e * 8 + n:C_AC8 + e * 8 + n + 1]
            b.act(rf, rf, AF.Exp, ["rf", "cols"], ["rf"], scale=c8)
            b.act(sf, rf, AF.Square, ["rf"], ["sf"])
            b.act(sf, sf, AF.Sqrt, ["sf"], ["sf"], bias=1.0, scale=-1.0)
            b.tt(uf, uf, xc, ALU.mult, ["uf", "xc"], ["uf"])
            b.tt(uf, uf, sf, ALU.mult, ["uf", "sf"], ["uf"])
            if e == 0:
                S.dve(lambda en: en.tensor_tensor_scan(out=hf[:, 0:SEG], data0=rf[:, 0:SEG], data1=uf[:, 0:SEG],
                                                       initial=0.0, op0=ALU.mult, op1=ALU.add), ["rf", "uf"], ["hf"])
                b.tt(carry[:, 0:1], hf[:, SEG - 1:SEG], flag, ALU.mult, ["hf"] + CK, ["carry0"])
                S.dve(lambda en: en.tensor_tensor_scan(out=hf[:, SEG:T], data0=rf[:, SEG:T], data1=uf[:, SEG:T],
                                                       initial=carry[:, 0:1], op0=ALU.mult, op1=ALU.add),
                      ["rf", "uf", "carry0"], ["hf"])
            else:
                S.dve(lambda en: en.tensor_tensor_scan(out=hb[:, SEG:T][:, ::-1], data0=rf[:, SEG:T][:, ::-1],
                                                       data1=uf[:, SEG:T][:, ::-1], initial=0.0,
                                                       op0=ALU.mult, op1=ALU.add), ["rf", "uf"], ["hb"])
                b.tt(carry[:, 1:2], hb[:, SEG:SEG + 1], flag, ALU.mult, ["hb"] + CK, ["carry1"])
                S.dve(lambda en: en.tensor_tensor_scan(out=hb[:, 0:SEG][:, ::-1], data0=rf[:, 0:SEG][:, ::-1],
                                                       data1=uf[:, 0:SEG][:, ::-1], initial=carry[:, 1:2],
                                                       op0=ALU.mult, op1=ALU.add), ["rf", "uf", "carry1"], ["hb"])
        b.act(sf, ga, AF.Gelu_apprx_tanh, ["ga"], ["sf"])
        b.tt(hf, hf, hb, ALU.add, ["hf", "hb"], ["hf"])
        b.tt(yb, hf, sf, ALU.mult, ["hf", "sf"], ["yb"])
        b.dma(Y[n, :, :], yb, ["yb"], [], q="act")


def bc_mid(ap2, n):
    a = ap2.ap
    return bass.AP(ap2.tensor, ap2.offset, [list(a[0]), [0, n]] + [list(x) for x in a[1:]])


def bc_l3(ap3, n):
    a = ap3.ap
    return bass.AP(ap3.tensor, ap3.offset, [list(a[0]), list(a[1]), [0, n]])


def bc_last(ap1, n):
    a = ap1.ap
    return bass.AP(ap1.tensor, ap1.offset, [list(a[0]), [0, n]])


def mixerB(b, c, layer):
    T, SEG = b.T, b.SEG
    NCH, NCS = T // 128, SEG // 128
    cols, CK, flag, S, ident = b.cols, b.CK, b.flag, b.S, b.ident
    Uf, Ub, NGf, NGb, gbB = c["Uf"], c["Ub"], c["NGf"], c["NGb"], c["gbB"]
    Bqk, Bv, Bk, Bo, Gt, Y = c["Bqk"], c["Bv"], c["Bk"], c["Bo"], c["Gt"], c["Y"]
    b.phase()
    G = b.alloc([NCH, 16], F32)
    lf = b.alloc([NCH, 8], F32)
    sc = b.alloc([NCH, 8], F32)
    dec = b.alloc([NCH, 8], F32)
    b.dma(G, Gt[:, 0:16].rearrange("(c p) n -> p c n", p=128), [], ["G"])
    b.tt(G, G, bc_mid(gbB, NCH), ALU.add, ["G", "bc"], ["G"])
    b.act(lf[:, :, 0:4], G[:, :, 4:8], AF.Exp, ["G"], ["lf"], scale=-1.0)
    b.act(lf[:, :, 4:8], G[:, :, 12:16], AF.Exp, ["G"], ["lf"], scale=-1.0)
    b.act(lf, lf, AF.Ln, ["lf"], ["lf"], bias=1.0)
    S.act(lambda e: e.mul(out=lf, in_=lf, mul=-1.0), ["lf"], ["lf"])
    p0, k0 = b.ps(0)
    p1, k1 = b.ps(1)
    p2, k2 = b.ps(2)
    v3 = lambda p, n, w: p[:, 0:n * w].rearrange("p (a b) -> p a b", b=w)
    b.mm(v3(p0, NCH, 4), Uf, lf[:, :, 0:4], True, True, ["lf"] + CK, [k0])
    b.mm(v3(p1, NCH, 4), Ub, lf[:, :, 4:8], True, True, ["lf"] + CK, [k1])
    b.tt(sc[:, :, 0:4], G[:, :, 0:4], v3(p0, NCH, 4), ALU.subtract, ["G", k0], ["sc"])
    b.tt(sc[:, :, 4:8], G[:, :, 8:12], v3(p1, NCH, 4), ALU.subtract, ["G", k1], ["sc"])
    b.mm(v3(p2, NCH, 8), b.ones_f, lf, True, True, ["lf"] + CK, [k2])
    b.act(dec, v3(p2, NCH, 8), AF.Exp, [k2], ["dec"])

    qT = b.alloc([2, T], BF16)
    kT = b.alloc([2, T], BF16)
    ogT = b.alloc([2, T], BF16)
    yB = b.alloc([2, T], BF16)
    vaug = b.alloc([NCH, 260], BF16)
    ktm = b.alloc([NCH, 256], BF16)
    hsum = b.alloc([NCH, 256], F32)
    Cm2 = [b.alloc([2, 260], F32) for _ in range(2)]
    Cb2 = [b.alloc([2, 260], BF16) for _ in range(2)]
    Wrow2 = [b.alloc([128], F32) for _ in range(2)]
    DT2 = [b.alloc([128], F32) for _ in range(2)]
    ST2 = [b.alloc([128], BF16) for _ in range(2)]
    qw2 = [b.alloc([2, 128], BF16) for _ in range(2)]
    kw2 = [b.alloc([256], BF16) for _ in range(2)]
    rden2 = [b.alloc([1], F32) for _ in range(2)]
    sqc = b.alloc([256], F32)
    hn = b.alloc([256], F32)
    ssq = b.alloc([1], F32)
    sgo = b.alloc([2, 128], F32)
    for hd in range(4):
        b.dma(qT, Bqk[2 * hd:2 * hd + 2, :, :].rearrange("c p t -> p c t"), [], ["qT"])
        b.dma(kT, Bqk[8 + 2 * hd:8 + 2 * hd + 2, :, :].rearrange("c p t -> p c t"), [], ["kT"])
        b.dma(ogT, Bo[2 * hd:2 * hd + 2, :, :].rearrange("c p t -> p c t"), [], ["ogT"])
        b.dma(vaug[:, :, 0:256], Bv[:, hd * 256:(hd + 1) * 256].rearrange("(c p) e -> p c e", p=128), [], ["vaug"])
        b.dma(ktm, Bk[:, hd * 256:(hd + 1) * 256].rearrange("(c p) e -> p c e", p=128), [], ["ktm"])
        b.memset(vaug[:, :, 256:257], 1.0, ["vaug"])
        b.ts(qT, qT, 1.0 / 16.0, None, ALU.mult, None, ["qT"], ["qT"])
        b.memset(hsum, 0.0, ["hsum"])
        for e in range(2):
            b.memset(Cm2[e], 0.0, ["Cm%d" % e])
            b.memset(Cb2[e], 0.0, ["Cb%d" % e])
        for idx in range(NCH):
            for e in range(2):
                U, NG = (Uf, NGf) if e == 0 else (Ub, NGb)
                ch = idx if e == 0 else NCH - 1 - idx
                Cm, Cb, Wrow, DT, ST, qw, kw, rden = Cm2[e], Cb2[e], Wrow2[e], DT2[e], ST2[e], qw2[e], kw2[e], rden2[e]
                kCm, kCb, kW, kDT, kST, kqw, kkw, krd = ("%s%d" % (n_, e) for n_ in ("Cm", "Cb", "Wrow", "DT", "ST", "qw", "kw", "rden"))
                if idx == NCS:
                    b.ts(Cm, Cm, flag, None, ALU.mult, None, [kCm] + CK, [kCm])
                    b.copy(Cb, Cm, [kCm], [kCb], eng="act")
                col = e * 4 + hd
                cs = slice(ch * 128, (ch + 1) * 128)
                lfb = bc_last(lf[:, ch, col:col + 1], 128)
                pB, pBk = b.ps(0)
                pD, pDk = b.ps(1)
                pS, pSk = b.ps(2)
                pO, pOk = b.ps(3 + 2 * e)
                pC, pCk = b.ps(4 + 2 * e)
                pN, pNk = b.ps(7)
                b.mm(pB[:, 0:128], lfb, U, True, True, ["lf"] + CK, [pBk])
                b.act(Wrow, pB[:, 0:128], AF.Exp, [pBk], [kW])
                b.mm(pD[:, 0:128], lfb, U, True, False, ["lf"] + CK, [pDk])
                b.mm(pD[:, 0:128], ident, NG, False, True, CK, [pDk])
                b.act(DT, pD[:, 0:128], AF.Exp, [pDk, "sc"], [kDT], bias=sc[:, ch, col:col + 1])
                b.mm(pS[:, 0:128], kT[:, 0, cs], qT[:, 0, cs], True, False, ["kT", "qT"], [pSk])
                b.mm(pS[:, 0:128], kT[:, 1, cs], qT[:, 1, cs], False, True, ["kT", "qT"], [pSk])
                b.tt(ST, pS[:, 0:128], DT, ALU.mult, [pSk, kDT], [kST])
                for dc in range(2):
                    b.tt(qw[:, dc, :], qT[:, dc, cs], Wrow, ALU.mult, ["qT", kW], [kqw], eng="pool")
                b.mm(pO[:, 0:257], ST, vaug[:, ch, 0:257], True, False, [kST, "vaug"], [pOk])
                b.mm(pO[:, 0:257], qw[:, 0, :], Cb[:, 0, 0:257], False, False, [kqw, kCb], [pOk])
                b.mm(pO[:, 0:257], qw[:, 1, :], Cb[:, 1, 0:257], False, True, [kqw, kCb], [pOk])
                b.act(rden, pO[:, 256:257], AF.Abs, [pOk], [krd])
                b.ts(rden, rden, 1.0, None, ALU.max, None, [krd], [krd])
                S.dve(lambda en, rden=rden: en.reciprocal(out=rden, in_=rden), [krd], [krd])
                b.stt(hsum[:, ch, :], pO[:, 0:256], rden, hsum[:, ch, :], ALU.mult, ALU.add, [pOk, krd, "hsum"], ["hsum"])
                wcol = DT[:, 127:128] if e == 0 else DT[:, 0:1]
                b.ts(kw, ktm[:, ch, :], wcol, None, ALU.mult, None, ["ktm", kDT], [kkw], eng="pool")
                for dc in range(2):
                    b.mm(pC[:, dc * 256:(dc + 1) * 256], kw[:, dc * 128:(dc + 1) * 128], vaug[:, ch, 0:256], True, True,
                         [kkw, "vaug"], [pCk])
                    b.mm(pN[:, 2 * e + dc:2 * e + dc + 1], kw[:, dc * 128:(dc + 1) * 128], vaug[:, ch, 256:257], True, True,
                         [kkw, "vaug"], [pNk])
                dcol = dec[:, ch, col:col + 1]
                b.stt(Cm[:, :, 0:256], Cm[:, :, 0:256], dcol, pC.rearrange("p (a b) -> p a b", a=2), ALU.mult, ALU.add,
                      [kCm, "dec", pCk], [kCm])
                b.stt(Cm[:, :, 256:257], Cm[:, :, 256:257], dcol, pN[:, 2 * e:2 * e + 2].rearrange("p (a b) -> p a b", b=1),
                      ALU.mult, ALU.add, [kCm, "dec", pNk], [kCm])
                b.copy(Cb[:, :, 0:257], Cm[:, :, 0:257], [kCm], [kCb], eng="act")
        for ch in range(NCH):
            cs = slice(ch * 128, (ch + 1) * 128)
            pT, pTk = b.ps(ch % 2)
            b.act(sqc, hsum[:, ch, :], AF.Square, ["hsum"], ["sqc"])
            S.dve(lambda en, ch=ch: en.reduce_sum(out=ssq, in_=sqc, axis=AX.X), ["sqc"], ["ssq"])
            b.act(ssq, ssq, AF.Sqrt, ["ssq"], ["ssq"], bias=EPS, scale=1.0 / 256.0)
            S.dve(lambda en: en.reciprocal(out=ssq, in_=ssq), ["ssq"], ["ssq"])
            b.ts(hn, hsum[:, ch, :], ssq, None, ALU.mult, None, ["hsum", "ssq"], ["hn"])
            for ec in range(2):
                b.tr(pT[:, ec * 128:(ec + 1) * 128], hn[:, ec * 128:(ec + 1) * 128], ident, ["hn"] + CK, [pTk])
            b.act(sgo, ogT[:, :, cs], AF.Sigmoid, ["ogT"], ["sgo"])
            for ec in range(2):
                gcol = cols[:, C_BNG + 2 * hd + ec:C_BNG + 2 * hd + ec + 1]
                b.stt(yB[:, ec, cs], pT[:, ec * 128:(ec + 1) * 128], gcol, sgo[:, ec, :], ALU.mult, ALU.mult,
                      [pTk, "cols", "sgo"], ["yB"])
        for ec in range(2):
            b.dma(Y[8 + 2 * hd + ec, :, :], yB[:, ec, :], ["yB"], [], q="act")


def mixerC(b, c, layer):
    T, SEG = b.T, b.SEG
    NC, NCS = T // 64, SEG // 64
    cols, CK, flag, S, ident = b.cols, b.CK, b.flag, b.S, b.ident
    Uf, Ub, NGf, NGb, PSf, PSb, dtb, negA = (c[k] for k in ("Uf", "Ub", "NGf", "NGb", "PSf", "PSb", "dtb", "negA"))
    Cqkv, Cz, Gt, Y = c["Cqkv"], c["Cz"], c["Gt"], c["Y"]
    b.phase()
    H = 64
    a64 = lambda shape, dt: b.alloc(shape, dt)[0:H]
    gp = a64([NC, 32], F32)
    g = a64([NC, 16], F32)
    be = a64([NC, 16], F32)
    Gc = a64([NC, 16], F32)
    nG = a64([NC, 16], F32)
    eg = a64([NC, 16], F32)
    ed = a64([NC, 16], F32)
    bk = a64([NC, 16], F32)
    gL = b.alloc([NC, 16], F32)
    b.dma(gp, Gt[:, 16:48].rearrange("(c p) n -> p c n", p=H), [], ["gpTt"])
    b.tt(g[:, :, 0:8], gp[:, :, 0:8], bc_mid(dtb[0:H, 0:8], NC), ALU.add, ["gpTt", "bc"], ["g"])
    b.tt(g[:, :, 8:16], gp[:, :, 16:24], bc_mid(dtb[0:H, 8:16], NC), ALU.add, ["gpTt", "bc"], ["g"])
    b.act(g, g, AF.Exp, ["g"], ["g"])
    b.act(g, g, AF.Ln, ["g"], ["g"], bias=1.0)
    b.tt(g, g, bc_mid(negA[0:H, :], NC), ALU.mult, ["g", "bc"], ["g"])
    b.act(be[:, :, 0:8], gp[:, :, 8:16], AF.Sigmoid, ["gpTt"], ["be"])
    b.act(be[:, :, 8:16], gp[:, :, 24:32], AF.Sigmoid, ["gpTt"], ["be"])
    v3 = lambda p, parts, n, w: p[0:parts, 0:n * w].rearrange("p (a b) -> p a b", b=w)
    for e in range(2):
        U = Uf if e == 0 else Ub
        pa, ka = b.ps(0 + e)
        pb_, kb_ = b.ps(2 + e)
        pc, kc = b.ps(4 + e)
        gs = g[:, :, e * 8:(e + 1) * 8]
        b.mm(v3(pa, H, NC, 8), U[0:H, 0:H], gs, True, True, ["g"] + CK, [ka])
        b.copy(Gc[:, :, e * 8:(e + 1) * 8], v3(pa, H, NC, 8), [ka], ["Gc"], eng="act")
        b.mm(v3(pb_, H, NC, 8), b.ones_f[0:H, 0:H], gs, True, True, ["g"] + CK, [kb_])
        b.tt(ed[:, :, e * 8:(e + 1) * 8], v3(pb_, H, NC, 8), Gc[:, :, e * 8:(e + 1) * 8], ALU.subtract, [kb_, "Gc"], ["ed"])
        b.mm(v3(pc, 128, NC, 8), b.ones_f[0:H, :], gs, True, True, ["g"] + CK, [kc])
        b.act(gL[:, :, e * 8:(e + 1) * 8], v3(pc, 128, NC, 8), AF.Exp, [kc], ["gL"])
    b.act(ed, ed, AF.Exp, ["ed"], ["ed"])
    b.act(eg, Gc, AF.Exp, ["Gc"], ["eg"])
    b.tt(bk, be, eg, ALU.mult, ["be", "eg"], ["bk"])
    S.act(lambda en: en.mul(out=nG, in_=Gc, mul=-1.0), ["Gc"], ["nG"])

    PW = SEG + 3
    xp = b.alloc([2 * PW], BF16)
    cvs = b.alloc([T], F32)
    sqb = b.alloc([T], BF16)
    yC = sqb
    qT = b.alloc([T], BF16)
    kT = b.alloc([T], BF16)
    zs = xp[:, 0:T]
    rn = b.alloc([512], F32)
    ktm = a64([NC, 128], BF16)
    vtm = a64([NC, 128], BF16)
    osum = a64([NC, 128], F32)
    Ttb2 = [a64([NC, 64], BF16) for _ in range(2)]
    qkb2 = [a64([NC, 64], BF16) for _ in range(2)]
    Sm2 = [b.alloc([128], F32) for _ in range(2)]
    Sb2 = [b.alloc([128], BF16) for _ in range(2)]
    G = min(8, NC)
    gsrc = cvs if T >= 8 * G * 64 else b.alloc([8 * G * 64], F32)
    gbuf = lambda i: gsrc[0:H, i * G * 64:(i + 1) * G * 64].rearrange("p (a b) -> p a b", b=64)
    DTg, DAg = gbuf(0), gbuf(1)
    gbf = lambda i: gsrc[0:H, i * G * 64:(i + 1) * G * 64].bitcast(BF16)[:, 0:G * 64].rearrange("p (a b) -> p a b", b=64)
    Ng = [gbf(2), gbf(3)]
    Mg = [gbf(4), gbf(5)]
    Pg = [gbf(6), gbf(7)]
    I64b = c["ident_b"][0:H, 0:H]
    sqg = gsrc[0:H, 0:G * 128].rearrange("p (a b) -> p a b", b=128)
    ssg = a64([G], F32)
    vb2 = [a64([128], F32) for _ in range(2)]
    negr2 = [a64([128], BF16) for _ in range(2)]
    vnew2 = [a64([128], BF16) for _ in range(2)]
    kdec2 = [a64([128], BF16) for _ in range(2)]
    o12 = [a64([128], F32) for _ in range(2)]
    o22 = [a64([128], F32) for _ in range(2)]
    sqc = a64([128], F32)
    on = a64([128], F32)
    ssq = a64([1], F32)
    I64 = ident[0:H, 0:H]
    cbs = b.colblocks(T)

    def conv_silu(ci):
        for s in range(2):
            b.dma(xp[:, s * PW + 2:s * PW + 2 + SEG], Cqkv[ci, :, s * SEG:(s + 1) * SEG], [], ["xp"])
        b.memset(xp[:, 0:2], 0.0, ["xp"])
        b.memset(xp[:, 2 * PW - 1:2 * PW], 0.0, ["xp"])
        b.ts(xp[:, PW - 1:PW], xp[:, PW + 2:PW + 3], flag, None, ALU.mult, None, ["xp"] + CK, ["xp"])
        b.ts(xp[:, PW:PW + 2], xp[:, SEG:SEG + 2], flag, None, ALU.mult, None, ["xp"] + CK, ["xp"])
        for s in range(2):
            o = s * PW
            dst = cvs[:, s * SEG:(s + 1) * SEG]
            wc = lambda k: cols[:, C_CCW + k * 24 + ci:C_CCW + k * 24 + ci + 1]
            b.ts(dst, xp[:, o:o + SEG], wc(0), None, ALU.mult, None, ["xp", "cols"], ["cvs"])
            for k in range(1, 4):
                b.stt(dst, xp[:, o + k:o + k + SEG], wc(k), dst, ALU.mult, ALU.add, ["xp", "cols", "cvs"], ["cvs"])
        b.act(cvs, cvs, AF.Silu, ["cvs"], ["cvs"])

    def l2norm(dstT, dkey, scale):
        b.act(sqb, cvs, AF.Square, ["cvs"], ["sqb"])
        for ci_, (a, e_) in enumerate(cbs):
            pst, pk = b.ps(ci_ % 2)
            b.mm(pst[:, 0:e_ - a], b.ones_b, sqb[:, a:e_], True, True, ["sqb"] + CK, [pk])
            b.act(rn[:, 0:e_ - a], pst[:, 0:e_ - a], AF.Sqrt, [pk], ["rn"], bias=EPS)
            S.dve(lambda en, w=e_ - a: en.reciprocal(out=rn[:, 0:w], in_=rn[:, 0:w]), ["rn"], ["rn"])
            b.stt(cvs[:, a:e_], cvs[:, a:e_], scale, rn[:, 0:e_ - a], ALU.mult, ALU.mult, ["cvs", "rn"], ["cvs"])
        b.copy(dstT, cvs, ["cvs"], [dkey], eng="act")

    def to_tm(dst, dkey):
        for ch in range(NC):
            pst, pk = b.ps(2 + (ch // 4) % 2)
            sub = pst[0:H, (ch % 4) * 128:(ch % 4 + 1) * 128]
            b.tr(sub, cvs[:, ch * 64:(ch + 1) * 64], ident, ["cvs"] + CK, [pk])
            if ch % 4 == 3 or ch == NC - 1:
                n = ch % 4 + 1
                c0 = ch - n + 1
                b.evac(dst[:, c0:c0 + n, :], pst[0:H, 0:n * 128].rearrange("p (a b) -> p a b", b=128), [pk], [dkey])

    for hd in range(8):
        S.barrier()
        conv_silu(hd)
        l2norm(qT, "qT", 128.0 ** -0.5)
        conv_silu(8 + hd)
        l2norm(kT, "kT", 1.0)
        to_tm(ktm, "ktm")
        conv_silu(16 + hd)
        to_tm(vtm, "vtm")
        b.dma(zs, Cz[hd, :, :], [], ["xp"])
        b.act(zs, zs, AF.Silu, ["xp"], ["xp"])
        S.barrier()
        def pre_gen(e, c0):
            col = e * 8 + hd
            U, NGm, PSm = (Uf, NGf, PSf) if e == 0 else (Ub, NGb, PSb)
            U64, NG64, PS64 = U[0:H, 0:H], NGm[0:H, 0:H], PSm[0:H, 0:H]
            gk = "%d_%d" % (e, c0 // G)
            P = [b.ps(i % 4) for i in range(8)]
            V = lambda i: P[i][0][0:H, 0:G * 64].rearrange("p (a b) -> p a b", b=64)
            slot = lambda i, j: P[i][0][0:H, j * 64:(j + 1) * 64]
            PK = lambda i: P[i][1]
            for j in range(G):
                gb = bc_last(g[:, c0 + j, col:col + 1], H)
                b.mm(slot(0, j), gb, U64, True, False, ["g"] + CK, [PK(0)])
                b.mm(slot(0, j), I64, NG64, False, True, CK, [PK(0)])
                b.mm(slot(1, j), gb, U64, True, False, ["g"] + CK, [PK(1)])
                b.mm(slot(1, j), I64, PS64, False, True, CK, [PK(1)])
                yield
            b.tt(DTg, V(0), bc_l3(nG[:, c0:c0 + G, col:col + 1], 64), ALU.add, [PK(0), "nG"], ["DTg"])
            b.act(DTg, DTg, AF.Exp, ["DTg"], ["DTg"])
            yield
            b.tt(DAg, bc_l3(Gc[:, c0:c0 + G, col:col + 1], 64), V(1), ALU.subtract, [PK(1), "Gc"], ["DAg"])
            b.act(DAg, DAg, AF.Exp, ["DAg"], ["DAg"])
            b.tt(DAg, DAg, bc_l3(be[:, c0:c0 + G, col:col + 1], 64), ALU.mult, ["DAg", "be"], ["DAg"], eng="pool")
            yield
            for j in range(G):
                cs = slice((c0 + j) * 64, (c0 + j + 1) * 64)
                b.mm(slot(2, j), kT[:, cs], kT[:, cs], True, True, ["kT"], [PK(2)])
                b.mm(slot(3, j), kT[:, cs], qT[:, cs], True, True, ["kT", "qT"], [PK(3)])
                yield
            b.tt(Ng[0], V(2), DAg, ALU.mult, [PK(2), "DAg"], ["N0"])
            b.tt(qkb2[e][:, c0:c0 + G, :], V(3), DTg, ALU.mult, [PK(3), "DTg"], ["qkb" + gk])
            yield
            for j in range(G):
                b.mm(slot(4, j), Ng[0][:, j, :], I64b, True, True, ["N0"] + CK, [PK(4)])
            yield
            b.copy(Mg[0], V(4), [PK(4)], ["M0"], eng="act")
            b.tt(Pg[0], bc_mid(I64, G), V(4), ALU.subtract, [PK(4)] + CK, ["P0"])
            yield
            for lv in range(1, 6):
                pi_, ci_ = (lv - 1) % 2, lv % 2
                for j in range(G):
                    b.mm(slot(5, j), Mg[pi_][:, j, :], Ng[pi_][:, j, :], True, True, ["M%d" % pi_, "N%d" % pi_], [PK(5)])
                yield
                b.copy(Ng[ci_], V(5), [PK(5)], ["N%d" % ci_], eng="act")
                if lv < 5:
                    for j in range(G):
                        b.mm(slot(6, j), Ng[pi_][:, j, :], Mg[pi_][:, j, :], True, True, ["M%d" % pi_, "N%d" % pi_], [PK(6)])
                    yield
                    b.copy(Mg[ci_], V(6), [PK(6)], ["M%d" % ci_], eng="dve")
                for j in range(G):
                    b.mm(slot(7, j), Ng[ci_][:, j, :], Pg[pi_][:, j, :], True, True, ["N%d" % ci_, "P%d" % pi_], [PK(7)])
                yield
                if lv < 5:
                    b.tt(Pg[ci_], Pg[pi_], V(7), ALU.add, ["P%d" % pi_, PK(7)], ["P%d" % ci_])
                else:
                    b.tt(Ttb2[e][:, c0:c0 + G, :], Pg[pi_], V(7), ALU.add, ["P%d" % pi_, PK(7)], ["Ttb" + gk])
                yield

        def drain(gen):
            for _ in gen:
                pass

        def chain_step(e, ch, idx):
            col = e * 8 + hd
            gk = "%d_%d" % (e, ch // G)
            Sm, Sb, vb, negr, vnew, kdec, o1, o2 = Sm2[e], Sb2[e], vb2[e], negr2[e], vnew2[e], kdec2[e], o12[e], o22[e]
            kS, kSb, kvb, knr, kvn, kkd, ko1, ko2 = ("%s%d" % (n_, e) for n_ in ("Sm", "Sb", "vb", "negr", "vnew", "kdec", "o1", "o2"))
            if idx == NCS:
                b.ts(Sm, Sm, flag, None, ALU.mult, None, [kS] + CK, [kS])
                b.copy(Sb, Sm, [kS], [kSb], eng="act")
            cs = slice(ch * 64, (ch + 1) * 64)
            pX, pXk = b.ps(4 + 2 * e)
            pD, pDk = b.ps(5 + 2 * e)
            b.mm(pX[0:H, 0:128], kT[:, cs], Sb, True, True, ["kT", kSb], [pXk])
            b.mm(pX[0:H, 128:256], qT[:, cs], Sb, True, True, ["qT", kSb], [pXk])
            b.ts(vb, vtm[:, ch, :], be[:, ch, col:col + 1], None, ALU.mult, None, ["vtm", "be"], [kvb], eng="pool")
            b.stt(negr, pX[0:H, 0:128], bk[:, ch, col:col + 1], vb, ALU.mult, ALU.subtract, [pXk, "bk", kvb], [knr])
            b.mm(pX[0:H, 256:384], Ttb2[e][:, ch, :], negr, True, True, ["Ttb" + gk, knr], [pXk])
            S.act(lambda en: en.mul(out=vnew, in_=pX[0:H, 256:384], mul=-1.0), [pXk], [kvn])
            b.mm(pX[0:H, 384:512], qkb2[e][:, ch, :], vnew, True, True, ["qkb" + gk, kvn], [pXk])
            b.ts(kdec, ktm[:, ch, :], ed[:, ch, col:col + 1], None, ALU.mult, None, ["ktm", "ed"], [kkd], eng="pool")
            b.mm(pD[:, 0:128], kdec, vnew, True, True, [kkd, kvn], [pDk])
            b.stt(Sm, Sm, gL[:, ch, col:col + 1], pD[:, 0:128], ALU.mult, ALU.add, [kS, "gL", pDk], [kS])
            b.copy(Sb, Sm, [kS], [kSb], eng="act")
            b.copy(o1, pX[0:H, 384:512], [pXk], [ko1], eng="act")
            b.stt(o2, pX[0:H, 128:256], eg[:, ch, col:col + 1], o1, ALU.mult, ALU.add, [pXk, "eg", ko1], [ko2])
            b.tt(osum[:, ch, :], osum[:, ch, :], o2, ALU.add, ["osum", ko2], ["osum"])

        NGR = NC // G
        b.memset(osum, 0.0, ["osum"])
        for e in range(2):
            b.memset(Sm2[e], 0.0, ["Sm%d" % e])
            b.memset(Sb2[e], 0.0, ["Sb%d" % e])
        drain(pre_gen(0, 0))
        drain(pre_gen(1, (NGR - 1) * G))
        UNITS = 18 * G + 30
        for gi in range(NGR):
            gens = []
            if gi + 1 < NGR:
                gens = [pre_gen(0, (gi + 1) * G), pre_gen(1, (NGR - 2 - gi) * G)]
            per = (2 * UNITS) // G + 1
            for k in range(G):
                idx = gi * G + k
                for e in range(2):
                    chain_step(e, idx if e == 0 else NC - 1 - idx, idx)
                n = per
                while gens and n > 0:
                    try:
                        next(gens[0])
                        n -= 1
                    except StopIteration:
                        gens.pop(0)
            for gen in gens:
                drain(gen)
        S.barrier()
        for gi, c0 in enumerate(range(0, NC, G)):
            pT, pTk = b.ps(gi % 2)
            og = osum[:, c0:c0 + G, :]
            b.act(sqg, og, AF.Square, ["osum"], ["sqg"])
            S.dve(lambda en: en.reduce_sum(out=ssg, in_=sqg, axis=AX.X), ["sqg"], ["ssg"])
            b.act(ssg, ssg, AF.Sqrt, ["ssg"], ["ssg"], bias=EPS, scale=1.0 / 128.0)
            S.dve(lambda en: en.reciprocal(out=ssg, in_=ssg), ["ssg"], ["ssg"])
            b.tt(sqg, og, bc_l3(ssg.rearrange("p (a b) -> p a b", b=1), 128), ALU.mult, ["osum", "ssg"], ["sqg"])
            for j in range(G):
                b.tr(pT[:, j * 64:(j + 1) * 64], sqg[:, j, :], I64, ["sqg"] + CK, [pTk])
            cs = slice(c0 * 64, (c0 + G) * 64)
            b.stt(yC[:, cs], pT[:, 0:G * 64], cols[:, C_CNG + hd:C_CNG + hd + 1], zs[:, cs], ALU.mult, ALU.mult,
                  [pTk, "cols", "xp"], ["sqb"])
        b.dma(Y[16 + hd, :, :], yC, ["sqb"], [], q="act")


def phase3a(b, c, layer):
    T = b.T
    TB = min(T, 512)
    W, Y, MG, hA, hM = c["W"], c["Y"], c["MG"], c["hA"], c["hM"]
    for tb in range(T // TB):
        b.phase()
        t0 = tb * TB
        b.wsetup()
        b.wcache = {"ap": c["Wc3a"], "idx": 0, "mode": "fill" if tb == 0 else "use"}
        Yt = b.alloc([24, TB], BF16)
        h = b.alloc([16, TB], F32)
        mrg = b.alloc([16, TB], BF16)
        mgt = [b.alloc([3, TB], BF16) for _ in range(2)]
        sg = [b.alloc([TB], F32) for _ in range(3)]
        acc = b.alloc([TB], F32)
        tmp = b.alloc([TB], F32)
        b.dma(Yt, Y[:, :, t0:t0 + TB].rearrange("c p t -> p c t"), [], ["Yt"])
        b.dma(h, hA[:, :, t0:t0 + TB].rearrange("c p t -> p c t"), [], ["h%d" % i for i in range(16)])
        for m in range(16):
            mg_ = mgt[m % 2]
            pks = []
            for g in range(3):
                mk = "mgt%d_%d" % (m % 2, g)
                b.dma(mg_[:, g, :], MG[g * 16 + m, :, t0:t0 + TB], [], [mk])
                wt, wk = b.load_w(W["w_branch"][layer, g, :, m * 128:(m + 1) * 128], 8)
                pst, pk = b.ps(g + 4 * (m % 2))
                pks.append((pst, pk))
                for k in range(8):
                    b.mm(pst[:, 0:TB], wt[:, k, :], Yt[:, g * 8 + k, :], k == 0, k == 7, [wk, "Yt"], [pk])
                b.act(sg[g], mg_[:, g, :], AF.Sigmoid, [mk], ["sg%d" % g])
            b.tt(acc, pks[0][0][:, 0:TB], sg[0], ALU.mult, [pks[0][1], "sg0"], ["acc"])
            b.tt(tmp, pks[1][0][:, 0:TB], sg[1], ALU.mult, [pks[1][1], "sg1"], ["tmp"])
            b.tt(acc, acc, tmp, ALU.add, ["acc", "tmp"], ["acc"])
            b.tt(tmp, pks[2][0][:, 0:TB], sg[2], ALU.mult, [pks[2][1], "sg2"], ["tmp"])
            b.tt(mrg[:, m, :], acc, tmp, ALU.add, ["acc", "tmp"], ["mrg"])
        for m in range(16):
            wt, wk = b.load_w(W["w_out"][layer, :, m * 128:(m + 1) * 128], 16)
            pst, pk = b.ps(3 + 4 * (m % 2))
            for k in range(16):
                b.mm(pst[:, 0:TB], wt[:, k, :], mrg[:, k, :], k == 0, k == 15, [wk, "mrg"], [pk])
            b.tt(h[:, m, :], h[:, m, :], pst[:, 0:TB], ALU.add, [pk, "h%d" % m], ["h%d" % m])
        b.dma(hM[:, :, t0:t0 + TB].rearrange("c p t -> p c t"), h, ["h%d" % i for i in range(16)], [], q="act")
    b.wcache = None


def phase3b(b, c, layer, last):
    T, SEG = b.T, b.SEG
    TB = min(SEG, 512)
    NE = TB + 2
    cols, CK, flag, S, ident = b.cols, b.CK, b.flag, b.S, b.ident
    W, hA, hM, p_d, y_d = c["W"], c["hA"], c["hM"], c["p_d"], c["y_d"]
    HK = ["h%d" % i for i in range(16)]
    for tb in range(T // TB):
        b.phase()
        t0 = tb * TB
        b.wsetup()
        b.wcache = {"ap": c["Wc3b"], "idx": 0, "mode": "fill" if tb == 0 else "use"}
        h = b.alloc([16, NE], F32)
        xn = b.alloc([16, NE], BF16)
        sq = [b.alloc([NE], BF16) for _ in range(2)]
        rstd = b.alloc([NE], F32)
        act_ = b.alloc([48, TB], BF16)
        cg = b.alloc([TB], F32)
        cu = b.alloc([TB], F32)
        pin = b.alloc([256], F32)
        pT = b.alloc([2, TB], BF16)
        b.dma(h[:, :, 1:TB + 1], hM[:, :, t0:t0 + TB].rearrange("c p t -> p c t"), [], HK)
        if t0 > 0:
            b.dma(h[:, :, 0:1], hM[:, :, t0 - 1:t0].rearrange("c p t -> p c t"), [], ["hl"], slow=True)
        else:
            b.memset(h[:, :, 0:1], 0.0, ["hl"])
        if t0 + TB < T:
            b.dma(h[:, :, TB + 1:TB + 2], hM[:, :, t0 + TB:t0 + TB + 1].rearrange("c p t -> p c t"), [], ["hr"], slow=True)
        else:
            b.memset(h[:, :, TB + 1:TB + 2], 0.0, ["hr"])
        HALL = HK + ["hl", "hr"]

        def rms(lo, hi, gcol0, out, okey, hkeys):
            n = hi - lo
            cb = [(a + lo, e_ + lo) for (a, e_) in b.colblocks(n)]
            for ch in range(16):
                q, qk = sq[ch % 2], "sq%d" % (ch % 2)
                b.act(q[:, lo:hi], h[:, ch, lo:hi], AF.Square, hkeys, [qk])
                for i, (a, e_) in enumerate(cb):
                    pst, pk = b.ps(i)
                    b.mm(pst[:, 0:e_ - a], b.ones_b, q[:, a:e_], ch == 0, ch == 15, [qk] + CK, [pk])
            for i, (a, e_) in enumerate(cb):
                pst, pk = b.ps(i)
                b.act(rstd[:, a:e_], pst[:, 0:e_ - a], AF.Sqrt, [pk], ["rstd"], bias=EPS, scale=1.0 / D)
            S.dve(lambda en: en.reciprocal(out=rstd[:, lo:hi], in_=rstd[:, lo:hi]), ["rstd"], ["rstd"])
            for ch in range(16):
                b.stt(out[:, ch, :], h[:, ch, lo:hi], cols[:, gcol0 + ch:gcol0 + ch + 1], rstd[:, lo:hi],
                      ALU.mult, ALU.mult, hkeys + ["cols", "rstd"], [okey])

        rms(0, NE, C_FFNG, xn, "xn", HALL)
        if t0 == SEG:
            b.ts(xn[:, :, 0:1], xn[:, :, 0:1], flag, None, ALU.mult, None, ["xn"] + CK, ["xn"])
        if t0 + TB == SEG:
            b.ts(xn[:, :, TB + 1:TB + 2], xn[:, :, TB + 1:TB + 2], flag, None, ALU.mult, None, ["xn"] + CK, ["xn"])
        cbs = b.colblocks(NE)
        for j in range(48):
            wg, wgk = b.load_w(W["ffn_w_up"][layer, :, j * 128:(j + 1) * 128], 16)
            pg = b.psum[2 * (j % 2)]
            pgk = ["ps%d" % (4 * (j % 2)), "ps%d" % (4 * (j % 2) + 1)]
            for (a, e_) in cbs:
                for k in range(16):
                    b.mm(pg[:, a:e_], wg[:, k, :], xn[:, k, a:e_], k == 0, k == 15, [wgk, "xn"], pgk)
            wu, wuk = b.load_w(W["ffn_w_up"][layer, :, DFF + j * 128:DFF + (j + 1) * 128], 16)
            pu = b.psum[2 * (j % 2) + 1]
            puk = ["ps%d" % (4 * (j % 2) + 2), "ps%d" % (4 * (j % 2) + 3)]
            for (a, e_) in cbs:
                for k in range(16):
                    b.mm(pu[:, a:e_], wu[:, k, :], xn[:, k, a:e_], k == 0, k == 15, [wuk, "xn"], puk)
            for (pp, ppk, dst, dk, ci) in ((pg, pgk, cg, "cg", j), (pu, puk, cu, "cu", 48 + j)):
                wc = lambda k: cols[:, C_FCW + k * 96 + ci:C_FCW + k * 96 + ci + 1]
                b.ts(dst, pp[:, 1:TB + 1], wc(1), None, ALU.mult, None, ppk + ["cols"], [dk])
                b.stt(dst, pp[:, 0:TB], wc(0), dst, ALU.mult, ALU.add, ppk + ["cols", dk], [dk])
                b.stt(dst, pp[:, 2:TB + 2], wc(2), dst, ALU.mult, ALU.add, ppk + ["cols", dk], [dk])
            b.act(cg, cg, AF.Gelu_apprx_tanh, ["cg"], ["cg"])
            b.tt(act_[:, j, :], cg, cu, ALU.mult, ["cg", "cu"], ["act"])
        for m in range(16):
            pst, pk = b.ps(m % 2)
            for q_ in range(3):
                wt, wk = b.load_w(W["ffn_w_down"][layer, q_ * 2048:(q_ + 1) * 2048, m * 128:(m + 1) * 128], 16)
                for k in range(16):
                    b.mm(pst[:, 0:TB], wt[:, k, :], act_[:, q_ * 16 + k, :], q_ == 0 and k == 0, q_ == 2 and k == 15,
                         [wk, "act"], [pk])
            b.tt(h[:, m, 1:TB + 1], h[:, m, 1:TB + 1], pst[:, 0:TB], ALU.add, [pk, "h%d" % m], ["h%d" % m])
        xn3 = xn[:, :, 1:TB + 1]
        rms(1, TB + 1, C_PLEG, xn3, "xn", HK)
        for tt in range(TB // 128):
            b.dma(pin, p_d[layer, t0 + tt * 128:t0 + (tt + 1) * 128, :], [], ["pin"])
            pst, pk = b.ps(2 + tt % 2)
            for k in range(2):
                b.tr(pst[:, k * 128:(k + 1) * 128], pin[:, k * 128:(k + 1) * 128], ident, ["pin"] + CK, [pk])
            b.evac(pT[:, :, tt * 128:(tt + 1) * 128], pst[:, 0:256].rearrange("p (a b) -> p a b", a=2), [pk], ["pT"])
        for m in range(16):
            wg, wgk = b.load_w(W["ple_w_gate"][layer, :, m * 128:(m + 1) * 128], 16)
            pa, pak = b.ps(4 + 2 * (m % 2))
            for k in range(16):
                b.mm(pa[:, 0:TB], wg[:, k, :], xn3[:, k, :], k == 0, k == 15, [wgk, "xn"], [pak])
            wp, wpk = b.load_w(W["ple_w_proj"][layer, :, m * 128:(m + 1) * 128], 2)
            pb_, pbk = b.ps(5 + 2 * (m % 2))
            for k in range(2):
                b.mm(pb_[:, 0:TB], wp[:, k, :], pT[:, k, :], k == 0, k == 1, [wpk, "pT"], [pbk])
            b.act(cg, pa[:, 0:TB], AF.Sigmoid, [pak], ["cg"])
            b.tt(cu, pb_[:, 0:TB], cg, ALU.mult, [pbk, "cg"], ["cu"])
            b.tt(h[:, m, 1:TB + 1], h[:, m, 1:TB + 1], cu, ALU.add, ["cu", "h%d" % m], ["h%d" % m])
        if not last:
            b.dma(hA[:, :, t0:t0 + TB].rearrange("c p t -> p c t"), h[:, :, 1:TB + 1], HK, [], q="act")
        else:
            if b.debug:
                b.dma(hA[:, :, t0:t0 + TB].rearrange("c p t -> p c t"), h[:, :, 1:TB + 1], HK, [], q="act")
            xf = b.alloc([16, TB], F32)
            yo = [b.alloc([D], F32) for _ in range(2)]
            rms(1, TB + 1, C_FING, xf, "xf", HK)
            for tt in range(TB // 128):
                o, ok = yo[tt % 2], "yo%d" % (tt % 2)
                for g in range(4):
                    pst, pk = b.ps(4 + (tt * 4 + g) % 4)
                    for j in range(4):
                        ch = g * 4 + j
                        b.tr(pst[:, j * 128:(j + 1) * 128], xf[:, ch, tt * 128:(tt + 1) * 128], ident, ["xf"] + CK, [pk])
                    b.evac(o[:, g * 512:(g + 1) * 512], pst, [pk], [ok])
                b.dma(y_d[t0 + tt * 128:t0 + (tt + 1) * 128, :], o, [ok], [], q="act")
    b.wcache = None


_NC_CACHE = {}


def kernel(**inputs):
    SEG = 2048
    T = 2 * SEG
    if "nc" not in _NC_CACHE:
        _NC_CACHE["nc"] = build_nc(SEG=SEG)[0]
    nc = _NC_CACHE["nc"]
    xp_ = np.asarray(inputs["x_prompt"], dtype=np.float32)
    xs_ = np.asarray(inputs["x_sample"], dtype=np.float32)
    pp_ = np.asarray(inputs["p_prompt"], dtype=np.float32)
    ps_ = np.asarray(inputs["p_sample"], dtype=np.float32)
    wts = {n: np.ascontiguousarray(np.asarray(inputs[n], dtype=np.float32)) for n, _ in WNAMES}
    in_maps = []
    for core in range(8):
        if core < 4:
            x = xp_[core]
            p = pp_[:, core]
            f = 1.0
        else:
            j = 2 * (core - 4)
            x = xs_[j:j + 2].reshape(T, D)
            p = ps_[:, j:j + 2].reshape(2, T, 256)
            f = 0.0
        m = {"x": np.ascontiguousarray(x), "p": np.ascontiguousarray(p),
             "flag": np.full((128, 1), f, np.float32)}
        m.update(wts)
        in_maps.append(m)
    res = run_bass_kernel_spmd(nc, in_maps, core_ids=list(range(8)))
    outs = [np.asarray(r["y"], dtype=np.float32) for r in res.results]
    y_prompt = np.stack(outs[0:4], axis=0)
    y_sample = np.concatenate([o.reshape(2, SEG, D) for o in outs[4:8]], axis=0)
    return (y_prompt, y_sample)
```
